# Optimizing a Trainium2 kernel written in Bass

```python
import jax, jax.numpy as jnp
from jax import lax
import numpy as np

D_MODEL = 1024
BATCH = 1
SEQ = 16384
DEPTH = 1

HEAD_DIM = 64
N_HEADS_MOBA = 8
N_HEADS_SB = 8
D_MOBA = N_HEADS_MOBA * HEAD_DIM
D_SB = N_HEADS_SB * HEAD_DIM
MOBA_BLOCK = 256
MOBA_TOP_K = 3
Q_CHUNK = 128
ROPE_THETA = 500000.0
ROPE_DIM = HEAD_DIM // 4
D_FF = 2816
FFN_RES = 0.5
EPS = 1e-6
D_IN = 3 * D_MOBA + 3 * D_SB + 2 * D_MODEL
NEG = -1e30

kernel_name = "hybrid_moba_stickbreaking_macaron"


def rms_norm(x, g):
    x32 = x.astype(jnp.float32)
    y = x32 * lax.rsqrt(jnp.mean(x32 * x32, axis=-1, keepdims=True) + EPS)
    return y.astype(x.dtype) * g


def swiglu(x, w_gate, w_up, w_down):
    return (jax.nn.silu(x @ w_gate) * (x @ w_up)) @ w_down


def partial_rotary(x, positions):
    half = ROPE_DIM // 2
    inv_freq = ROPE_THETA ** (-jnp.arange(half, dtype=jnp.float32) * 2.0 / ROPE_DIM)
    ang = positions.astype(jnp.float32)[:, None, :, None] * inv_freq
    cos = jnp.cos(ang).astype(x.dtype)
    sin = jnp.sin(ang).astype(x.dtype)
    x1, x2, rest = x[..., :half], x[..., half:ROPE_DIM], x[..., ROPE_DIM:]
    return jnp.concatenate([x1 * cos - x2 * sin, x2 * cos + x1 * sin, rest], axis=-1)


def split_heads(x, n):
    b, t, _ = x.shape
    return x.reshape(b, t, n, HEAD_DIM).transpose(0, 2, 1, 3)


def merge_heads(x):
    b, h, t, d = x.shape
    return x.transpose(0, 2, 1, 3).reshape(b, t, h * d)


def moba_attention(q, k, v):
    B, H, T, hd = q.shape
    nb = -(-T // MOBA_BLOCK)
    pad = nb * MOBA_BLOCK - T
    k_pad = jnp.pad(k, ((0, 0), (0, 0), (0, pad), (0, 0)))
    v_pad = jnp.pad(v, ((0, 0), (0, 0), (0, pad), (0, 0)))
    kb = k_pad.reshape(B, H, nb, MOBA_BLOCK, hd)
    vb = v_pad.reshape(B, H, nb, MOBA_BLOCK, hd)
    k_mean = jnp.mean(kb.astype(jnp.float32), axis=3)
    k_sel = min(MOBA_TOP_K, nb)
    scale = hd ** -0.5
    bi = jnp.arange(B)[:, None, None, None]
    hi = jnp.arange(H)[None, :, None, None]
    offs = jnp.arange(MOBA_BLOCK)
    blk_ids = jnp.arange(nb)

    def chunk(c):
        start = c * Q_CHUNK
        qc = lax.dynamic_slice_in_dim(q, start, Q_CHUNK, axis=2)
        t = start + jnp.arange(Q_CHUNK)
        own = start // MOBA_BLOCK
        gate = jnp.einsum('bhqd,bhnd->bhqn', qc.astype(jnp.float32), k_mean)
        gate = jnp.where(blk_ids < own, gate, NEG)
        _, idx = lax.top_k(gate, k_sel)
        valid = idx < own
        ks = kb[bi, hi, idx]
        vs = vb[bi, hi, idx]
        s_sel = jnp.einsum('bhqd,bhqkld->bhqkl', qc, ks).astype(jnp.float32) * scale
        s_sel = jnp.where(valid[..., None], s_sel, NEG).reshape(B, H, Q_CHUNK, k_sel * MOBA_BLOCK)
        k_own = lax.dynamic_index_in_dim(kb, own, axis=2, keepdims=False)
        v_own = lax.dynamic_index_in_dim(vb, own, axis=2, keepdims=False)
        s_own = jnp.einsum('bhqd,bhld->bhql', qc, k_own).astype(jnp.float32) * scale
        pos_own = own * MOBA_BLOCK + offs
        s_own = jnp.where(pos_own[None, :] <= t[:, None], s_own, NEG)
        p = jax.nn.softmax(jnp.concatenate([s_sel, s_own], axis=-1), axis=-1).astype(v.dtype)
        p_sel = p[..., :k_sel * MOBA_BLOCK].reshape(B, H, Q_CHUNK, k_sel, MOBA_BLOCK)
        p_own = p[..., k_sel * MOBA_BLOCK:]
        return (jnp.einsum('bhqkl,bhqkld->bhqd', p_sel, vs)
                + jnp.einsum('bhql,bhld->bhqd', p_own, v_own))

    out = lax.map(chunk, jnp.arange(T // Q_CHUNK))
    return out.transpose(1, 2, 0, 3, 4).reshape(B, H, T, hd)


def stick_breaking_attention(q, k, v):
    B, H, T, hd = q.shape
    scale = hd ** -0.5
    s_pos = jnp.arange(T)

    def chunk(c):
        start = c * Q_CHUNK
        qc = lax.dynamic_slice_in_dim(q, start, Q_CHUNK, axis=2)
        t = start + jnp.arange(Q_CHUNK)
        z = jnp.einsum('bhqd,bhsd->bhqs', qc, k).astype(jnp.float32) * scale
        past = s_pos[None, :] < t[:, None]
        log_fail = jnp.where(past, jax.nn.log_sigmoid(-z), 0.0)
        later = lax.cumsum(log_fail, axis=3, reverse=True) - log_fail
        w = jnp.where(past, jnp.exp(jax.nn.log_sigmoid(z) + later), 0.0)
        return jnp.einsum('bhqs,bhsd->bhqd', w.astype(v.dtype), v)

    out = lax.map(chunk, jnp.arange(T // Q_CHUNK))
    return out.transpose(1, 2, 0, 3, 4).reshape(B, H, T, hd)


def hybrid_mixer(u, positions, w_in, b_gate, w_branch_moba, w_branch_sb, w_out):
    proj = u @ w_in
    cuts = [D_MOBA, 2 * D_MOBA, 3 * D_MOBA, 3 * D_MOBA + D_SB, 3 * D_MOBA + 2 * D_SB,
            3 * D_MOBA + 3 * D_SB, 3 * D_MOBA + 3 * D_SB + D_MODEL]
    q_a, k_a, v_a, q_b, k_b, v_b, g_a, g_b = jnp.split(proj, cuts, axis=-1)
    q_a = partial_rotary(split_heads(q_a, N_HEADS_MOBA), positions)
    k_a = partial_rotary(split_heads(k_a, N_HEADS_MOBA), positions)
    y_a = merge_heads(moba_attention(q_a, k_a, split_heads(v_a, N_HEADS_MOBA)))
    y_b = merge_heads(stick_breaking_attention(split_heads(q_b, N_HEADS_SB),
                                               split_heads(k_b, N_HEADS_SB),
                                               split_heads(v_b, N_HEADS_SB)))
    gates = jax.nn.sigmoid(jnp.concatenate([g_a, g_b], axis=-1) + b_gate)
    gate_a, gate_b = gates[..., :D_MODEL], gates[..., D_MODEL:]
    merged = gate_a * (y_a @ w_branch_moba) + gate_b * (y_b @ w_branch_sb)
    return merged @ w_out


def setup_inputs(seed: int = 0) -> dict:
    key = jax.random.key(seed)
    ks = jax.random.split(key, 18)

    def w(k, shape, fan_in):
        return jax.random.normal(k, shape, jnp.float32) * fan_in ** -0.5

    def gain(k):
        return 1.0 + 0.05 * jax.random.normal(k, (DEPTH, D_MODEL), jnp.float32)

    return {
        "x": jax.random.normal(ks[0], (BATCH, SEQ, D_MODEL), jnp.float32),
        "positions": jnp.broadcast_to(jnp.arange(SEQ, dtype=jnp.int32), (BATCH, SEQ)),
        "ffn1_norm": gain(ks[1]),
        "ffn1_w_gate": w(ks[2], (DEPTH, D_MODEL, D_FF), D_MODEL),
        "ffn1_w_up": w(ks[3], (DEPTH, D_MODEL, D_FF), D_MODEL),
        "ffn1_w_down": w(ks[4], (DEPTH, D_FF, D_MODEL), D_FF),
        "mix_norm": gain(ks[5]),
        "w_in": w(ks[6], (DEPTH, D_MODEL, D_IN), D_MODEL),
        "b_gate": 0.02 * jax.random.normal(ks[7], (DEPTH, 2 * D_MODEL), jnp.float32),
        "w_branch_moba": w(ks[8], (DEPTH, D_MOBA, D_MODEL), D_MOBA),
        "w_branch_sb": w(ks[9], (DEPTH, D_SB, D_MODEL), D_SB),
        "w_out": w(ks[10], (DEPTH, D_MODEL, D_MODEL), D_MODEL),
        "ffn2_norm": gain(ks[11]),
        "ffn2_w_gate": w(ks[12], (DEPTH, D_MODEL, D_FF), D_MODEL),
        "ffn2_w_up": w(ks[13], (DEPTH, D_MODEL, D_FF), D_MODEL),
        "ffn2_w_down": w(ks[14], (DEPTH, D_FF, D_MODEL), D_FF),
        "final_norm": 1.0 + 0.05 * jax.random.normal(ks[15], (D_MODEL,), jnp.float32),
    }


def reference(x, positions, ffn1_norm, ffn1_w_gate, ffn1_w_up, ffn1_w_down, mix_norm, w_in,
              b_gate, w_branch_moba, w_branch_sb, w_out, ffn2_norm, ffn2_w_gate, ffn2_w_up,
              ffn2_w_down, final_norm):
    h = x
    for l in range(DEPTH):
        h = h + FFN_RES * swiglu(rms_norm(h, ffn1_norm[l]), ffn1_w_gate[l], ffn1_w_up[l], ffn1_w_down[l])
        h = h + hybrid_mixer(rms_norm(h, mix_norm[l]), positions, w_in[l], b_gate[l],
                             w_branch_moba[l], w_branch_sb[l], w_out[l])
        h = h + FFN_RES * swiglu(rms_norm(h, ffn2_norm[l]), ffn2_w_gate[l], ffn2_w_up[l], ffn2_w_down[l])
    return rms_norm(h, final_norm)
```

```python
import os
import contextlib
import numpy as np
import ml_dtypes
import concourse.bass as bass
import concourse.mybir as mybir
from concourse.bass_utils import run_bass_kernel_spmd

F32 = mybir.dt.float32
BF16 = mybir.dt.bfloat16
I32 = mybir.dt.int32
AF = mybir.ActivationFunctionType
ALU = mybir.AluOpType
AX = mybir.AxisListType

NCORE = 8
T = 16384
D = 1024
TL = T // NCORE
NTL = TL // 128
NG = TL // 512
FF = 2816
NF = FF // 128
EPS = 1e-6
NEGM = -30000.0
PI = float(np.pi)


class Sched:
    def __init__(self):
        self.ops = []
        self.lastw = {}
        self.readers = {}
        self.stream_cnt = {}
        self.barriers = []
        self.last_on_eng = {}

    def add(self, eng, emit, r=(), w=(), stream=None, inc=16):
        idx = len(self.ops)
        deps = {}
        for t in r:
            x = self.lastw.get(t)
            if x is not None:
                deps[x] = 'raw'
        for t in w:
            x = self.lastw.get(t)
            if x is not None and x not in deps:
                deps[x] = 'waw'
            for y in self.readers.get(t, ()):
                if y not in deps:
                    deps[y] = 'war'
        op = dict(eng=eng, emit=emit, deps=deps, stream=stream, inc=inc, idx=idx,
                  nbar=len(self.barriers), sig=None, signaler=False)
        if stream is not None:
            c = self.stream_cnt.get(stream, 0) + (inc if inc is not None else 1)
            self.stream_cnt[stream] = c
            op['sig'] = (('s', stream), c)
        else:
            self.last_on_eng[eng] = idx
        self.ops.append(op)
        for t in r:
            self.readers.setdefault(t, []).append(idx)
        for t in w:
            self.lastw[t] = idx
            self.readers[t] = []
        return idx

    def barrier(self):
        self.barriers.append(dict(compute_last=dict(self.last_on_eng), streams=dict(self.stream_cnt)))

    @staticmethod
    def _needs_wait(o, w, kind):
        if w['stream'] is not None:
            return True
        if o['stream'] is None and w['eng'] == o['eng']:
            if o['eng'] == 'pe':
                return False
            return kind == 'raw'
        return True

    def finalize(self):
        ops = self.ops
        for o in ops:
            for wi, kind in o['deps'].items():
                w = ops[wi]
                if self._needs_wait(o, w, kind) and w['stream'] is None:
                    w['signaler'] = True
        for b in self.barriers:
            for e, wi in b['compute_last'].items():
                ops[wi]['signaler'] = True
        cnt = {}
        for o in ops:
            if o['stream'] is None and o['signaler']:
                c = cnt.get(o['eng'], 0) + 1
                cnt[o['eng']] = c
                o['sig'] = (('e', o['eng']), c)
        sems = set()
        for o in ops:
            if o['sig'] is not None:
                sems.add(o['sig'][0])
        return sorted(sems, key=str)

    def emit_engine(self, eng, e, semh):
        ops = self.ops
        waited = {}

        def wait(key, c):
            if waited.get(key, 0) < c:
                e.wait_ge(semh[key], c)
                waited[key] = c

        nbar_done = 0
        for o in ops:
            if o['eng'] != eng:
                continue
            while nbar_done < o['nbar']:
                b = self.barriers[nbar_done]
                for en, wi in b['compute_last'].items():
                    if en == eng:
                        continue
                    k, c = ops[wi]['sig']
                    wait(k, c)
                for st, c in b['streams'].items():
                    wait(('s', st), c)
                nbar_done += 1
            for wi, kind in o['deps'].items():
                w = ops[wi]
                if self._needs_wait(o, w, kind):
                    k, c = w['sig']
                    wait(k, c)
            ins = o['emit'](e)
            if o['sig'] is not None:
                k, c = o['sig']
                if o['inc'] is None:
                    ins.then_inc(semh[k])
                elif o['stream'] is not None:
                    ins.then_inc(semh[k], o['inc'])
                else:
                    ins.then_inc(semh[k], 1)


def _bf(a):
    return np.ascontiguousarray(a.astype(ml_dtypes.bfloat16))


def host_consts():
    c = {}
    c['ident'] = _bf(np.eye(128, dtype=np.float32))
    j = np.arange(128)[:, None]
    s = np.arange(128)[None, :]
    c['negtri'] = _bf(np.where(j >= s, -1.0, 0.0))
    c['negones'] = _bf(-np.ones((128, 128), np.float32))
    Rm = np.zeros((128, 128), np.float32)
    for b in (0, 64):
        for i in range(8):
            Rm[b + i + 8, b + i] = -1.0
            Rm[b + i, b + i + 8] = 1.0
    c['rot32'] = Rm
    half = 8
    inv_freq = (np.float32(500000.0) ** (-np.arange(half, dtype=np.float32) * np.float32(2.0) / np.float32(16))).astype(np.float32)
    invf = np.zeros((128, 1), np.float32)
    for r in range(128):
        if (r % 64) < 16:
            invf[r, 0] = inv_freq[(r % 64) % 8]
    c['invf'] = invf
    scl = np.ones((128, 2), np.float32)
    scl[64:128, 0] = 0.125
    c['scl'] = scl
    p = np.arange(128)[:, None, None]
    i = np.arange(4)[None, :, None]
    t = np.arange(512)[None, None, :]
    valid = (128 * i + p) < t
    c['sbm01'] = _bf(np.where(valid, 1.0, 0.0))
    c['sbneg'] = _bf(np.where(valid, 0.0, NEGM))
    kp = 128 * i + p
    kb = kp // 256
    qb = t // 256
    mm = np.where(kb == qb, np.where(kp <= t, 0.0, NEGM), np.where(kb < qb, 0.0, NEGM))
    c['mobam'] = _bf(mm)
    n = np.arange(64)[:, None]
    ss = np.arange(T)[None, :]
    c['blockind'] = _bf(np.where((ss // 256) == n, 1.0, 0.0))
    c['ones_f32'] = np.ones((128, 64), np.float32)
    return c


class _PidProxy:
    def __init__(self, e):
        self.e = e

    def __mul__(self, k):
        return self.e.partition_id() * k


def build_program(debug=False, phases=3):
    nc = bass.Bass("TRN2", target_bir_lowering=False)
    S = Sched()

    def din(name, shape, dt=F32):
        return nc.dram_tensor(name, list(shape), dt, kind="ExternalInput").ap()

    x_d = din("x", [TL, D])
    pos_d = din("pos", [1, TL], I32)
    wg_d = [din(f"wg{i}", [NF, 128, 8, 128]) for i in (1, 2)]
    wu_d = [din(f"wu{i}", [NF, 128, 8, 128]) for i in (1, 2)]
    wd_d = [din(f"wd{i}", [NF, 128, D]) for i in (1, 2)]
    win_d = din("win", [40, 128, 8, 128])
    wa_d = din("wa", [4, 128, D])
    wb_d = din("wb", [4, 128, D])
    wo_d = din("wo", [8, 128, D])
    gains_d = din("gains", [128, 24])
    bgate_d = din("bgate", [128, 16])
    gfin_d = din("gfin", [1, D])
    ident_d = din("ident", [128, 128], BF16)
    negtri_d = din("negtri", [128, 128], BF16)
    negones_d = din("negones", [128, 128], BF16)
    rot32_d = din("rot32", [128, 128])
    invf_d = din("invf", [128, 1])
    scl_d = din("scl", [128, 2])
    sbm01_d = din("sbm01", [128, 4, 512], BF16)
    sbneg_d = din("sbneg", [128, 4, 512], BF16)
    mobam_d = din("mobam", [128, 4, 512], BF16)
    blockind_d = din("blockind", [64, T], BF16)
    onesf_d = din("ones_f32", [128, 64])
    y_d = nc.dram_tensor("y", [TL, D], F32, kind="ExternalOutput").ap()
    if debug:
        dbg_h1 = nc.dram_tensor("dbg_h1", [TL, D], F32, kind="ExternalOutput").ap()
        dbg_B = nc.dram_tensor("dbg_B", [128, T], BF16, kind="ExternalOutput").ap()
        dbg_q = nc.dram_tensor("dbg_q", [6 * 64, T], BF16, kind="ExternalOutput").ap()

    A_d = [[nc.dram_tensor(f"xa{i}_{g}", [1024, 512], BF16) for g in range(NG)] for i in range(3)]
    G_d = [[nc.dram_tensor(f"xg{i}_{g}", [NCORE * 1024, 512], BF16) for g in range(NG)] for i in range(3)]
    B_d = [nc.dram_tensor(f"xb{m}", [64, T], BF16) for m in range(2)]
    GB_d = [nc.dram_tensor(f"xgb{m}", [NCORE * 64, T], BF16) for m in range(2)]
    Lq_d = [nc.dram_tensor(f"xl{i}", [NCORE * 128, TL], BF16) for i in range(3)]
    LB_d = [nc.dram_tensor(f"xlb{m}", [NCORE * 64, TL], BF16) for m in range(2)]

    es = contextlib.ExitStack()
    with es:
        def sb(name, shape, dt):
            return es.enter_context(nc.sbuf_tensor(name, list(shape), dt))

        def ps(name, shape, dt):
            return es.enter_context(nc.psum_tensor(name, list(shape), dt))

        h_sb = sb("h_sb", [128, NTL, D], F32)
        big2 = sb("big2", [128, T], BF16)
        big3 = sb("big3", [128, T], BF16)
        w32 = sb("w32", [128, 4096], F32)
        wgs = sb("wgs", [128, 3, 8, 128], BF16)
        wus = sb("wus", [128, 3, 8, 128], BF16)
        wds = sb("wds", [128, 3, 512], BF16)
        xn_s = sb("xn_s", [128, 4, D], BF16)
        stg = sb("stg", [128, 2, 512], BF16)
        junk = sb("junk", [128, D], BF16)
        ident = sb("identb", [128, 128], BF16)
        negtri = sb("negtrib", [128, 128], BF16)
        negones = sb("negonesb", [128, 128], BF16)
        rot32 = sb("rot32b", [128, 128], F32)
        invf = sb("invfb", [128, 1], F32)
        scl = sb("sclb", [128, 2], F32)
        gains = sb("gainsb", [128, 24], F32)
        bgate = sb("bgateb", [128, 16], F32)
        sbm01 = sb("sbm01b", [128, 4, 512], BF16)
        sbneg = sb("sbnegb", [128, 4, 512], BF16)
        mobam = sb("mobamb", [128, 4, 512], BF16)
        onesf = sb("onesfb", [128, 64], F32)
        ss = sb("ssb", [128, 96], F32)
        rstd = sb("rstdb", [128, 96], F32)
        small = sb("smallb", [128, 512], F32)
        posi = sb("posib", [128, 512], I32)
        epsc = sb("epscb", [128, 1], F32)

        psA = ps("psA", [128, 1024], F32)
        psB = ps("psB", [128, 1024], F32)
        psC = ps("psC", [128, 1024], F32)
        psD = ps("psD", [128, 1024], F32)
        banks = [psA[:, 0:512], psA[:, 512:1024], psB[:, 0:512], psB[:, 512:1024],
                 psC[:, 0:512], psC[:, 512:1024], psD[:, 0:512], psD[:, 512:1024]]

        xnT = big2[:, 0:4096].rearrange("p (k n) -> p k n", n=512)
        actT = big2[:, 4096:4096 + NF * 512].rearrange("p (k n) -> p k n", n=512)
        sgt = w32[:, 0:512]
        xf32 = w32[:, 512:1024]
        rt1 = w32[:, 1024:1536]
        rt2 = w32[:, 1536:2048]
        cfull = w32[:, 2048:2560]
        sfull = w32[:, 2560:3072]
        posf = w32[:, 3072:3584]
        angt = w32[:, 3584:4096]

        def dma(eng, out, in_, r=(), w=(), stream=None):
            return S.add(eng, lambda e, out=out, in_=in_: e.dma_start(out=out, in_=in_), r=r, w=w, stream=stream)

        for (dst, src, nm) in [(ident, ident_d, 'ident'), (negtri, negtri_d, 'negtri'), (negones, negones_d, 'negones'),
                               (rot32, rot32_d, 'rot32'), (invf, invf_d, 'invf'), (scl, scl_d, 'scl'),
                               (gains, gains_d, 'gains'), (bgate, bgate_d, 'bgate'), (onesf, onesf_d, 'onesf')]:
            dma('sp', dst[:, :], src[:, :], w=[nm], stream='const')
        dma('sp', sbm01[:, :, :], sbm01_d[:, :, :], w=['sbm01'], stream='const')
        dma('sp', sbneg[:, :, :], sbneg_d[:, :, :], w=['sbneg'], stream='const')
        dma('sp', mobam[:, :, :], mobam_d[:, :, :], w=['mobam'], stream='const')
        S.add('dve', lambda e: e.memset(ss[:, :], 0.0), w=['ss'])
        S.add('dve', lambda e: e.memset(epsc[:, :], EPS), w=['epsc'])
        S.barrier()

        norm_ctr = [0]

        def rmsnorm_group(tiles, gain_col0):
            cols = []
            for i, tt in enumerate(tiles):
                col = norm_ctr[0]
                norm_ctr[0] += 1
                cols.append(col)
                S.add('act', lambda e, tt=tt, col=col: e.activation(out=junk[:, :], in_=h_sb[:, tt, :], func=AF.Square,
                                                                     accum_out=ss[:, col:col + 1]),
                      r=[('h', tt), 'ss'], w=[('ss', col), 'junk'])
            c0 = cols[0]
            S.add('act', lambda e, c0=c0: e.activation(out=rstd[:, c0:c0 + 4], in_=ss[:, c0:c0 + 4], func=AF.Sqrt,
                                                       scale=1.0 / D, bias=epsc[:, 0:1]),
                  r=[('ss', c) for c in cols] + ['epsc'], w=[('rstd0', c0)])
            S.add('dve', lambda e, c0=c0: e.reciprocal(out=rstd[:, c0:c0 + 4], in_=rstd[:, c0:c0 + 4]),
                  r=[('rstd0', c0)], w=[('rstd', c) for c in cols])
            xn4 = xn_s
            for i, tt in enumerate(tiles):
                S.add('dve', lambda e, tt=tt, col=cols[i], i=i: e.tensor_scalar(
                    out=xn4[:, i, :], in0=h_sb[:, tt, :], scalar1=rstd[:, col:col + 1], scalar2=None, op0=ALU.mult),
                      r=[('h', tt), ('rstd', cols[i])], w=[('xn4', i)])
            for rnd in range(4):
                for kk in range(2):
                    k = rnd * 2 + kk
                    bk = 6 + kk
                    for i in range(4):
                        S.add('pe', lambda e, k=k, bk=bk, i=i: e.matmul(
                            banks[bk][:, i * 128:(i + 1) * 128], lhsT=xn4[:, i, k * 128:(k + 1) * 128], rhs=ident[:, :],
                            start=True, stop=True),
                              r=[('xn4', i), 'ident'], w=[('bank', bk)])
                    gc = gain_col0 + k
                    eng = 'dve' if kk == 0 else 'act'
                    if eng == 'dve':
                        S.add('dve', lambda e, k=k, bk=bk, gc=gc: e.tensor_scalar(
                            out=xnT[:, k, :], in0=banks[bk], scalar1=gains[:, gc:gc + 1], scalar2=None, op0=ALU.mult),
                              r=[('bank', bk), 'gains'], w=[('xnT', k)])
                    else:
                        S.add('act', lambda e, k=k, bk=bk, gc=gc: e.activation(
                            out=xnT[:, k, :], in_=banks[bk], func=AF.Copy, scale=gains[:, gc:gc + 1]),
                              r=[('bank', bk), 'gains'], w=[('xnT', k)])

        wcnt = {'g': 0, 'u': 0, 'd': 0, 'c': 0}

        def ffn_group(tiles, widx):
            for half in range(2):
                for f in range(NF):
                    if half == 0:
                        sg_ = wcnt['g'] % 3
                        wcnt['g'] += 1
                        S.add('pool', lambda e, f=f, sg_=sg_: e.dma_start(out=wgs[:, sg_, :, :], in_=wg_d[widx][f, :, :, :]),
                              w=[('wgs', sg_)], stream=('wg', sg_))
                        S.add('pool', lambda e, f=f, sg_=sg_: e.dma_start(out=wus[:, sg_, :, :], in_=wu_d[widx][f, :, :, :]),
                              w=[('wus', sg_)], stream=('wu', sg_))
                        for k in range(8):
                            S.add('pe', lambda e, k=k, sg_=sg_: e.matmul(banks[4], lhsT=wgs[:, sg_, k, :], rhs=xnT[:, k, :],
                                                                          start=(k == 0), stop=(k == 7)),
                                  r=[('wgs', sg_), ('xnT', k)], w=[('bank', 4)])
                        for k in range(8):
                            S.add('pe', lambda e, k=k, sg_=sg_: e.matmul(banks[5], lhsT=wus[:, sg_, k, :], rhs=xnT[:, k, :],
                                                                          start=(k == 0), stop=(k == 7)),
                                  r=[('wus', sg_), ('xnT', k)], w=[('bank', 5)])
                        S.add('act', lambda e: e.activation(out=sgt, in_=banks[4], func=AF.Silu),
                              r=[('bank', 4)], w=['sgt'])
                        S.add('dve', lambda e, f=f: e.tensor_tensor(out=actT[:, f, :], in0=sgt, in1=banks[5], op=ALU.mult),
                              r=['sgt', ('bank', 5)], w=[('actT', f)])
                    sd = wcnt['d'] % 3
                    wcnt['d'] += 1
                    S.add('pool', lambda e, f=f, sd=sd, half=half: e.dma_start(
                        out=wds[:, sd, :], in_=wd_d[widx][f, :, half * 512:(half + 1) * 512]),
                          w=[('wds', sd)], stream=('wd', sd))
                    for ts in range(4):
                        S.add('pe', lambda e, f=f, sd=sd, ts=ts: e.matmul(
                            banks[ts], lhsT=actT[:, f, ts * 128:(ts + 1) * 128], rhs=wds[:, sd, :],
                            start=(f == 0), stop=(f == NF - 1)),
                              r=[('actT', f), ('wds', sd)], w=[('bank', ts)])
                for ts in range(4):
                    tt = tiles[ts]
                    S.add('dve', lambda e, ts=ts, tt=tt, half=half: e.scalar_tensor_tensor(
                        out=h_sb[:, tt, half * 512:(half + 1) * 512], in0=banks[ts], scalar=0.5,
                        in1=h_sb[:, tt, half * 512:(half + 1) * 512], op0=ALU.mult, op1=ALU.add),
                          r=[('bank', ts), ('h', tt)], w=[('h', tt)])

        def gather_group(g):
            for i in (2, 1, 0):
                S.add('pool', lambda e, i=i, g=g: e.collective_compute(
                    "AllGather", ALU.bypass, replica_groups=[list(range(NCORE))],
                    ins=[A_d[i][g].ap().opt()], outs=[G_d[i][g].ap().opt()]),
                      r=[('A', i, g)], w=[('G', i, g)], stream=('cc', i), inc=None)

        for tt in range(NTL):
            dma('sp', h_sb[:, tt, :], x_d[tt * 128:(tt + 1) * 128, :], w=[('h', tt)], stream=('x', tt))

        for g in range(NG):
            tiles = [g * 4 + i for i in range(4)]
            rmsnorm_group(tiles, 0)
            ffn_group(tiles, 0)
            if g > 0:
                gather_group(g - 1)
            if debug:
                for tt in tiles:
                    dma('sp', dbg_h1[tt * 128:(tt + 1) * 128, :], h_sb[:, tt, :], r=[('h', tt)], stream='dbg')
            rmsnorm_group(tiles, 8)
            S.add('sp', lambda e, g=g: e.dma_start(out=posi[:, :], in_=pos_d[0:1, g * 512:(g + 1) * 512].broadcast_to([128, 512])),
                  w=['posi'], stream='pos')
            S.add('dve', lambda e: e.tensor_copy(out=posf, in_=posi[:, :]), r=['posi'], w=['posf'])
            S.add('dve', lambda e: e.tensor_scalar(out=angt, in0=posf, scalar1=invf[:, 0:1], scalar2=None, op0=ALU.mult),
                  r=['posf', 'invf'], w=['angt'])
            MAGIC = 12582912.0
            for (dst, shift) in ((sfull, 0.0), (cfull, 0.5 * PI)):
                if shift != 0.0:
                    S.add('dve', lambda e, shift=shift: e.tensor_scalar(out=rt2, in0=angt, scalar1=shift, scalar2=None, op0=ALU.add),
                          r=['angt'], w=['rt2'])
                    src_ = rt2
                    srct = 'rt2'
                else:
                    src_ = angt
                    srct = 'angt'
                S.add('dve', lambda e, src_=src_: e.tensor_scalar(out=rt1, in0=src_, scalar1=1.0 / (2 * PI), scalar2=MAGIC,
                                                                  op0=ALU.mult, op1=ALU.add), r=[srct], w=['rt1'])
                S.add('dve', lambda e: e.tensor_scalar(out=rt1, in0=rt1, scalar1=-MAGIC, scalar2=None, op0=ALU.add),
                      r=['rt1'], w=['rt1'])
                S.add('dve', lambda e, src_=src_: e.scalar_tensor_tensor(out=rt1, in0=rt1, scalar=-2 * PI, in1=src_,
                                                                         op0=ALU.mult, op1=ALU.add), r=['rt1', srct], w=['rt1'])
                S.add('dve', lambda e: e.tensor_scalar(out=rt1, in0=rt1, scalar1=-PI, scalar2=PI, op0=ALU.max, op1=ALU.min),
                      r=['rt1'], w=['rt1'])
                S.add('act', lambda e, dst=dst: e.activation(out=dst, in_=rt1, func=AF.Sin), r=['rt1'], w=[('cs', shift)])
            for cc in range(24):
                hd, typ = cc // 3, cc % 3
                sc = wcnt['c'] % 3
                wcnt['c'] += 1
                bk = 6 + (cc % 2)
                S.add('pool', lambda e, cc=cc, sc=sc: e.dma_start(out=wgs[:, sc, :, :], in_=win_d[cc, :, :, :]),
                      w=[('wgs', sc)], stream=('wg', sc))
                for k in range(8):
                    S.add('pe', lambda e, k=k, sc=sc, bk=bk: e.matmul(banks[bk], lhsT=wgs[:, sc, k, :], rhs=xnT[:, k, :],
                                                                      start=(k == 0), stop=(k == 7)),
                          r=[('wgs', sc), ('xnT', k)], w=[('bank', bk)])
                so = cc % 2
                if typ == 0:
                    S.add('act', lambda e, bk=bk: e.activation(out=xf32, in_=banks[bk], func=AF.Copy),
                          r=[('bank', bk)], w=['xf32'])
                    S.add('pe', lambda e: e.matmul(banks[5], lhsT=rot32[:, :], rhs=xf32, start=True, stop=True),
                          r=['xf32', 'rot32'], w=[('bank', 5)])
                    S.add('dve', lambda e: e.tensor_tensor(out=rt1, in0=xf32, in1=cfull, op=ALU.mult),
                          r=['xf32', ('cs', 0.5 * PI)], w=['rt1'])
                    S.add('dve', lambda e: e.tensor_tensor(out=rt2, in0=banks[5], in1=sfull, op=ALU.mult),
                          r=[('bank', 5), ('cs', 0.0)], w=['rt2'])
                    S.add('dve', lambda e, so=so: e.tensor_tensor(out=stg[:, so, :], in0=rt1, in1=rt2, op=ALU.add),
                          r=['rt1', 'rt2'], w=[('stg', so)])
                else:
                    S.add('dve', lambda e, bk=bk, so=so, typ=typ: e.tensor_scalar(
                        out=stg[:, so, :], in0=banks[bk], scalar1=scl[:, typ - 1:typ], scalar2=None, op0=ALU.mult),
                          r=[('bank', bk), 'scl'], w=[('stg', so)])
                S.add('sp', lambda e, so=so, typ=typ, hd=hd, g=g: e.dma_start(
                    out=A_d[typ][g][hd * 128:(hd + 1) * 128, :], in_=stg[:, so, :]),
                      r=[('stg', so)], w=[('A', typ, g)], stream=('aout', so))

        gather_group(NG - 1)
        S.barrier()

        pid = {}

        for typ in (2, 1, 0):
            for g in range(NG):
                S.add('sp', lambda e, typ=typ, g=g: e.dma_start(
                    out=Lq_d[typ].ap().rearrange("(s o r) t -> s o r t", s=NCORE, o=1)[:, :, :, g * 512:(g + 1) * 512],
                    in_=G_d[typ][g].ap().rearrange("(s h r) t -> s h r t", s=NCORE, h=NCORE)[:, bass.ds(pid['sp'], 1), :, :]),
                      r=[('G', typ, g)], w=[('Lq', typ)], stream=('lq', typ))
        S.barrier()

        def gsl(typ, src, off64, eng, cols=None):
            base = src * 128 + off64
            if cols is None:
                return Lq_d[typ].ap()[base:base + 64, :]
            return Lq_d[typ].ap()[base:base + 64, cols[0]:cols[1]]

        if debug:
            for it in range(6):
                typ, off = it // 2, (it % 2) * 64
                for src in range(NCORE):
                    S.add('sp', lambda e, it=it, typ=typ, off=off, src=src: e.dma_start(
                        out=big2[0:64, src * TL:(src + 1) * TL],
                        in_=gsl(typ, src, off, 'sp')), r=[('Lq', typ)], w=['dbgbuf'], stream='dbg')
                S.add('sp', lambda e, it=it: e.dma_start(out=dbg_q[it * 64:(it + 1) * 64, :], in_=big2[0:64, :]),
                      r=['dbgbuf'], w=['dbgq'], stream='dbg2')
            S.barrier()

        if phases >= 2:
            Vsb = big3[:, 0:8320].rearrange("p (a n) -> p a n", n=65)
            o = 8320
            Abuf = [big3[:, o + i * 1024:o + (i + 1) * 1024] for i in range(2)]
            o += 2048
            Lbuf = [big3[:, o + i * 1024:o + (i + 1) * 1024] for i in range(2)]
            o += 2048
            LaccA = [big3[:, o + i * 512:o + (i + 1) * 512] for i in range(3)]
            o += 1536
            LaccB = [big3[:, o + i * 512:o + (i + 1) * 512] for i in range(2)]
            o += 1024
            Qp = [big3[:, o + i * 512:o + (i + 1) * 512] for i in range(2)]
            qsb = Qp
            o += 1024
            selpad = big3[:, o:o + 128]
            o += 128
            vstage = big3[:, 8320:8320 + 2048]
            assert o <= T, o
            ebuf = [w32[:, i * 1024:(i + 1) * 1024] for i in range(2)]
            Lacc32 = w32[:, 2048:2560]
            Lsum = w32[:, 2560:3072]
            ystg32 = w32[:, 3072:3584]
            ybc = w32[:, 3584:4096]
            gsb = small[:, 0:64]
            m8 = small[:, 64:72]
            kms = small[:, 128:192]
            kmh = small[:, 192:256]
            recrow = small[:, 256:512]

            kmhi = stg[:, 0, 0:64]
            kmlo = stg[:, 0, 64:128]
            ystgb = stg[:, 1, :]

            def load_kv(ktyp, koff, vtyp, voff, tag):
                for src in range(NCORE):
                    S.add('sp', lambda e, src=src: e.dma_start(out=big2[0:64, src * TL:(src + 1) * TL],
                                                                 in_=gsl(ktyp, src, koff, 'sp')),
                          r=[('Lq', ktyp)], w=['kT'], stream=('kv', src % 2))
                for src in range(NCORE):
                    S.add('sp', lambda e, src=src: e.dma_start(out=vstage[0:64, :],
                                                                 in_=gsl(vtyp, src, voff, 'sp')),
                          r=[('Lq', vtyp)], w=['vstage'], stream=('vst', 0))
                    for half in range(2):
                        bk = 6 + half
                        for i in range(8):
                            tl = half * 8 + i
                            S.add('pe', lambda e, bk=bk, i=i, tl=tl: e.matmul(
                                banks[bk][:, i * 64:(i + 1) * 64], lhsT=vstage[0:64, tl * 128:(tl + 1) * 128],
                                rhs=ident[0:64, 0:64], start=True, stop=True),
                                  r=['vstage', 'ident'], w=[('bank', bk)])
                        t0 = src * 16 + half * 8
                        S.add('dve', lambda e, bk=bk, t0=t0: e.tensor_copy(
                            out=Vsb[:, t0:t0 + 8, 0:64], in_=banks[bk].rearrange("p (a n) -> p a n", n=64)),
                              r=[('bank', bk)], w=['Vsb'])

            S.add('dve', lambda e: e.memset(Vsb[:, :, 64:65], 1.0), w=['Vsb'])
            load_kv(2, 0, 2, 64, 'sb')
            steps = []
            for qg in range(T // 512):
                n = 2 * qg + 2
                for s in range(n - 1, -1, -1):
                    steps.append((qg, s, s == n - 1, s == 0))
            NS = len(steps)
            zts = [psA, psB, psC]

            def qload(qg):
                src, gg = qg // 4, qg % 4
                sl = qg % 2
                S.add('sp', lambda e, src=src, gg=gg, sl=sl: e.dma_start(
                    out=qsb[sl][0:64, :], in_=gsl(1, src, 64, 'sp', (gg * 512, (gg + 1) * 512))),
                      r=[('Lq', 1)], w=[('qsb', sl)], stream=('q', sl))

            def z_mm(i):
                qg, s, first, last = steps[i]
                zs = i % 3
                for j, kt in enumerate((2 * s + 1, 2 * s)):
                    S.add('pe', lambda e, kt=kt, zs=zs, j=j, qg=qg: e.matmul(
                        banks[2 * zs + j], lhsT=big2[0:64, kt * 128:(kt + 1) * 128], rhs=qsb[qg % 2][0:64, :],
                        start=True, stop=True),
                          r=['kT', ('qsb', qg % 2)], w=[('Z', zs)])

            def act12(i):
                qg, s, first, last = steps[i]
                zs = i % 3
                es_ = i % 2
                S.add('act', lambda e, zs=zs, es_=es_: e.activation(out=ebuf[es_], in_=zts[zs][:, :], func=AF.Exp),
                      r=[('Z', zs)], w=[('e', es_)])
                S.add('act', lambda e, es_=es_: e.activation(out=Lbuf[es_], in_=ebuf[es_], func=AF.Ln, bias=1.0),
                      r=[('e', es_)], w=[('L', es_)])
                diag = s >= 2 * qg
                if diag:
                    dj = (s - 2 * qg) * 2
                    S.add('pool', lambda e, es_=es_, dj=dj: e.tensor_tensor(
                        out=Lbuf[es_][:, 0:512], in0=Lbuf[es_][:, 0:512], in1=sbm01[:, dj + 1, :], op=ALU.mult),
                          r=[('L', es_), 'sbm01'], w=[('L', es_)])
                    S.add('pool', lambda e, es_=es_, dj=dj: e.tensor_tensor(
                        out=Lbuf[es_][:, 512:1024], in0=Lbuf[es_][:, 512:1024], in1=sbm01[:, dj, :], op=ALU.mult),
                          r=[('L', es_), 'sbm01'], w=[('L', es_)])

            def lacc(i):
                qg, s, first, last = steps[i]
                es_ = i % 2
                Lh = Lbuf[es_][:, 0:512]
                Ll = Lbuf[es_][:, 512:1024]
                if not first:
                    S.add('dve', lambda e, i=i, Lh=Lh: e.tensor_tensor(out=LaccB[i % 2], in0=Lacc32, in1=Lh, op=ALU.add),
                          r=[('L', es_), 'Lacc32'], w=[('LaccB', i % 2)])
                if last:
                    return
                if first:
                    S.add('dve', lambda e, Lh=Lh, Ll=Ll: e.tensor_tensor(out=Lacc32, in0=Lh, in1=Ll, op=ALU.add),
                          r=[('L', es_)], w=['Lacc32'])
                else:
                    S.add('dve', lambda e, Lh=Lh, Ll=Ll: e.tensor_tensor(out=Lsum, in0=Lh, in1=Ll, op=ALU.add),
                          r=[('L', es_)], w=['Lsum'])
                    S.add('dve', lambda e: e.tensor_tensor(out=Lacc32, in0=Lacc32, in1=Lsum, op=ALU.add),
                          r=['Lsum', 'Lacc32'], w=['Lacc32'])
                S.add('dve', lambda e, i=i: e.tensor_copy(out=LaccA[(i + 1) % 3], in_=Lacc32),
                      r=['Lacc32'], w=[('LaccA', (i + 1) % 3)])

            def p_mm(i):
                qg, s, first, last = steps[i]
                zs = i % 3
                es_ = i % 2
                diag = s >= 2 * qg
                dj = (s - 2 * qg) * 2
                Lh = Lbuf[es_][:, 0:512]
                for j in range(2):
                    Lme = Lbuf[es_][:, j * 512:(j + 1) * 512]
                    seq = [(negtri[:, :], Lme)]
                    if j == 0:
                        if not first:
                            seq.append((negones[:, :], LaccA[i % 3]))
                    else:
                        if first:
                            seq.append((negones[:, :], Lh))
                        else:
                            seq.append((negones[:, :], LaccB[i % 2]))
                    if diag:
                        seq.append((ident[:, :], sbneg[:, dj + (1 - j), :]))
                    for m, (lt, rh) in enumerate(seq):
                        S.add('pe', lambda e, zs=zs, j=j, lt=lt, rh=rh, m=m, nm=len(seq): e.matmul(
                            banks[2 * zs + j], lhsT=lt, rhs=rh, start=False, stop=(m == nm - 1)),
                              r=[('L', es_), ('LaccA', i % 3), ('LaccB', i % 2), 'sbneg', 'negtri', 'negones', 'ident', ('e', es_)],
                              w=[('Z', zs)])

            def act3(i):
                zs = i % 3
                as_ = i % 2
                S.add('act', lambda e, zs=zs, as_=as_: e.activation(out=Abuf[as_], in_=zts[zs][:, :], func=AF.Exp),
                      r=[('Z', zs)], w=[('A', as_)])

            def pv_mm(i):
                qg, s, first, last = steps[i]
                as_ = i % 2
                for j, kt in enumerate((2 * s + 1, 2 * s)):
                    S.add('pe', lambda e, kt=kt, j=j, as_=as_, first=first, last=last: e.matmul(
                        banks[6][0:64, :], lhsT=Vsb[:, kt, 0:64], rhs=Abuf[as_][:, j * 512:(j + 1) * 512],
                        start=(first and j == 0), stop=(last and j == 1)),
                          r=[('A', as_), 'Vsb'], w=['Y'])
                if last:
                    S.add('dve', lambda e: e.tensor_copy(out=ystgb[0:64, :], in_=banks[6][0:64, :]), r=['Y'], w=['ystgb'])
                    S.add('sp', lambda e, qg=qg: e.dma_start(out=B_d[1].ap()[:, qg * 512:(qg + 1) * 512], in_=ystgb[0:64, :]),
                          r=['ystgb'], w=[('B', 1)], stream=('bout', 0))

            qload(0)
            z_mm(0)
            act12(0)
            lacc(0)
            if NS > 1:
                if steps[1][2]:
                    qload(steps[1][0])
                z_mm(1)
            for i in range(NS):
                if i + 1 < NS:
                    act12(i + 1)
                    lacc(i + 1)
                p_mm(i)
                if i + 2 < NS:
                    if steps[i + 2][2]:
                        qload(steps[i + 2][0])
                    z_mm(i + 2)
                act3(i)
                pv_mm(i)

            S.barrier()
            S.add('pool', lambda e: e.collective_compute("AllGather", ALU.bypass, replica_groups=[list(range(NCORE))],
                                                         ins=[B_d[1].ap().opt()], outs=[GB_d[1].ap().opt()]),
                  r=[('B', 1)], w=[('GB', 1)], stream=('cc', 3), inc=None)
            load_kv(0, 64, 1, 0, 'moba')
            dma('sp', big2[64:128, :], blockind_d[:, :], w=['kT'], stream=('kv', 0))
            S.add('dve', lambda e: e.tensor_reduce(out=kms[0:64, :], in_=big2[0:64, :].rearrange("p (n b) -> p n b", b=256),
                                                   axis=AX.X, op=ALU.add), r=['kT'], w=['kms'])
            S.add('dve', lambda e: e.tensor_scalar(out=kms[0:64, :], in0=kms[0:64, :], scalar1=1.0 / 256, scalar2=None, op0=ALU.mult),
                  r=['kms'], w=['kms'])
            S.add('dve', lambda e: e.tensor_copy(out=kmhi[0:64, :], in_=kms[0:64, :]), r=['kms'], w=['kmhi'])
            S.add('dve', lambda e: e.tensor_copy(out=kmh[0:64, :], in_=kmhi[0:64, :]), r=['kmhi'], w=['kmh'])
            S.add('dve', lambda e: e.tensor_tensor(out=kmh[0:64, :], in0=kms[0:64, :], in1=kmh[0:64, :], op=ALU.subtract),
                  r=['kms', 'kmh'], w=['kmh'])
            S.add('dve', lambda e: e.tensor_copy(out=kmlo[0:64, :], in_=kmh[0:64, :]), r=['kmh'], w=['kmlo'])
            S.add('dve', lambda e: e.memset(gsb, -1e30), w=['gsb'])
            S.add('dve', lambda e: e.memset(selpad[:, 0:64], 0.0), w=['selpad'])

            def moba_prep_stages(qg):
                sl = qg % 2
                src, gg = qg // 4, qg % 4
                st = []

                def s_load():
                    S.add('sp', lambda e: e.dma_start(
                        out=Qp[sl][0:64, :], in_=gsl(0, src, 0, 'sp', (gg * 512, (gg + 1) * 512))),
                          r=[('Lq', 0)], w=[('Qp', sl)], stream=('q', sl))
                st.append(s_load)
                for ci in range(4):
                    c = qg * 4 + ci
                    own = c // 2

                    def s_gate(ci=ci, own=own):
                        if own <= 3:
                            S.add('dve', lambda e: e.memset(selpad[:, 64:128], NEGM), w=['selpad'])
                            S.add('dve', lambda e: e.memset(selpad[:, 64:64 + own + 1], 0.0), w=['selpad'])
                        else:
                            for m, km in enumerate((kmhi, kmlo)):
                                S.add('pe', lambda e, km=km, m=m: e.matmul(
                                    banks[7][:, 0:64], lhsT=Qp[sl][0:64, ci * 128:(ci + 1) * 128], rhs=km[0:64, :],
                                    start=(m == 0), stop=(m == 1)),
                                      r=[('Qp', sl), 'kmhi', 'kmlo'], w=[('bank', 7)])
                            S.add('dve', lambda e: e.tensor_copy(out=gsb[:, 0:own], in_=banks[7][:, 0:own]),
                                  r=[('bank', 7)], w=['gsb'])
                            S.add('dve', lambda e: e.max(out=m8, in_=gsb), r=['gsb'], w=['m8'])
                            S.add('dve', lambda e: e.tensor_scalar(out=selpad[:, 64:128], in0=gsb, scalar1=m8[:, 2:3],
                                                                   scalar2=NEGM, op0=ALU.is_lt, op1=ALU.mult),
                                  r=['gsb', 'm8'], w=['selpad'])
                            S.add('dve', lambda e: e.memset(selpad[:, 64 + own:64 + own + 1], 0.0), w=['selpad'])

                    def s_tr(ci=ci):
                        S.add('pe', lambda e: e.matmul(banks[7][:, 0:128], lhsT=selpad, rhs=ident[:, :], start=True, stop=True),
                              r=['selpad', 'ident'], w=[('bank', 7)])
                        S.add('dve', lambda e: e.tensor_copy(out=Qp[sl][64:128, ci * 128:(ci + 1) * 128],
                                                             in_=banks[7][64:128, 0:128]),
                              r=[('bank', 7)], w=[('Qp', sl)])
                    st.append(s_gate)
                    st.append(s_tr)
                return st

            msteps = []
            for qg in range(T // 512):
                nb = 2 * qg + 2
                for n in range(nb):
                    msteps.append((qg, n, n == 0, n == nb - 1))
            NM = len(msteps)

            def s_mm(i):
                qg, n, first, last = msteps[i]
                zs = i % 2
                zb = [0, 2][zs]
                diag = n >= 2 * qg
                for j in range(2):
                    kt = 2 * n + j
                    S.add('pe', lambda e, kt=kt, zb=zb, j=j, qg=qg, diag=diag: e.matmul(
                        banks[zb + j], lhsT=big2[:, kt * 128:(kt + 1) * 128], rhs=Qp[qg % 2], start=True, stop=(not diag)),
                          r=['kT', ('Qp', qg % 2)], w=[('Z', zs)])
                    if diag:
                        dj = (n - 2 * qg) * 2 + j
                        S.add('pe', lambda e, zb=zb, j=j, dj=dj: e.matmul(
                            banks[zb + j], lhsT=ident[:, :], rhs=mobam[:, dj, :], start=False, stop=True),
                              r=['mobam', 'ident'], w=[('Z', zs)])

            def m_act(i):
                zs = i % 2
                zt = [psA, psB][zs]
                S.add('act', lambda e, zt=zt, zs=zs: e.activation(out=Abuf[zs], in_=zt[:, :], func=AF.Exp, scale=0.125),
                      r=[('Z', zs)], w=[('A', zs)])

            def m_pv(i):
                qg, n, first, last = msteps[i]
                zs = i % 2
                for j in range(2):
                    kt = 2 * n + j
                    S.add('pe', lambda e, kt=kt, j=j, zs=zs, first=first, last=last: e.matmul(
                        banks[6][0:65, :], lhsT=Vsb[:, kt, 0:65], rhs=Abuf[zs][:, j * 512:(j + 1) * 512],
                        start=(first and j == 0), stop=(last and j == 1)),
                          r=[('A', zs), 'Vsb'], w=['Y'])
                if last:
                    S.add('dve', lambda e: e.reciprocal(out=ystg32[64:65, :], in_=banks[6][64:65, :]), r=['Y'], w=['rec'])
                    S.add('pe', lambda e: e.matmul(banks[7][0:64, :], lhsT=onesf[64:65, 0:64], rhs=ystg32[64:65, :],
                                                    start=True, stop=True), r=['rec', 'onesf'], w=[('bank', 7)])
                    S.add('act', lambda e: e.activation(out=ybc[0:64, :], in_=banks[7][0:64, :], func=AF.Copy),
                          r=[('bank', 7)], w=['ybc'])
                    S.add('dve', lambda e: e.tensor_tensor(out=ystgb[0:64, :], in0=banks[6][0:64, :], in1=ybc[0:64, :], op=ALU.mult),
                          r=['Y', 'ybc'], w=['ystgb'])
                    S.add('sp', lambda e, qg=qg: e.dma_start(out=B_d[0].ap()[:, qg * 512:(qg + 1) * 512], in_=ystgb[0:64, :]),
                          r=['ystgb'], w=[('B', 0)], stream=('bout', 0))

            for f_ in moba_prep_stages(0):
                f_()
            pending = moba_prep_stages(1)
            s_mm(0)
            for i in range(NM):
                qg_i = msteps[i][0]
                if i + 1 < NM:
                    if msteps[i + 1][2]:
                        for f_ in pending:
                            f_()
                        nq = msteps[i + 1][0] + 1
                        pending = moba_prep_stages(nq) if nq < T // 512 else []
                    s_mm(i + 1)
                m_act(i)
                m_pv(i)
                if pending:
                    pending.pop(0)()

        S.barrier()
        if debug and phases >= 2:
            dma('sp', dbg_B[0:64, :], B_d[0].ap()[:, :], r=[('B', 0)], stream='dbg')
            dma('sp', dbg_B[64:128, :], B_d[1].ap()[:, :], r=[('B', 1)], stream='dbg')

        if phases >= 3:
            S.add('pool', lambda e: e.collective_compute("AllGather", ALU.bypass, replica_groups=[list(range(NCORE))],
                                                         ins=[B_d[0].ap().opt()], outs=[GB_d[0].ap().opt()]),
                  r=[('B', 0)], w=[('GB', 0)], stream=('cc', 4), inc=None)
            S.barrier()
            for m in (1, 0):
                S.add('sp', lambda e, m=m: e.dma_start(out=LB_d[m].ap()[:, :], in_=GB_d[m].ap()[:, bass.ds(pid['sp'] * TL, TL)]),
                      r=[('GB', m)], w=['LB'], stream=('lq', 3))
            S.barrier()
            wa_s = big3[:, 0:4096].rearrange("p (k n) -> p k n", n=D)
            wb_s = big3[:, 4096:8192].rearrange("p (k n) -> p k n", n=D)
            wo_s = big3[:, 8192:16384].rearrange("p (k n) -> p k n", n=D)
            for kk in range(4):
                S.add('pool', lambda e, kk=kk: e.dma_start(out=wa_s[:, kk, :], in_=wa_d[kk, :, :]), w=['wa'], stream='w3')
                S.add('pool', lambda e, kk=kk: e.dma_start(out=wb_s[:, kk, :], in_=wb_d[kk, :, :]), w=['wb'], stream='w3')
            for kk in range(8):
                S.add('pool', lambda e, kk=kk: e.dma_start(out=wo_s[:, kk, :], in_=wo_d[kk, :, :]), w=['wo'], stream='w3')
            gfin = w32[:, 3072:4096]
            S.add('sp', lambda e: e.dma_start(out=gfin, in_=gfin_d[0:1, :].broadcast_to([128, D])), w=['gfin'], stream='gfin')
            S.barrier()
            for g in range(NG):
                tiles = [g * 4 + i for i in range(4)]
                rmsnorm_group(tiles, 8)
                for m in range(2):
                    for kk in range(4):
                        for hh in range(2):
                            hd = kk * 2 + hh
                            S.add('sp', lambda e, m=m, kk=kk, hh=hh, hd=hd, g=g: e.dma_start(
                                out=actT[hh * 64:(hh + 1) * 64, m * 4 + kk, :],
                                in_=LB_d[m].ap()[hd * 64:(hd + 1) * 64, g * 512:(g + 1) * 512]),
                                  r=['LB'], w=[('actT', m * 4 + kk)], stream=('yl', m))
                mT = actT[:, 8:16, :]
                for j in range(8):
                    for m in range(2):
                        cc = 24 + m * 8 + j
                        sc = wcnt['c'] % 3
                        wcnt['c'] += 1
                        S.add('pool', lambda e, cc=cc, sc=sc: e.dma_start(out=wgs[:, sc, :, :], in_=win_d[cc, :, :, :]),
                              w=[('wgs', sc)], stream=('wg', sc))
                        for k in range(8):
                            S.add('pe', lambda e, k=k, sc=sc, m=m: e.matmul(banks[4 + m], lhsT=wgs[:, sc, k, :], rhs=xnT[:, k, :],
                                                                           start=(k == 0), stop=(k == 7)),
                                  r=[('wgs', sc), ('xnT', k)], w=[('bank', 4 + m)])
                        gt = [sgt, xf32][m]
                        S.add('act', lambda e, m=m, j=j, gt=gt: e.activation(out=gt, in_=banks[4 + m], func=AF.Sigmoid,
                                                                            bias=bgate[:, m * 8 + j:m * 8 + j + 1]),
                              r=[('bank', 4 + m), 'bgate'], w=[('gt', m)])
                        ws = [wa_s, wb_s][m]
                        for kk in range(4):
                            S.add('pe', lambda e, kk=kk, m=m, j=j, ws=ws: e.matmul(
                                banks[6 + m], lhsT=ws[:, kk, j * 128:(j + 1) * 128], rhs=actT[:, m * 4 + kk, :],
                                start=(kk == 0), stop=(kk == 3)),
                                  r=['wa', 'wb', ('actT', m * 4 + kk)], w=[('bank', 6 + m)])
                    S.add('dve', lambda e: e.tensor_tensor(out=rt1, in0=sgt, in1=banks[6], op=ALU.mult),
                          r=[('gt', 0), ('bank', 6)], w=['rt1'])
                    S.add('dve', lambda e: e.tensor_tensor(out=rt2, in0=xf32, in1=banks[7], op=ALU.mult),
                          r=[('gt', 1), ('bank', 7)], w=['rt2'])
                    S.add('dve', lambda e, j=j: e.tensor_tensor(out=mT[:, j, :], in0=rt1, in1=rt2, op=ALU.add),
                          r=['rt1', 'rt2'], w=[('actT', 8 + j)])
                for half in range(2):
                    for ts in range(4):
                        for j in range(8):
                            S.add('pe', lambda e, ts=ts, j=j, half=half: e.matmul(
                                banks[ts], lhsT=mT[:, j, ts * 128:(ts + 1) * 128], rhs=wo_s[:, j, half * 512:(half + 1) * 512],
                                start=(j == 0), stop=(j == 7)),
                                  r=[('actT', 8 + j), 'wo'], w=[('bank', ts)])
                    for ts in range(4):
                        tt = tiles[ts]
                        S.add('dve', lambda e, ts=ts, tt=tt, half=half: e.tensor_tensor(
                            out=h_sb[:, tt, half * 512:(half + 1) * 512], in0=banks[ts],
                            in1=h_sb[:, tt, half * 512:(half + 1) * 512], op=ALU.add),
                              r=[('bank', ts), ('h', tt)], w=[('h', tt)])
                rmsnorm_group(tiles, 16)
                ffn_group(tiles, 1)
                for i, tt in enumerate(tiles):
                    col = norm_ctr[0]
                    norm_ctr[0] += 1
                    S.add('act', lambda e, tt=tt, col=col: e.activation(out=junk[:, :], in_=h_sb[:, tt, :], func=AF.Square,
                                                                         accum_out=ss[:, col:col + 1]),
                          r=[('h', tt), 'ss'], w=[('ss', col), 'junk'])
                    S.add('act', lambda e, col=col: e.activation(out=rstd[:, col:col + 1], in_=ss[:, col:col + 1], func=AF.Sqrt,
                                                                 scale=1.0 / D, bias=epsc[:, 0:1]),
                          r=[('ss', col), 'epsc'], w=[('rstd0', col)])
                    S.add('dve', lambda e, col=col: e.reciprocal(out=rstd[:, col:col + 1], in_=rstd[:, col:col + 1]),
                          r=[('rstd0', col)], w=[('rstd', col)])
                    S.add('dve', lambda e, tt=tt, col=col: e.scalar_tensor_tensor(
                        out=h_sb[:, tt, :], in0=h_sb[:, tt, :], scalar=rstd[:, col:col + 1], in1=gfin, op0=ALU.mult, op1=ALU.mult),
                          r=[('h', tt), ('rstd', col), 'gfin'], w=[('h', tt)])
                    dma('sp', y_d[tt * 128:(tt + 1) * 128, :], h_sb[:, tt, :], r=[('h', tt)], w=['yout'], stream='yout')
        else:
            for tt in range(NTL):
                dma('sp', y_d[tt * 128:(tt + 1) * 128, :], h_sb[:, tt, :], r=[('h', tt)], w=['yout'], stream='yout')
        S.barrier()
        S.add('sp', lambda e: e.nop(), r=['yout'])

        keys = S.finalize()
        semh = {}
        for k in keys:
            nm = "s_" + "_".join(str(x) for x in (k[1] if isinstance(k[1], tuple) else (k[1],)))
            semh[k] = es.enter_context(nc.semaphore(nm))
        block = es.enter_context(nc.Block())

        def section(engname):
            def body(e):
                if engname in ('sp', 'pool'):
                    pid[engname] = e.partition_id()
                S.emit_engine(engname, e, semh)
            return body

        block.sync(section('sp'))
        block.tensor(section('pe'))
        block.scalar(section('act'))
        block.vector(section('dve'))
        block.gpsimd(section('pool'))
    return nc


def _prep_inputs(x, positions, ffn1_norm, ffn1_w_gate, ffn1_w_up, ffn1_w_down, mix_norm, w_in, b_gate,
                 w_branch_moba, w_branch_sb, w_out, ffn2_norm, ffn2_w_gate, ffn2_w_up, ffn2_w_down, final_norm):
    f = lambda a: np.ascontiguousarray(np.asarray(a, dtype=np.float32))
    shared = {}

    def gu(w):
        w = f(w).reshape(8, 128, NF, 128)
        return np.ascontiguousarray(w.transpose(2, 1, 0, 3))

    shared['wg1'] = gu(ffn1_w_gate[0]); shared['wu1'] = gu(ffn1_w_up[0]); shared['wd1'] = f(ffn1_w_down[0]).reshape(NF, 128, D)
    shared['wg2'] = gu(ffn2_w_gate[0]); shared['wu2'] = gu(ffn2_w_up[0]); shared['wd2'] = f(ffn2_w_down[0]).reshape(NF, 128, D)
    wi = f(w_in[0])
    qa, ka, va, qb, kb, vb = [wi[:, i * 512:(i + 1) * 512] for i in range(6)]
    cols = []
    for h in range(8):
        for part in (qa, ka, va, qb, kb, vb):
            cols.append(part[:, h * 64:(h + 1) * 64])
    cols.append(wi[:, 3072:5120])
    wr = np.concatenate(cols, axis=1)
    wr = wr.reshape(8, 128, 40, 128)
    shared['win'] = np.ascontiguousarray(wr.transpose(2, 1, 0, 3))
    shared['wa'] = f(w_branch_moba[0]).reshape(4, 128, D)
    shared['wb'] = f(w_branch_sb[0]).reshape(4, 128, D)
    shared['wo'] = f(w_out[0]).reshape(8, 128, D)
    gn = np.concatenate([f(ffn1_norm[0]).reshape(8, 128).T, f(mix_norm[0]).reshape(8, 128).T,
                         f(ffn2_norm[0]).reshape(8, 128).T], axis=1)
    shared['gains'] = np.ascontiguousarray(gn)
    shared['bgate'] = np.ascontiguousarray(f(b_gate[0]).reshape(16, 128).T)
    shared['gfin'] = f(final_norm).reshape(1, D)
    shared.update(host_consts())
    xs = f(x)[0]
    ps_ = np.asarray(positions).astype(np.int32)[0]
    in_maps = []
    for c in range(NCORE):
        m = dict(shared)
        m['x'] = np.ascontiguousarray(xs[c * TL:(c + 1) * TL])
        m['pos'] = np.ascontiguousarray(ps_[c * TL:(c + 1) * TL]).reshape(1, TL)
        in_maps.append(m)
    return in_maps


_NC_CACHE = {}


def kernel(**inputs):
    debug = bool(int(os.environ.get("KDEBUG", "0")))
    phases = int(os.environ.get("KPHASES", "3"))
    key = (debug, phases)
    if key not in _NC_CACHE:
        _NC_CACHE[key] = build_program(debug=debug, phases=phases)
    nc = _NC_CACHE[key]
    in_maps = _prep_inputs(**inputs)
    res = run_bass_kernel_spmd(nc, in_maps, core_ids=list(range(NCORE)))
    out = np.concatenate([np.asarray(r["y"], dtype=np.float32) for r in res.results], axis=0)[None]
    if debug:
        kernel.last_results = res.results
    return out
```

```python
import os
import contextlib
import numpy as np
import ml_dtypes
import concourse.bass as bass
import concourse.mybir as mybir
from concourse.bass_utils import run_bass_kernel_spmd

F32 = mybir.dt.float32
BF16 = mybir.dt.bfloat16
I32 = mybir.dt.int32
AF = mybir.ActivationFunctionType
ALU = mybir.AluOpType
AX = mybir.AxisListType

NCORE = 8
T = 16384
D = 1024
TL = T // NCORE
NTL = TL // 128
NG = TL // 512
FF = 2816
NF = FF // 128
EPS = 1e-6
NEGM = -30000.0
PI = float(np.pi)


class Sched:
    def __init__(self):
        self.ops = []
        self.lastw = {}
        self.readers = {}
        self.stream_cnt = {}
        self.barriers = []
        self.last_on_eng = {}

    def add(self, eng, emit, r=(), w=(), stream=None, inc=16):
        idx = len(self.ops)
        deps = {}
        for t in r:
            x = self.lastw.get(t)
            if x is not None:
                deps[x] = 'raw'
        for t in w:
            x = self.lastw.get(t)
            if x is not None and x not in deps:
                deps[x] = 'waw'
            for y in self.readers.get(t, ()):
                if y not in deps:
                    deps[y] = 'war'
        op = dict(eng=eng, emit=emit, deps=deps, stream=stream, inc=inc, idx=idx,
                  nbar=len(self.barriers), sig=None, signaler=False)
        if stream is not None:
            c = self.stream_cnt.get(stream, 0) + (inc if inc is not None else 1)
            self.stream_cnt[stream] = c
            op['sig'] = (('s', stream), c)
        else:
            self.last_on_eng[eng] = idx
        self.ops.append(op)
        for t in r:
            self.readers.setdefault(t, []).append(idx)
        for t in w:
            self.lastw[t] = idx
            self.readers[t] = []
        return idx

    def barrier(self):
        self.barriers.append(dict(compute_last=dict(self.last_on_eng), streams=dict(self.stream_cnt)))

    @staticmethod
    def _needs_wait(o, w, kind):
        if w['stream'] is not None:
            return True
        if o['stream'] is None and w['eng'] == o['eng']:
            if o['eng'] == 'pe':
                return False
            return kind == 'raw'
        return True

    def finalize(self):
        ops = self.ops
        for o in ops:
            for wi, kind in o['deps'].items():
                w = ops[wi]
                if self._needs_wait(o, w, kind) and w['stream'] is None:
                    w['signaler'] = True
        for b in self.barriers:
            for e, wi in b['compute_last'].items():
                ops[wi]['signaler'] = True
        cnt = {}
        for o in ops:
            if o['stream'] is None and o['signaler']:
                c = cnt.get(o['eng'], 0) + 1
                cnt[o['eng']] = c
                o['sig'] = (('e', o['eng']), c)
        sems = set()
        for o in ops:
            if o['sig'] is not None:
                sems.add(o['sig'][0])
        return sorted(sems, key=str)

    def emit_engine(self, eng, e, semh):
        ops = self.ops
        waited = {}

        def wait(key, c):
            if waited.get(key, 0) < c:
                e.wait_ge(semh[key], c)
                waited[key] = c

        nbar_done = 0
        for o in ops:
            if o['eng'] != eng:
                continue
            while nbar_done < o['nbar']:
                b = self.barriers[nbar_done]
                for en, wi in b['compute_last'].items():
                    if en == eng:
                        continue
                    k, c = ops[wi]['sig']
                    wait(k, c)
                for st, c in b['streams'].items():
                    wait(('s', st), c)
                nbar_done += 1
            for wi, kind in o['deps'].items():
                w = ops[wi]
                if self._needs_wait(o, w, kind):
                    k, c = w['sig']
                    wait(k, c)
            ins = o['emit'](e)
            if o['sig'] is not None:
                k, c = o['sig']
                if o['inc'] is None:
                    ins.then_inc(semh[k])
                elif o['stream'] is not None:
                    ins.then_inc(semh[k], o['inc'])
                else:
                    ins.then_inc(semh[k], 1)


def _bf(a):
    return np.ascontiguousarray(a.astype(ml_dtypes.bfloat16))


def host_consts():
    c = {}
    c['ident'] = _bf(np.eye(128, dtype=np.float32))
    j = np.arange(128)[:, None]
    s = np.arange(128)[None, :]
    c['negtri'] = _bf(np.where(j >= s, -1.0, 0.0))
    c['negones'] = _bf(-np.ones((128, 128), np.float32))
    Rm = np.zeros((128, 128), np.float32)
    for b in (0, 64):
        for i in range(8):
            Rm[b + i + 8, b + i] = -1.0
            Rm[b + i, b + i + 8] = 1.0
    c['rot32'] = Rm
    half = 8
    inv_freq = (np.float32(500000.0) ** (-np.arange(half, dtype=np.float32) * np.float32(2.0) / np.float32(16))).astype(np.float32)
    invf = np.zeros((128, 1), np.float32)
    for r in range(128):
        if (r % 64) < 16:
            invf[r, 0] = inv_freq[(r % 64) % 8]
    c['invf'] = invf
    scl = np.ones((128, 2), np.float32)
    scl[64:128, 0] = 0.125
    c['scl'] = scl
    p = np.arange(128)[:, None, None]
    i = np.arange(4)[None, :, None]
    t = np.arange(512)[None, None, :]
    valid = (128 * i + p) < t
    c['sbm01'] = _bf(np.where(valid, 1.0, 0.0))
    c['sbneg'] = _bf(np.where(valid, 0.0, NEGM))
    kp = 128 * i + p
    kb = kp // 256
    qb = t // 256
    mm = np.where(kb == qb, np.where(kp <= t, 0.0, NEGM), np.where(kb < qb, 0.0, NEGM))
    c['mobam'] = _bf(mm)
    n = np.arange(64)[:, None]
    ss = np.arange(T)[None, :]
    c['blockind'] = _bf(np.where((ss // 256) == n, 1.0, 0.0))
    c['ones_f32'] = np.ones((128, 64), np.float32)
    return c


class _PidProxy:
    def __init__(self, e):
        self.e = e

    def __mul__(self, k):
        return self.e.partition_id() * k


def build_program(debug=False, phases=3):
    nc = bass.Bass("TRN2", target_bir_lowering=False)
    S = Sched()

    def din(name, shape, dt=F32):
        return nc.dram_tensor(name, list(shape), dt, kind="ExternalInput").ap()

    x_d = din("x", [TL, D])
    pos_d = din("pos", [1, TL], I32)
    wg_d = [din(f"wg{i}", [NF, 128, 8, 128]) for i in (1, 2)]
    wu_d = [din(f"wu{i}", [NF, 128, 8, 128]) for i in (1, 2)]
    wd_d = [din(f"wd{i}", [NF, 128, D]) for i in (1, 2)]
    win_d = din("win", [40, 128, 8, 128])
    wa_d = din("wa", [4, 128, D])
    wb_d = din("wb", [4, 128, D])
    wo_d = din("wo", [8, 128, D])
    gains_d = din("gains", [128, 24])
    bgate_d = din("bgate", [128, 16])
    gfin_d = din("gfin", [1, D])
    ident_d = din("ident", [128, 128], BF16)
    negtri_d = din("negtri", [128, 128], BF16)
    negones_d = din("negones", [128, 128], BF16)
    rot32_d = din("rot32", [128, 128])
    invf_d = din("invf", [128, 1])
    scl_d = din("scl", [128, 2])
    sbm01_d = din("sbm01", [128, 4, 512], BF16)
    sbneg_d = din("sbneg", [128, 4, 512], BF16)
    mobam_d = din("mobam", [128, 4, 512], BF16)
    blockind_d = din("blockind", [64, T], BF16)
    onesf_d = din("ones_f32", [128, 64])
    y_d = nc.dram_tensor("y", [TL, D], F32, kind="ExternalOutput").ap()
    if debug:
        dbg_h1 = nc.dram_tensor("dbg_h1", [TL, D], F32, kind="ExternalOutput").ap()
        dbg_B = nc.dram_tensor("dbg_B", [128, T], BF16, kind="ExternalOutput").ap()
        dbg_q = nc.dram_tensor("dbg_q", [6 * 64, T], BF16, kind="ExternalOutput").ap()

    A_d = [nc.dram_tensor(f"xa{i}", [1024, TL], BF16) for i in range(3)]
    G_d = [nc.dram_tensor(f"xg{i}", [NCORE * 1024, TL], BF16) for i in range(3)]
    B_d = [nc.dram_tensor(f"xb{m}", [64, T], BF16) for m in range(2)]
    GB_d = [nc.dram_tensor(f"xgb{m}", [NCORE * 64, T], BF16) for m in range(2)]
    Lq_d = [nc.dram_tensor(f"xl{i}", [NCORE * 128, TL], BF16) for i in range(3)]
    LB_d = [nc.dram_tensor(f"xlb{m}", [NCORE * 64, TL], BF16) for m in range(2)]

    es = contextlib.ExitStack()
    with es:
        def sb(name, shape, dt):
            return es.enter_context(nc.sbuf_tensor(name, list(shape), dt))

        def ps(name, shape, dt):
            return es.enter_context(nc.psum_tensor(name, list(shape), dt))

        h_sb = sb("h_sb", [128, NTL, D], F32)
        big2 = sb("big2", [128, T], BF16)
        big3 = sb("big3", [128, T], BF16)
        w32 = sb("w32", [128, 4096], F32)
        wgs = sb("wgs", [128, 4, 8, 128], BF16)
        wus = sb("wus", [128, 4, 8, 128], BF16)
        wds = sb("wds", [128, 4, D], BF16)
        xn_s = sb("xn_s", [128, 4, D], BF16)
        stg = sb("stg", [128, 2, 512], BF16)
        junk = sb("junk", [128, D], BF16)
        ident = sb("identb", [128, 128], BF16)
        negtri = sb("negtrib", [128, 128], BF16)
        negones = sb("negonesb", [128, 128], BF16)
        rot32 = sb("rot32b", [128, 128], F32)
        invf = sb("invfb", [128, 1], F32)
        scl = sb("sclb", [128, 2], F32)
        gains = sb("gainsb", [128, 24], F32)
        bgate = sb("bgateb", [128, 16], F32)
        sbm01 = sb("sbm01b", [128, 4, 512], BF16)
        sbneg = sb("sbnegb", [128, 4, 512], BF16)
        mobam = sb("mobamb", [128, 4, 512], BF16)
        onesf = sb("onesfb", [128, 64], F32)
        ss = sb("ssb", [128, 96], F32)
        rstd = sb("rstdb", [128, 96], F32)
        small = sb("smallb", [128, 512], F32)
        posi = sb("posib", [128, 512], I32)
        epsc = sb("epscb", [128, 1], F32)

        psA = ps("psA", [128, 1024], F32)
        psB = ps("psB", [128, 1024], F32)
        psC = ps("psC", [128, 1024], F32)
        psD = ps("psD", [128, 1024], F32)
        banks = [psA[:, 0:512], psA[:, 512:1024], psB[:, 0:512], psB[:, 512:1024],
                 psC[:, 0:512], psC[:, 512:1024], psD[:, 0:512], psD[:, 512:1024]]

        xnT = big2[:, 0:4096].rearrange("p (k n) -> p k n", n=512)
        actT = big2[:, 4096:4096 + NF * 512].rearrange("p (k n) -> p k n", n=512)
        sgt = w32[:, 0:512]
        xf32 = w32[:, 512:1024]
        rt1 = w32[:, 1024:1536]
        rt2 = w32[:, 1536:2048]
        cfull = w32[:, 2048:2560]
        sfull = w32[:, 2560:3072]
        posf = w32[:, 3072:3584]
        angt = w32[:, 3584:4096]

        def dma(eng, out, in_, r=(), w=(), stream=None):
            return S.add(eng, lambda e, out=out, in_=in_: e.dma_start(out=out, in_=in_), r=r, w=w, stream=stream)

        for (dst, src, nm) in [(ident, ident_d, 'ident'), (negtri, negtri_d, 'negtri'), (negones, negones_d, 'negones'),
                               (rot32, rot32_d, 'rot32'), (invf, invf_d, 'invf'), (scl, scl_d, 'scl'),
                               (gains, gains_d, 'gains'), (bgate, bgate_d, 'bgate'), (onesf, onesf_d, 'onesf')]:
            dma('sp', dst[:, :], src[:, :], w=[nm], stream='const')
        dma('sp', sbm01[:, :, :], sbm01_d[:, :, :], w=['sbm01'], stream='const')
        dma('sp', sbneg[:, :, :], sbneg_d[:, :, :], w=['sbneg'], stream='const')
        dma('sp', mobam[:, :, :], mobam_d[:, :, :], w=['mobam'], stream='const')
        S.add('dve', lambda e: e.memset(ss[:, :], 0.0), w=['ss'])
        S.add('dve', lambda e: e.memset(epsc[:, :], EPS), w=['epsc'])
        S.barrier()

        norm_ctr = [0]

        def rmsnorm_group(tiles, gain_col0):
            cols = []
            for i, tt in enumerate(tiles):
                col = norm_ctr[0]
                norm_ctr[0] += 1
                cols.append(col)
                S.add('act', lambda e, tt=tt, col=col: e.activation(out=junk[:, :], in_=h_sb[:, tt, :], func=AF.Square,
                                                                     accum_out=ss[:, col:col + 1]),
                      r=[('h', tt), 'ss'], w=[('ss', col), 'junk'])
            c0 = cols[0]
            S.add('act', lambda e, c0=c0: e.activation(out=rstd[:, c0:c0 + 4], in_=ss[:, c0:c0 + 4], func=AF.Sqrt,
                                                       scale=1.0 / D, bias=epsc[:, 0:1]),
                  r=[('ss', c) for c in cols] + ['epsc'], w=[('rstd0', c0)])
            S.add('dve', lambda e, c0=c0: e.reciprocal(out=rstd[:, c0:c0 + 4], in_=rstd[:, c0:c0 + 4]),
                  r=[('rstd0', c0)], w=[('rstd', c) for c in cols])
            xn4 = xn_s
            for i, tt in enumerate(tiles):
                S.add('dve', lambda e, tt=tt, col=cols[i], i=i: e.tensor_scalar(
                    out=xn4[:, i, :], in0=h_sb[:, tt, :], scalar1=rstd[:, col:col + 1], scalar2=None, op0=ALU.mult),
                      r=[('h', tt), ('rstd', cols[i])], w=[('xn4', i)])
            for rnd in range(4):
                for kk in range(2):
                    k = rnd * 2 + kk
                    bk = 6 + kk
                    for i in range(4):
                        S.add('pe', lambda e, k=k, bk=bk, i=i: e.matmul(
                            banks[bk][:, i * 128:(i + 1) * 128], lhsT=xn4[:, i, k * 128:(k + 1) * 128], rhs=ident[:, :],
                            start=True, stop=True),
                              r=[('xn4', i), 'ident'], w=[('bank', bk)])
                    gc = gain_col0 + k
                    eng = 'dve' if kk == 0 else 'act'
                    if eng == 'dve':
                        S.add('dve', lambda e, k=k, bk=bk, gc=gc: e.tensor_scalar(
                            out=xnT[:, k, :], in0=banks[bk], scalar1=gains[:, gc:gc + 1], scalar2=None, op0=ALU.mult),
                              r=[('bank', bk), 'gains'], w=[('xnT', k)])
                    else:
                        S.add('act', lambda e, k=k, bk=bk, gc=gc: e.activation(
                            out=xnT[:, k, :], in_=banks[bk], func=AF.Copy, scale=gains[:, gc:gc + 1]),
                              r=[('bank', bk), 'gains'], w=[('xnT', k)])

        wcnt = {'g': 0, 'u': 0, 'd': 0, 'c': 0}

        xnTa = big2[:, :].rearrange("p (k n) -> p k n", n=TL)
        actTp = [w32[:, 1024 + i * 512:1024 + (i + 1) * 512].bitcast(BF16).rearrange("p (j n) -> p j n", n=512) for i in range(2)]
        sgt2 = [w32[:, 0:512], w32[:, 512:1024]]

        def rmsnorm_all(gain_col0):
            for g in range(NG):
                tiles = [g * 4 + i for i in range(4)]
                cols = []
                for i, tt in enumerate(tiles):
                    col = norm_ctr[0]
                    norm_ctr[0] += 1
                    cols.append(col)
                    S.add('act', lambda e, tt=tt, col=col: e.activation(out=junk[:, :], in_=h_sb[:, tt, :], func=AF.Square,
                                                                         accum_out=ss[:, col:col + 1]),
                          r=[('h', tt), 'ss'], w=[('ss', col), 'junk'])
                c0 = cols[0]
                S.add('act', lambda e, c0=c0: e.activation(out=rstd[:, c0:c0 + 4], in_=ss[:, c0:c0 + 4], func=AF.Sqrt,
                                                           scale=1.0 / D, bias=epsc[:, 0:1]),
                      r=[('ss', c) for c in cols] + ['epsc'], w=[('rstd0', c0)])
                S.add('dve', lambda e, c0=c0: e.reciprocal(out=rstd[:, c0:c0 + 4], in_=rstd[:, c0:c0 + 4]),
                      r=[('rstd0', c0)], w=[('rstd', c) for c in cols])
                for i, tt in enumerate(tiles):
                    S.add('dve', lambda e, tt=tt, col=cols[i], i=i: e.tensor_scalar(
                        out=xn_s[:, i, :], in0=h_sb[:, tt, :], scalar1=rstd[:, col:col + 1], scalar2=None, op0=ALU.mult),
                          r=[('h', tt), ('rstd', cols[i])], w=[('xn4', i)])
                for rnd in range(4):
                    for kk in range(2):
                        k = rnd * 2 + kk
                        bk = 6 + kk
                        for i in range(4):
                            S.add('pe', lambda e, k=k, bk=bk, i=i: e.matmul(
                                banks[bk][:, i * 128:(i + 1) * 128], lhsT=xn_s[:, i, k * 128:(k + 1) * 128], rhs=ident[:, :],
                                start=True, stop=True),
                                  r=[('xn4', i), 'ident'], w=[('bank', bk)])
                        gc = gain_col0 + k
                        if kk == 0:
                            S.add('dve', lambda e, k=k, bk=bk, gc=gc, g=g: e.tensor_scalar(
                                out=xnTa[:, k, g * 512:(g + 1) * 512], in0=banks[bk], scalar1=gains[:, gc:gc + 1], scalar2=None,
                                op0=ALU.mult), r=[('bank', bk), 'gains'], w=[('xnTa', k, g)])
                        else:
                            S.add('act', lambda e, k=k, bk=bk, gc=gc, g=g: e.activation(
                                out=xnTa[:, k, g * 512:(g + 1) * 512], in_=banks[bk], func=AF.Copy, scale=gains[:, gc:gc + 1]),
                                  r=[('bank', bk), 'gains'], w=[('xnTa', k, g)])

        pool6 = [0, 1, 2, 3, 6, 7]
        dctr = [0]

        def ffn_all(widx):
            for fp in range(NF // 2):
                slots = []
                for j in range(2):
                    f = 2 * fp + j
                    sg_ = wcnt['g'] % 4
                    wcnt['g'] += 1
                    slots.append(sg_)
                    S.add('pool', lambda e, f=f, sg_=sg_: e.dma_start(out=wgs[:, sg_, :, :], in_=wg_d[widx][f, :, :, :]),
                          w=[('wgs', sg_)], stream=('wg', sg_))
                    S.add('pool', lambda e, f=f, sg_=sg_: e.dma_start(out=wus[:, sg_, :, :], in_=wu_d[widx][f, :, :, :]),
                          w=[('wus', sg_)], stream=('wu', sg_))
                    S.add('pool', lambda e, f=f, sg_=sg_: e.dma_start(out=wds[:, sg_, :], in_=wd_d[widx][f, :, :]),
                          w=[('wds', sg_)], stream=('wd', sg_))
                for g in range(NG):
                    ap_ = (fp * NG + g) % 2
                    for j in range(2):
                        sg_ = slots[j]
                        for k in range(8):
                            S.add('pe', lambda e, k=k, sg_=sg_, g=g: e.matmul(
                                banks[4], lhsT=wgs[:, sg_, k, :], rhs=xnTa[:, k, g * 512:(g + 1) * 512],
                                start=(k == 0), stop=(k == 7)),
                                  r=[('wgs', sg_), ('xnTa', k, g)], w=[('bank', 4)])
                        for k in range(8):
                            S.add('pe', lambda e, k=k, sg_=sg_, g=g: e.matmul(
                                banks[5], lhsT=wus[:, sg_, k, :], rhs=xnTa[:, k, g * 512:(g + 1) * 512],
                                start=(k == 0), stop=(k == 7)),
                                  r=[('wus', sg_), ('xnTa', k, g)], w=[('bank', 5)])
                        S.add('act', lambda e, j=j: e.activation(out=sgt2[j], in_=banks[4], func=AF.Silu),
                              r=[('bank', 4)], w=[('sgt2', j)])
                        S.add('dve', lambda e, j=j, ap_=ap_: e.tensor_tensor(out=actTp[ap_][:, j, :], in0=sgt2[j], in1=banks[5],
                                                                             op=ALU.mult),
                              r=[('sgt2', j), ('bank', 5)], w=[('actTp', ap_)])
                    for half in range(2):
                        for ts in range(4):
                            bk = pool6[dctr[0] % 6]
                            dctr[0] += 1
                            tt = g * 4 + ts
                            for j in range(2):
                                S.add('pe', lambda e, j=j, ap_=ap_, ts=ts, half=half, bk=bk, sg_=slots[j]: e.matmul(
                                    banks[bk], lhsT=actTp[ap_][:, j, ts * 128:(ts + 1) * 128],
                                    rhs=wds[:, sg_, half * 512:(half + 1) * 512], start=(j == 0), stop=(j == 1)),
                                      r=[('actTp', ap_), ('wds', slots[j])], w=[('bank', bk)])
                            S.add('dve', lambda e, bk=bk, tt=tt, half=half: e.scalar_tensor_tensor(
                                out=h_sb[:, tt, half * 512:(half + 1) * 512], in0=banks[bk], scalar=0.5,
                                in1=h_sb[:, tt, half * 512:(half + 1) * 512], op0=ALU.mult, op1=ALU.add),
                                  r=[('bank', bk), ('h', tt)], w=[('h', tt)])

        for tt in range(NTL):
            dma('sp', h_sb[:, tt, :], x_d[tt * 128:(tt + 1) * 128, :], w=[('h', tt)], stream=('x', tt))

        rmsnorm_all(0)
        ffn_all(0)
        S.barrier()
        if debug:
            for tt in range(NTL):
                dma('sp', dbg_h1[tt * 128:(tt + 1) * 128, :], h_sb[:, tt, :], r=[('h', tt)], stream='dbg')
        rmsnorm_all(8)
        cs32 = big3[:, 0:8192].bitcast(F32).rearrange("p (a n) -> p a n", n=512)
        MAGIC = 12582912.0
        for g in range(NG):
            S.add('sp', lambda e, g=g: e.dma_start(out=posi[:, :], in_=pos_d[0:1, g * 512:(g + 1) * 512].broadcast_to([128, 512])),
                  w=['posi'], stream='pos')
            S.add('dve', lambda e: e.tensor_copy(out=posf, in_=posi[:, :]), r=['posi'], w=['posf'])
            S.add('dve', lambda e: e.tensor_scalar(out=angt, in0=posf, scalar1=invf[:, 0:1], scalar2=None, op0=ALU.mult),
                  r=['posf', 'invf'], w=['angt'])
            for which, shift in ((0, 0.0), (1, 0.5 * PI)):
                dst = cs32[:, 2 * g + which, :]
                if shift != 0.0:
                    S.add('dve', lambda e, shift=shift: e.tensor_scalar(out=rt2, in0=angt, scalar1=shift, scalar2=None, op0=ALU.add),
                          r=['angt'], w=['rt2'])
                    src_, srct = rt2, 'rt2'
                else:
                    src_, srct = angt, 'angt'
                S.add('dve', lambda e, src_=src_: e.tensor_scalar(out=rt1, in0=src_, scalar1=1.0 / (2 * PI), scalar2=MAGIC,
                                                                  op0=ALU.mult, op1=ALU.add), r=[srct], w=['rt1'])
                S.add('dve', lambda e: e.tensor_scalar(out=rt1, in0=rt1, scalar1=-MAGIC, scalar2=None, op0=ALU.add),
                      r=['rt1'], w=['rt1'])
                S.add('dve', lambda e, src_=src_: e.scalar_tensor_tensor(out=rt1, in0=rt1, scalar=-2 * PI, in1=src_,
                                                                         op0=ALU.mult, op1=ALU.add), r=['rt1', srct], w=['rt1'])
                S.add('dve', lambda e: e.tensor_scalar(out=rt1, in0=rt1, scalar1=-PI, scalar2=PI, op0=ALU.max, op1=ALU.min),
                      r=['rt1'], w=['rt1'])
                S.add('act', lambda e, dst=dst: e.activation(out=dst, in_=rt1, func=AF.Sin), r=['rt1'], w=[('cs', g, which)])
        pctr = 0
        for typ in (2, 1, 0):
            for hd in range(8):
                cc = hd * 3 + typ
                sc = wcnt['g'] % 4
                wcnt['g'] += 1
                S.add('pool', lambda e, cc=cc, sc=sc: e.dma_start(out=wgs[:, sc, :, :], in_=win_d[cc, :, :, :]),
                      w=[('wgs', sc)], stream=('wg', sc))
                for g in range(NG):
                    bk = 6 + (pctr % 2)
                    so = pctr % 2
                    pctr += 1
                    for k in range(8):
                        S.add('pe', lambda e, k=k, sc=sc, bk=bk, g=g: e.matmul(
                            banks[bk], lhsT=wgs[:, sc, k, :], rhs=xnTa[:, k, g * 512:(g + 1) * 512], start=(k == 0), stop=(k == 7)),
                              r=[('wgs', sc), ('xnTa', k, g)], w=[('bank', bk)])
                    if typ == 0:
                        S.add('act', lambda e, bk=bk: e.activation(out=xf32, in_=banks[bk], func=AF.Copy),
                              r=[('bank', bk)], w=['xf32'])
                        S.add('pe', lambda e: e.matmul(banks[5], lhsT=rot32[:, :], rhs=xf32, start=True, stop=True),
                              r=['xf32', 'rot32'], w=[('bank', 5)])
                        S.add('dve', lambda e, g=g: e.tensor_tensor(out=rt1, in0=xf32, in1=cs32[:, 2 * g + 1, :], op=ALU.mult),
                              r=['xf32', ('cs', g, 1)], w=['rt1'])
                        S.add('dve', lambda e, g=g: e.tensor_tensor(out=rt2, in0=banks[5], in1=cs32[:, 2 * g, :], op=ALU.mult),
                              r=[('bank', 5), ('cs', g, 0)], w=['rt2'])
                        S.add('dve', lambda e, so=so: e.tensor_tensor(out=stg[:, so, :], in0=rt1, in1=rt2, op=ALU.add),
                              r=['rt1', 'rt2'], w=[('stg', so)])
                    else:
                        S.add('dve', lambda e, bk=bk, so=so, typ=typ: e.tensor_scalar(
                            out=stg[:, so, :], in0=banks[bk], scalar1=scl[:, typ - 1:typ], scalar2=None, op0=ALU.mult),
                              r=[('bank', bk), 'scl'], w=[('stg', so)])
                    S.add('sp', lambda e, so=so, typ=typ, hd=hd, g=g: e.dma_start(
                        out=A_d[typ][hd * 128:(hd + 1) * 128, g * 512:(g + 1) * 512], in_=stg[:, so, :]),
                          r=[('stg', so)], w=[('A', typ)], stream=('aout', so))
            S.add('pool', lambda e, typ=typ: e.collective_compute(
                "AllGather", ALU.bypass, replica_groups=[list(range(NCORE))],
                ins=[A_d[typ].ap().opt()], outs=[G_d[typ].ap().opt()]),
                  r=[('A', typ)], w=[('G', typ)], stream=('cc', typ), inc=None)
        S.barrier()


        pid = {}

        for typ in (2, 1, 0):
            S.add('sp', lambda e, typ=typ: e.dma_start(
                out=Lq_d[typ].ap().rearrange("(s o r) t -> s o r t", s=NCORE, o=1),
                in_=G_d[typ].ap().rearrange("(s h r) t -> s h r t", s=NCORE, h=NCORE)[:, bass.ds(pid['sp'], 1), :, :]),
                  r=[('G', typ)], w=[('Lq', typ)], stream=('lq', typ))
        S.barrier()

        def gsl(typ, src, off64, eng, cols=None):
            base = src * 128 + off64
            if cols is None:
                return Lq_d[typ].ap()[base:base + 64, :]
            return Lq_d[typ].ap()[base:base + 64, cols[0]:cols[1]]

        if debug:
            for it in range(6):
                typ, off = it // 2, (it % 2) * 64
                for src in range(NCORE):
                    S.add('sp', lambda e, it=it, typ=typ, off=off, src=src: e.dma_start(
                        out=big2[0:64, src * TL:(src + 1) * TL],
                        in_=gsl(typ, src, off, 'sp')), r=[('Lq', typ)], w=['dbgbuf'], stream='dbg')
                S.add('sp', lambda e, it=it: e.dma_start(out=dbg_q[it * 64:(it + 1) * 64, :], in_=big2[0:64, :]),
                      r=['dbgbuf'], w=['dbgq'], stream='dbg2')
            S.barrier()

        if phases >= 2:
            Vsb = big3[:, 0:8320].rearrange("p (a n) -> p a n", n=65)
            o = 8320
            Abuf = [big3[:, o + i * 1024:o + (i + 1) * 1024] for i in range(2)]
            o += 2048
            Lbuf = [big3[:, o + i * 1024:o + (i + 1) * 1024] for i in range(2)]
            o += 2048
            LaccA = [big3[:, o + i * 512:o + (i + 1) * 512] for i in range(3)]
            o += 1536
            LaccB = [big3[:, o + i * 512:o + (i + 1) * 512] for i in range(2)]
            o += 1024
            Qp = [big3[:, o + i * 512:o + (i + 1) * 512] for i in range(2)]
            qsb = Qp
            o += 1024
            selpad = big3[:, o:o + 128]
            o += 128
            vstage = big3[:, 8320:8320 + 2048]
            assert o <= T, o
            ebuf = [w32[:, i * 1024:(i + 1) * 1024] for i in range(2)]
            Lacc32 = w32[:, 2048:2560]
            Lsum = w32[:, 2560:3072]
            ystg32 = w32[:, 3072:3584]
            ybc = w32[:, 3584:4096]
            gsb = small[:, 0:64]
            m8 = small[:, 64:72]
            kms = small[:, 128:192]
            kmh = small[:, 192:256]
            recrow = small[:, 256:512]

            kmhi = stg[:, 0, 0:64]
            kmlo = stg[:, 0, 64:128]
            ystgb = stg[:, 1, :]

            def load_kv(ktyp, koff, vtyp, voff, tag):
                for src in range(NCORE):
                    S.add('sp', lambda e, src=src: e.dma_start(out=big2[0:64, src * TL:(src + 1) * TL],
                                                                 in_=gsl(ktyp, src, koff, 'sp')),
                          r=[('Lq', ktyp)], w=['kT'], stream=('kv', src % 2))
                for src in range(NCORE):
                    S.add('sp', lambda e, src=src: e.dma_start(out=vstage[0:64, :],
                                                                 in_=gsl(vtyp, src, voff, 'sp')),
                          r=[('Lq', vtyp)], w=['vstage'], stream=('vst', 0))
                    for half in range(2):
                        bk = 6 + half
                        for i in range(8):
                            tl = half * 8 + i
                            S.add('pe', lambda e, bk=bk, i=i, tl=tl: e.matmul(
                                banks[bk][:, i * 64:(i + 1) * 64], lhsT=vstage[0:64, tl * 128:(tl + 1) * 128],
                                rhs=ident[0:64, 0:64], start=True, stop=True),
                                  r=['vstage', 'ident'], w=[('bank', bk)])
                        t0 = src * 16 + half * 8
                        S.add('dve', lambda e, bk=bk, t0=t0: e.tensor_copy(
                            out=Vsb[:, t0:t0 + 8, 0:64], in_=banks[bk].rearrange("p (a n) -> p a n", n=64)),
                              r=[('bank', bk)], w=['Vsb'])

            S.add('dve', lambda e: e.memset(Vsb[:, :, 64:65], 1.0), w=['Vsb'])
            load_kv(2, 0, 2, 64, 'sb')
            steps = []
            for qg in range(T // 512):
                n = 2 * qg + 2
                for s in range(n - 1, -1, -1):
                    steps.append((qg, s, s == n - 1, s == 0))
            NS = len(steps)
            zts = [psA, psB, psC]

            def qload(qg):
                src, gg = qg // 4, qg % 4
                sl = qg % 2
                S.add('sp', lambda e, src=src, gg=gg, sl=sl: e.dma_start(
                    out=qsb[sl][0:64, :], in_=gsl(1, src, 64, 'sp', (gg * 512, (gg + 1) * 512))),
                      r=[('Lq', 1)], w=[('qsb', sl)], stream=('q', sl))

            def z_mm(i):
                qg, s, first, last = steps[i]
                zs = i % 3
                for j, kt in enumerate((2 * s + 1, 2 * s)):
                    S.add('pe', lambda e, kt=kt, zs=zs, j=j, qg=qg: e.matmul(
                        banks[2 * zs + j], lhsT=big2[0:64, kt * 128:(kt + 1) * 128], rhs=qsb[qg % 2][0:64, :],
                        start=True, stop=True),
                          r=['kT', ('qsb', qg % 2)], w=[('Z', zs)])

            def act12(i):
                qg, s, first, last = steps[i]
                zs = i % 3
                es_ = i % 2
                S.add('act', lambda e, zs=zs, es_=es_: e.activation(out=ebuf[es_], in_=zts[zs][:, :], func=AF.Exp),
                      r=[('Z', zs)], w=[('e', es_)])
                S.add('act', lambda e, es_=es_: e.activation(out=Lbuf[es_], in_=ebuf[es_], func=AF.Ln, bias=1.0),
                      r=[('e', es_)], w=[('L', es_)])
                diag = s >= 2 * qg
                if diag:
                    dj = (s - 2 * qg) * 2
                    S.add('pool', lambda e, es_=es_, dj=dj: e.tensor_tensor(
                        out=Lbuf[es_][:, 0:512], in0=Lbuf[es_][:, 0:512], in1=sbm01[:, dj + 1, :], op=ALU.mult),
                          r=[('L', es_), 'sbm01'], w=[('L', es_)])
                    S.add('pool', lambda e, es_=es_, dj=dj: e.tensor_tensor(
                        out=Lbuf[es_][:, 512:1024], in0=Lbuf[es_][:, 512:1024], in1=sbm01[:, dj, :], op=ALU.mult),
                          r=[('L', es_), 'sbm01'], w=[('L', es_)])

            def lacc(i):
                qg, s, first, last = steps[i]
                es_ = i % 2
                Lh = Lbuf[es_][:, 0:512]
                Ll = Lbuf[es_][:, 512:1024]
                if not first:
                    S.add('dve', lambda e, i=i, Lh=Lh: e.tensor_tensor(out=LaccB[i % 2], in0=Lacc32, in1=Lh, op=ALU.add),
                          r=[('L', es_), 'Lacc32'], w=[('LaccB', i % 2)])
                if last:
                    return
                if first:
                    S.add('dve', lambda e, Lh=Lh, Ll=Ll: e.tensor_tensor(out=Lacc32, in0=Lh, in1=Ll, op=ALU.add),
                          r=[('L', es_)], w=['Lacc32'])
                else:
                    S.add('dve', lambda e, Lh=Lh, Ll=Ll: e.tensor_tensor(out=Lsum, in0=Lh, in1=Ll, op=ALU.add),
                          r=[('L', es_)], w=['Lsum'])
                    S.add('dve', lambda e: e.tensor_tensor(out=Lacc32, in0=Lacc32, in1=Lsum, op=ALU.add),
                          r=['Lsum', 'Lacc32'], w=['Lacc32'])
                S.add('dve', lambda e, i=i: e.tensor_copy(out=LaccA[(i + 1) % 3], in_=Lacc32),
                      r=['Lacc32'], w=[('LaccA', (i + 1) % 3)])

            def p_mm(i):
                qg, s, first, last = steps[i]
                zs = i % 3
                es_ = i % 2
                diag = s >= 2 * qg
                dj = (s - 2 * qg) * 2
                Lh = Lbuf[es_][:, 0:512]
                for j in range(2):
                    Lme = Lbuf[es_][:, j * 512:(j + 1) * 512]
                    seq = [(negtri[:, :], Lme)]
                    if j == 0:
                        if not first:
                            seq.append((negones[:, :], LaccA[i % 3]))
                    else:
                        if first:
                            seq.append((negones[:, :], Lh))
                        else:
                            seq.append((negones[:, :], LaccB[i % 2]))
                    if diag:
                        seq.append((ident[:, :], sbneg[:, dj + (1 - j), :]))
                    for m, (lt, rh) in enumerate(seq):
                        S.add('pe', lambda e, zs=zs, j=j, lt=lt, rh=rh, m=m, nm=len(seq): e.matmul(
                            banks[2 * zs + j], lhsT=lt, rhs=rh, start=False, stop=(m == nm - 1)),
                              r=[('L', es_), ('LaccA', i % 3), ('LaccB', i % 2), 'sbneg', 'negtri', 'negones', 'ident', ('e', es_)],
                              w=[('Z', zs)])

            def act3(i):
                zs = i % 3
                as_ = i % 2
                S.add('act', lambda e, zs=zs, as_=as_: e.activation(out=Abuf[as_], in_=zts[zs][:, :], func=AF.Exp),
                      r=[('Z', zs)], w=[('A', as_)])

            def pv_mm(i):
                qg, s, first, last = steps[i]
                as_ = i % 2
                for j, kt in enumerate((2 * s + 1, 2 * s)):
                    S.add('pe', lambda e, kt=kt, j=j, as_=as_, first=first, last=last: e.matmul(
                        banks[6][0:64, :], lhsT=Vsb[:, kt, 0:64], rhs=Abuf[as_][:, j * 512:(j + 1) * 512],
                        start=(first and j == 0), stop=(last and j == 1)),
                          r=[('A', as_), 'Vsb'], w=['Y'])
                if last:
                    S.add('dve', lambda e: e.tensor_copy(out=ystgb[0:64, :], in_=banks[6][0:64, :]), r=['Y'], w=['ystgb'])
                    S.add('sp', lambda e, qg=qg: e.dma_start(out=B_d[1].ap()[:, qg * 512:(qg + 1) * 512], in_=ystgb[0:64, :]),
                          r=['ystgb'], w=[('B', 1)], stream=('bout', 0))

            qload(0)
            z_mm(0)
            act12(0)
            lacc(0)
            if NS > 1:
                if steps[1][2]:
                    qload(steps[1][0])
                z_mm(1)
            for i in range(NS):
                if i + 1 < NS:
                    act12(i + 1)
                    lacc(i + 1)
                p_mm(i)
                if i + 2 < NS:
                    if steps[i + 2][2]:
                        qload(steps[i + 2][0])
                    z_mm(i + 2)
                act3(i)
                pv_mm(i)

            S.barrier()
            S.add('pool', lambda e: e.collective_compute("AllGather", ALU.bypass, replica_groups=[list(range(NCORE))],
                                                         ins=[B_d[1].ap().opt()], outs=[GB_d[1].ap().opt()]),
                  r=[('B', 1)], w=[('GB', 1)], stream=('cc', 3), inc=None)
            load_kv(0, 64, 1, 0, 'moba')
            dma('sp', big2[64:128, :], blockind_d[:, :], w=['kT'], stream=('kv', 0))
            S.add('dve', lambda e: e.tensor_reduce(out=kms[0:64, :], in_=big2[0:64, :].rearrange("p (n b) -> p n b", b=256),
                                                   axis=AX.X, op=ALU.add), r=['kT'], w=['kms'])
            S.add('dve', lambda e: e.tensor_scalar(out=kms[0:64, :], in0=kms[0:64, :], scalar1=1.0 / 256, scalar2=None, op0=ALU.mult),
                  r=['kms'], w=['kms'])
            S.add('dve', lambda e: e.tensor_copy(out=kmhi[0:64, :], in_=kms[0:64, :]), r=['kms'], w=['kmhi'])
            S.add('dve', lambda e: e.tensor_copy(out=kmh[0:64, :], in_=kmhi[0:64, :]), r=['kmhi'], w=['kmh'])
            S.add('dve', lambda e: e.tensor_tensor(out=kmh[0:64, :], in0=kms[0:64, :], in1=kmh[0:64, :], op=ALU.subtract),
                  r=['kms', 'kmh'], w=['kmh'])
            S.add('dve', lambda e: e.tensor_copy(out=kmlo[0:64, :], in_=kmh[0:64, :]), r=['kmh'], w=['kmlo'])
            S.add('dve', lambda e: e.memset(gsb, -1e30), w=['gsb'])
            S.add('dve', lambda e: e.memset(selpad[:, 0:64], 0.0), w=['selpad'])

            def moba_prep_stages(qg):
                sl = qg % 2
                src, gg = qg // 4, qg % 4
                st = []

                def s_load():
                    S.add('sp', lambda e: e.dma_start(
                        out=Qp[sl][0:64, :], in_=gsl(0, src, 0, 'sp', (gg * 512, (gg + 1) * 512))),
                          r=[('Lq', 0)], w=[('Qp', sl)], stream=('q', sl))
                st.append(s_load)
                for ci in range(4):
                    c = qg * 4 + ci
                    own = c // 2

                    def s_gate(ci=ci, own=own):
                        if own <= 3:
                            S.add('dve', lambda e: e.memset(selpad[:, 64:128], NEGM), w=['selpad'])
                            S.add('dve', lambda e: e.memset(selpad[:, 64:64 + own + 1], 0.0), w=['selpad'])
                        else:
                            for m, km in enumerate((kmhi, kmlo)):
                                S.add('pe', lambda e, km=km, m=m: e.matmul(
                                    banks[7][:, 0:64], lhsT=Qp[sl][0:64, ci * 128:(ci + 1) * 128], rhs=km[0:64, :],
                                    start=(m == 0), stop=(m == 1)),
                                      r=[('Qp', sl), 'kmhi', 'kmlo'], w=[('bank', 7)])
                            S.add('dve', lambda e: e.tensor_copy(out=gsb[:, 0:own], in_=banks[7][:, 0:own]),
                                  r=[('bank', 7)], w=['gsb'])
                            S.add('dve', lambda e: e.max(out=m8, in_=gsb), r=['gsb'], w=['m8'])
                            S.add('dve', lambda e: e.tensor_scalar(out=selpad[:, 64:128], in0=gsb, scalar1=m8[:, 2:3],
                                                                   scalar2=NEGM, op0=ALU.is_lt, op1=ALU.mult),
                                  r=['gsb', 'm8'], w=['selpad'])
                            S.add('dve', lambda e: e.memset(selpad[:, 64 + own:64 + own + 1], 0.0), w=['selpad'])

                    def s_tr(ci=ci):
                        S.add('pe', lambda e: e.matmul(banks[7][:, 0:128], lhsT=selpad, rhs=ident[:, :], start=True, stop=True),
                              r=['selpad', 'ident'], w=[('bank', 7)])
                        S.add('dve', lambda e: e.tensor_copy(out=Qp[sl][64:128, ci * 128:(ci + 1) * 128],
                                                             in_=banks[7][64:128, 0:128]),
                              r=[('bank', 7)], w=[('Qp', sl)])
                    st.append(s_gate)
                    st.append(s_tr)
                return st

            msteps = []
            for qg in range(T // 512):
                nb = 2 * qg + 2
                for n in range(nb):
                    msteps.append((qg, n, n == 0, n == nb - 1))
            NM = len(msteps)

            def s_mm(i):
                qg, n, first, last = msteps[i]
                zs = i % 2
                zb = [0, 2][zs]
                diag = n >= 2 * qg
                for j in range(2):
                    kt = 2 * n + j
                    S.add('pe', lambda e, kt=kt, zb=zb, j=j, qg=qg, diag=diag: e.matmul(
                        banks[zb + j], lhsT=big2[:, kt * 128:(kt + 1) * 128], rhs=Qp[qg % 2], start=True, stop=(not diag)),
                          r=['kT', ('Qp', qg % 2)], w=[('Z', zs)])
                    if diag:
                        dj = (n - 2 * qg) * 2 + j
                        S.add('pe', lambda e, zb=zb, j=j, dj=dj: e.matmul(
                            banks[zb + j], lhsT=ident[:, :], rhs=mobam[:, dj, :], start=False, stop=True),
                              r=['mobam', 'ident'], w=[('Z', zs)])

            def m_act(i):
                zs = i % 2
                zt = [psA, psB][zs]
                S.add('act', lambda e, zt=zt, zs=zs: e.activation(out=Abuf[zs], in_=zt[:, :], func=AF.Exp, scale=0.125),
                      r=[('Z', zs)], w=[('A', zs)])

            def m_pv(i):
                qg, n, first, last = msteps[i]
                zs = i % 2
                for j in range(2):
                    kt = 2 * n + j
                    S.add('pe', lambda e, kt=kt, j=j, zs=zs, first=first, last=last: e.matmul(
                        banks[6][0:65, :], lhsT=Vsb[:, kt, 0:65], rhs=Abuf[zs][:, j * 512:(j + 1) * 512],
                        start=(first and j == 0), stop=(last and j == 1)),
                          r=[('A', zs), 'Vsb'], w=['Y'])
                if last:
                    S.add('dve', lambda e: e.reciprocal(out=ystg32[64:65, :], in_=banks[6][64:65, :]), r=['Y'], w=['rec'])
                    S.add('pe', lambda e: e.matmul(banks[7][0:64, :], lhsT=onesf[64:65, 0:64], rhs=ystg32[64:65, :],
                                                    start=True, stop=True), r=['rec', 'onesf'], w=[('bank', 7)])
                    S.add('act', lambda e: e.activation(out=ybc[0:64, :], in_=banks[7][0:64, :], func=AF.Copy),
                          r=[('bank', 7)], w=['ybc'])
                    S.add('dve', lambda e: e.tensor_tensor(out=ystgb[0:64, :], in0=banks[6][0:64, :], in1=ybc[0:64, :], op=ALU.mult),
                          r=['Y', 'ybc'], w=['ystgb'])
                    S.add('sp', lambda e, qg=qg: e.dma_start(out=B_d[0].ap()[:, qg * 512:(qg + 1) * 512], in_=ystgb[0:64, :]),
                          r=['ystgb'], w=[('B', 0)], stream=('bout', 0))

            for f_ in moba_prep_stages(0):
                f_()
            pending = moba_prep_stages(1)
            s_mm(0)
            for i in range(NM):
                qg_i = msteps[i][0]
                if i + 1 < NM:
                    if msteps[i + 1][2]:
                        for f_ in pending:
                            f_()
                        nq = msteps[i + 1][0] + 1
                        pending = moba_prep_stages(nq) if nq < T // 512 else []
                    s_mm(i + 1)
                m_act(i)
                m_pv(i)
                if pending:
                    pending.pop(0)()

        S.barrier()
        if debug and phases >= 2:
            dma('sp', dbg_B[0:64, :], B_d[0].ap()[:, :], r=[('B', 0)], stream='dbg')
            dma('sp', dbg_B[64:128, :], B_d[1].ap()[:, :], r=[('B', 1)], stream='dbg')

        if phases >= 3:
            S.add('pool', lambda e: e.collective_compute("AllGather", ALU.bypass, replica_groups=[list(range(NCORE))],
                                                         ins=[B_d[0].ap().opt()], outs=[GB_d[0].ap().opt()]),
                  r=[('B', 0)], w=[('GB', 0)], stream=('cc', 4), inc=None)
            S.barrier()
            for m in (1, 0):
                S.add('sp', lambda e, m=m: e.dma_start(out=LB_d[m].ap()[:, :], in_=GB_d[m].ap()[:, bass.ds(pid['sp'] * TL, TL)]),
                      r=[('GB', m)], w=['LB'], stream=('lq', 3))
            S.barrier()
            wa_s = big3[:, 0:4096].rearrange("p (k n) -> p k n", n=D)
            wb_s = big3[:, 4096:8192].rearrange("p (k n) -> p k n", n=D)
            wo_s = big3[:, 8192:16384].rearrange("p (k n) -> p k n", n=D)
            for kk in range(4):
                S.add('pool', lambda e, kk=kk: e.dma_start(out=wa_s[:, kk, :], in_=wa_d[kk, :, :]), w=['wa'], stream='w3')
                S.add('pool', lambda e, kk=kk: e.dma_start(out=wb_s[:, kk, :], in_=wb_d[kk, :, :]), w=['wb'], stream='w3')
            for kk in range(8):
                S.add('pool', lambda e, kk=kk: e.dma_start(out=wo_s[:, kk, :], in_=wo_d[kk, :, :]), w=['wo'], stream='w3')
            gfin = w32[:, 3072:4096]
            S.add('sp', lambda e: e.dma_start(out=gfin, in_=gfin_d[0:1, :].broadcast_to([128, D])), w=['gfin'], stream='gfin')
            S.barrier()
            for g in range(NG):
                tiles = [g * 4 + i for i in range(4)]
                rmsnorm_group(tiles, 8)
                for m in range(2):
                    for kk in range(4):
                        for hh in range(2):
                            hd = kk * 2 + hh
                            S.add('sp', lambda e, m=m, kk=kk, hh=hh, hd=hd, g=g: e.dma_start(
                                out=actT[hh * 64:(hh + 1) * 64, m * 4 + kk, :],
                                in_=LB_d[m].ap()[hd * 64:(hd + 1) * 64, g * 512:(g + 1) * 512]),
                                  r=['LB'], w=[('actT', m * 4 + kk)], stream=('yl', m))
                mT = actT[:, 8:16, :]
                for j in range(8):
                    for m in range(2):
                        cc = 24 + m * 8 + j
                        sc = wcnt['c'] % 3
                        wcnt['c'] += 1
                        S.add('pool', lambda e, cc=cc, sc=sc: e.dma_start(out=wgs[:, sc, :, :], in_=win_d[cc, :, :, :]),
                              w=[('wgs', sc)], stream=('wg', sc))
                        for k in range(8):
                            S.add('pe', lambda e, k=k, sc=sc, m=m: e.matmul(banks[4 + m], lhsT=wgs[:, sc, k, :], rhs=xnT[:, k, :],
                                                                           start=(k == 0), stop=(k == 7)),
                                  r=[('wgs', sc), ('xnT', k)], w=[('bank', 4 + m)])
                        gt = [sgt, xf32][m]
                        S.add('act', lambda e, m=m, j=j, gt=gt: e.activation(out=gt, in_=banks[4 + m], func=AF.Sigmoid,
                                                                            bias=bgate[:, m * 8 + j:m * 8 + j + 1]),
                              r=[('bank', 4 + m), 'bgate'], w=[('gt', m)])
                        ws = [wa_s, wb_s][m]
                        for kk in range(4):
                            S.add('pe', lambda e, kk=kk, m=m, j=j, ws=ws: e.matmul(
                                banks[6 + m], lhsT=ws[:, kk, j * 128:(j + 1) * 128], rhs=actT[:, m * 4 + kk, :],
                                start=(kk == 0), stop=(kk == 3)),
                                  r=['wa', 'wb', ('actT', m * 4 + kk)], w=[('bank', 6 + m)])
                    S.add('dve', lambda e: e.tensor_tensor(out=rt1, in0=sgt, in1=banks[6], op=ALU.mult),
                          r=[('gt', 0), ('bank', 6)], w=['rt1'])
                    S.add('dve', lambda e: e.tensor_tensor(out=rt2, in0=xf32, in1=banks[7], op=ALU.mult),
                          r=[('gt', 1), ('bank', 7)], w=['rt2'])
                    S.add('dve', lambda e, j=j: e.tensor_tensor(out=mT[:, j, :], in0=rt1, in1=rt2, op=ALU.add),
                          r=['rt1', 'rt2'], w=[('actT', 8 + j)])
                for half in range(2):
                    for ts in range(4):
                        for j in range(8):
                            S.add('pe', lambda e, ts=ts, j=j, half=half: e.matmul(
                                banks[ts], lhsT=mT[:, j, ts * 128:(ts + 1) * 128], rhs=wo_s[:, j, half * 512:(half + 1) * 512],
                                start=(j == 0), stop=(j == 7)),
                                  r=[('actT', 8 + j), 'wo'], w=[('bank', ts)])
                    for ts in range(4):
                        tt = tiles[ts]
                        S.add('dve', lambda e, ts=ts, tt=tt, half=half: e.tensor_tensor(
                            out=h_sb[:, tt, half * 512:(half + 1) * 512], in0=banks[ts],
                            in1=h_sb[:, tt, half * 512:(half + 1) * 512], op=ALU.add),
                              r=[('bank', ts), ('h', tt)], w=[('h', tt)])
            S.barrier()
            rmsnorm_all(16)
            ffn_all(1)
            for g in range(NG):
                tiles = [g * 4 + i for i in range(4)]
                for i, tt in enumerate(tiles):
                    col = norm_ctr[0]
                    norm_ctr[0] += 1
                    S.add('act', lambda e, tt=tt, col=col: e.activation(out=junk[:, :], in_=h_sb[:, tt, :], func=AF.Square,
                                                                         accum_out=ss[:, col:col + 1]),
                          r=[('h', tt), 'ss'], w=[('ss', col), 'junk'])
                    S.add('act', lambda e, col=col: e.activation(out=rstd[:, col:col + 1], in_=ss[:, col:col + 1], func=AF.Sqrt,
                                                                 scale=1.0 / D, bias=epsc[:, 0:1]),
                          r=[('ss', col), 'epsc'], w=[('rstd0', col)])
                    S.add('dve', lambda e, col=col: e.reciprocal(out=rstd[:, col:col + 1], in_=rstd[:, col:col + 1]),
                          r=[('rstd0', col)], w=[('rstd', col)])
                    S.add('dve', lambda e, tt=tt, col=col: e.scalar_tensor_tensor(
                        out=h_sb[:, tt, :], in0=h_sb[:, tt, :], scalar=rstd[:, col:col + 1], in1=gfin, op0=ALU.mult, op1=ALU.mult),
                          r=[('h', tt), ('rstd', col), 'gfin'], w=[('h', tt)])
                    dma('sp', y_d[tt * 128:(tt + 1) * 128, :], h_sb[:, tt, :], r=[('h', tt)], w=['yout'], stream='yout')
        else:
            for tt in range(NTL):
                dma('sp', y_d[tt * 128:(tt + 1) * 128, :], h_sb[:, tt, :], r=[('h', tt)], w=['yout'], stream='yout')
        S.barrier()
        S.add('sp', lambda e: e.nop(), r=['yout'])

        keys = S.finalize()
        semh = {}
        for k in keys:
            nm = "s_" + "_".join(str(x) for x in (k[1] if isinstance(k[1], tuple) else (k[1],)))
            semh[k] = es.enter_context(nc.semaphore(nm))
        block = es.enter_context(nc.Block())

        def section(engname):
            def body(e):
                if engname in ('sp', 'pool'):
                    pid[engname] = e.partition_id()
                S.emit_engine(engname, e, semh)
            return body

        block.sync(section('sp'))
        block.tensor(section('pe'))
        block.scalar(section('act'))
        block.vector(section('dve'))
        block.gpsimd(section('pool'))
    return nc


def _prep_inputs(x, positions, ffn1_norm, ffn1_w_gate, ffn1_w_up, ffn1_w_down, mix_norm, w_in, b_gate,
                 w_branch_moba, w_branch_sb, w_out, ffn2_norm, ffn2_w_gate, ffn2_w_up, ffn2_w_down, final_norm):
    f = lambda a: np.ascontiguousarray(np.asarray(a, dtype=np.float32))
    shared = {}

    def gu(w):
        w = f(w).reshape(8, 128, NF, 128)
        return np.ascontiguousarray(w.transpose(2, 1, 0, 3))

    shared['wg1'] = gu(ffn1_w_gate[0]); shared['wu1'] = gu(ffn1_w_up[0]); shared['wd1'] = f(ffn1_w_down[0]).reshape(NF, 128, D)
    shared['wg2'] = gu(ffn2_w_gate[0]); shared['wu2'] = gu(ffn2_w_up[0]); shared['wd2'] = f(ffn2_w_down[0]).reshape(NF, 128, D)
    wi = f(w_in[0])
    qa, ka, va, qb, kb, vb = [wi[:, i * 512:(i + 1) * 512] for i in range(6)]
    cols = []
    for h in range(8):
        for part in (qa, ka, va, qb, kb, vb):
            cols.append(part[:, h * 64:(h + 1) * 64])
    cols.append(wi[:, 3072:5120])
    wr = np.concatenate(cols, axis=1)
    wr = wr.reshape(8, 128, 40, 128)
    shared['win'] = np.ascontiguousarray(wr.transpose(2, 1, 0, 3))
    shared['wa'] = f(w_branch_moba[0]).reshape(4, 128, D)
    shared['wb'] = f(w_branch_sb[0]).reshape(4, 128, D)
    shared['wo'] = f(w_out[0]).reshape(8, 128, D)
    gn = np.concatenate([f(ffn1_norm[0]).reshape(8, 128).T, f(mix_norm[0]).reshape(8, 128).T,
                         f(ffn2_norm[0]).reshape(8, 128).T], axis=1)
    shared['gains'] = np.ascontiguousarray(gn)
    shared['bgate'] = np.ascontiguousarray(f(b_gate[0]).reshape(16, 128).T)
    shared['gfin'] = f(final_norm).reshape(1, D)
    shared.update(host_consts())
    xs = f(x)[0]
    ps_ = np.asarray(positions).astype(np.int32)[0]
    in_maps = []
    for c in range(NCORE):
        m = dict(shared)
        m['x'] = np.ascontiguousarray(xs[c * TL:(c + 1) * TL])
        m['pos'] = np.ascontiguousarray(ps_[c * TL:(c + 1) * TL]).reshape(1, TL)
        in_maps.append(m)
    return in_maps


_NC_CACHE = {}


def kernel(**inputs):
    debug = bool(int(os.environ.get("KDEBUG", "0")))
    phases = int(os.environ.get("KPHASES", "3"))
    key = (debug, phases)
    if key not in _NC_CACHE:
        _NC_CACHE[key] = build_program(debug=debug, phases=phases)
    nc = _NC_CACHE[key]
    in_maps = _prep_inputs(**inputs)
    res = run_bass_kernel_spmd(nc, in_maps, core_ids=list(range(NCORE)))
    out = np.concatenate([np.asarray(r["y"], dtype=np.float32) for r in res.results], axis=0)[None]
    if debug:
        kernel.last_results = res.results
    return out
```

```python
import os
import contextlib
import numpy as np
import ml_dtypes
import concourse.bass as bass
import concourse.mybir as mybir
from concourse.bass_utils import run_bass_kernel_spmd

F32 = mybir.dt.float32
BF16 = mybir.dt.bfloat16
I32 = mybir.dt.int32
AF = mybir.ActivationFunctionType
ALU = mybir.AluOpType
AX = mybir.AxisListType

NCORE = 8
T = 16384
D = 1024
TL = T // NCORE
NTL = TL // 128
NG = TL // 512
FF = 2816
NF = FF // 128
EPS = 1e-6
NEGM = -30000.0
PI = float(np.pi)


class Sched:
    def __init__(self):
        self.ops = []
        self.lastw = {}
        self.readers = {}
        self.stream_cnt = {}
        self.barriers = []
        self.last_on_eng = {}

    def add(self, eng, emit, r=(), w=(), stream=None, inc=16):
        idx = len(self.ops)
        deps = {}
        for t in r:
            x = self.lastw.get(t)
            if x is not None:
                deps[x] = 'raw'
        for t in w:
            x = self.lastw.get(t)
            if x is not None and x not in deps:
                deps[x] = 'waw'
            for y in self.readers.get(t, ()):
                if y not in deps:
                    deps[y] = 'war'
        op = dict(eng=eng, emit=emit, deps=deps, stream=stream, inc=inc, idx=idx,
                  nbar=len(self.barriers), sig=None, signaler=False)
        if stream is not None:
            c = self.stream_cnt.get(stream, 0) + (inc if inc is not None else 1)
            self.stream_cnt[stream] = c
            op['sig'] = (('s', stream), c)
        else:
            self.last_on_eng[eng] = idx
        self.ops.append(op)
        for t in r:
            self.readers.setdefault(t, []).append(idx)
        for t in w:
            self.lastw[t] = idx
            self.readers[t] = []
        return idx

    def barrier(self):
        self.barriers.append(dict(compute_last=dict(self.last_on_eng), streams=dict(self.stream_cnt)))

    @staticmethod
    def _needs_wait(o, w, kind):
        if w['stream'] is not None:
            return True
        if o['stream'] is None and w['eng'] == o['eng']:
            if o['eng'] == 'pe':
                return False
            return kind == 'raw'
        return True

    def finalize(self):
        ops = self.ops
        for o in ops:
            for wi, kind in o['deps'].items():
                w = ops[wi]
                if self._needs_wait(o, w, kind) and w['stream'] is None:
                    w['signaler'] = True
        for b in self.barriers:
            for e, wi in b['compute_last'].items():
                ops[wi]['signaler'] = True
        cnt = {}
        for o in ops:
            if o['stream'] is None and o['signaler']:
                c = cnt.get(o['eng'], 0) + 1
                cnt[o['eng']] = c
                o['sig'] = (('e', o['eng']), c)
        sems = set()
        for o in ops:
            if o['sig'] is not None:
                sems.add(o['sig'][0])
        return sorted(sems, key=str)

    def emit_engine(self, eng, e, semh):
        ops = self.ops
        waited = {}

        def wait(key, c):
            if waited.get(key, 0) < c:
                e.wait_ge(semh[key], c)
                waited[key] = c

        nbar_done = 0
        for o in ops:
            if o['eng'] != eng:
                continue
            while nbar_done < o['nbar']:
                b = self.barriers[nbar_done]
                for en, wi in b['compute_last'].items():
                    if en == eng:
                        continue
                    k, c = ops[wi]['sig']
                    wait(k, c)
                for st, c in b['streams'].items():
                    wait(('s', st), c)
                nbar_done += 1
            for wi, kind in o['deps'].items():
                w = ops[wi]
                if self._needs_wait(o, w, kind):
                    k, c = w['sig']
                    wait(k, c)
            ins = o['emit'](e)
            if o['sig'] is not None:
                k, c = o['sig']
                if o['inc'] is None:
                    ins.then_inc(semh[k])
                elif o['stream'] is not None:
                    ins.then_inc(semh[k], o['inc'])
                else:
                    ins.then_inc(semh[k], 1)


def _bf(a):
    return np.ascontiguousarray(a.astype(ml_dtypes.bfloat16))


def host_consts():
    c = {}
    c['ident'] = _bf(np.eye(128, dtype=np.float32))
    j = np.arange(128)[:, None]
    s = np.arange(128)[None, :]
    c['negtri'] = _bf(np.where(j >= s, -1.0, 0.0))
    c['negones'] = _bf(-np.ones((128, 128), np.float32))
    Rm = np.zeros((128, 128), np.float32)
    for b in (0, 64):
        for i in range(8):
            Rm[b + i + 8, b + i] = -1.0
            Rm[b + i, b + i + 8] = 1.0
    c['rot32'] = Rm
    half = 8
    inv_freq = (np.float32(500000.0) ** (-np.arange(half, dtype=np.float32) * np.float32(2.0) / np.float32(16))).astype(np.float32)
    invf = np.zeros((128, 1), np.float32)
    for r in range(128):
        if (r % 64) < 16:
            invf[r, 0] = inv_freq[(r % 64) % 8]
    c['invf'] = invf
    scl = np.ones((128, 2), np.float32)
    scl[64:128, 0] = 0.125
    c['scl'] = scl
    p = np.arange(128)[:, None, None]
    i = np.arange(4)[None, :, None]
    t = np.arange(512)[None, None, :]
    valid = (128 * i + p) < t
    c['sbm01'] = _bf(np.where(valid, 1.0, 0.0))
    c['sbneg'] = _bf(np.where(valid, 0.0, NEGM))
    kp = 128 * i + p
    kb = kp // 256
    qb = t // 256
    mm = np.where(kb == qb, np.where(kp <= t, 0.0, NEGM), np.where(kb < qb, 0.0, NEGM))
    c['mobam'] = _bf(mm)
    n = np.arange(64)[:, None]
    ss = np.arange(T)[None, :]
    c['blockind'] = _bf(np.where((ss // 256) == n, 1.0, 0.0))
    c['ones_f32'] = np.ones((128, 64), np.float32)
    return c


class _PidProxy:
    def __init__(self, e):
        self.e = e

    def __mul__(self, k):
        return self.e.partition_id() * k


def build_program(debug=False, phases=3):
    nc = bass.Bass("TRN2", target_bir_lowering=False)
    S = Sched()

    def din(name, shape, dt=F32):
        return nc.dram_tensor(name, list(shape), dt, kind="ExternalInput").ap()

    x_d = din("x", [TL, D])
    pos_d = din("pos", [1, TL], I32)
    wg_d = [din(f"wg{i}", [NF, 128, 8, 128]) for i in (1, 2)]
    wu_d = [din(f"wu{i}", [NF, 128, 8, 128]) for i in (1, 2)]
    wd_d = [din(f"wd{i}", [NF, 128, D]) for i in (1, 2)]
    win_d = din("win", [40, 128, 8, 128])
    wa_d = din("wa", [4, 128, D])
    wb_d = din("wb", [4, 128, D])
    wo_d = din("wo", [8, 128, D])
    gains_d = din("gains", [128, 24])
    bgate_d = din("bgate", [128, 16])
    gfin_d = din("gfin", [1, D])
    ident_d = din("ident", [128, 128], BF16)
    negtri_d = din("negtri", [128, 128], BF16)
    negones_d = din("negones", [128, 128], BF16)
    rot32_d = din("rot32", [128, 128])
    invf_d = din("invf", [128, 1])
    scl_d = din("scl", [128, 2])
    sbm01_d = din("sbm01", [128, 4, 512], BF16)
    sbneg_d = din("sbneg", [128, 4, 512], BF16)
    mobam_d = din("mobam", [128, 4, 512], BF16)
    blockind_d = din("blockind", [64, T], BF16)
    onesf_d = din("ones_f32", [128, 64])
    y_d = nc.dram_tensor("y", [TL, D], F32, kind="ExternalOutput").ap()
    if debug:
        dbg_h1 = nc.dram_tensor("dbg_h1", [TL, D], F32, kind="ExternalOutput").ap()
        dbg_B = nc.dram_tensor("dbg_B", [128, T], BF16, kind="ExternalOutput").ap()
        dbg_q = nc.dram_tensor("dbg_q", [6 * 64, T], BF16, kind="ExternalOutput").ap()

    A_d = [[nc.dram_tensor(f"xa{i}_{hf}", [1024, TL // 2], BF16) for hf in range(2)] for i in range(3)]
    G_d = [[nc.dram_tensor(f"xg{i}_{hf}", [NCORE * 1024, TL // 2], BF16) for hf in range(2)] for i in range(3)]
    B_d = [nc.dram_tensor(f"xb{m}", [64, T], BF16) for m in range(2)]
    GB_d = [nc.dram_tensor(f"xgb{m}", [NCORE * 64, T], BF16) for m in range(2)]
    Lq_d = [nc.dram_tensor(f"xl{i}", [NCORE * 128, TL], BF16) for i in range(3)]
    LB_d = [nc.dram_tensor(f"xlb{m}", [NCORE * 64, TL], BF16) for m in range(2)]

    es = contextlib.ExitStack()
    with es:
        def sb(name, shape, dt):
            return es.enter_context(nc.sbuf_tensor(name, list(shape), dt))

        def ps(name, shape, dt):
            return es.enter_context(nc.psum_tensor(name, list(shape), dt))

        h_sb = sb("h_sb", [128, NTL, D], F32)
        big2 = sb("big2", [128, T], BF16)
        big3 = sb("big3", [128, T], BF16)
        w32 = sb("w32", [128, 4096], F32)
        wgs = sb("wgs", [128, 4, 8, 128], BF16)
        wus = sb("wus", [128, 4, 8, 128], BF16)
        wds = sb("wds", [128, 4, D], BF16)
        xn_s = sb("xn_s", [128, 4, D], BF16)
        stg = sb("stg", [128, 2, 512], BF16)
        junk = sb("junk", [128, D], BF16)
        ident = sb("identb", [128, 128], BF16)
        negtri = sb("negtrib", [128, 128], BF16)
        negones = sb("negonesb", [128, 128], BF16)
        rot32 = sb("rot32b", [128, 128], F32)
        invf = sb("invfb", [128, 1], F32)
        scl = sb("sclb", [128, 2], F32)
        gains = sb("gainsb", [128, 24], F32)
        bgate = sb("bgateb", [128, 16], F32)
        sbm01 = sb("sbm01b", [128, 4, 512], BF16)
        sbneg = sb("sbnegb", [128, 4, 512], BF16)
        mobam = sb("mobamb", [128, 4, 512], BF16)
        onesf = sb("onesfb", [128, 64], F32)
        ss = sb("ssb", [128, 96], F32)
        rstd = sb("rstdb", [128, 96], F32)
        small = sb("smallb", [128, 512], F32)
        posi = sb("posib", [128, 512], I32)
        epsc = sb("epscb", [128, 1], F32)

        psA = ps("psA", [128, 1024], F32)
        psB = ps("psB", [128, 1024], F32)
        psC = ps("psC", [128, 1024], F32)
        psD = ps("psD", [128, 1024], F32)
        banks = [psA[:, 0:512], psA[:, 512:1024], psB[:, 0:512], psB[:, 512:1024],
                 psC[:, 0:512], psC[:, 512:1024], psD[:, 0:512], psD[:, 512:1024]]

        xnT = big2[:, 0:4096].rearrange("p (k n) -> p k n", n=512)
        actT = big2[:, 4096:4096 + NF * 512].rearrange("p (k n) -> p k n", n=512)
        sgt = w32[:, 0:512]
        xf32 = w32[:, 512:1024]
        rt1 = w32[:, 1024:1536]
        rt2 = w32[:, 1536:2048]
        cfull = w32[:, 2048:2560]
        sfull = w32[:, 2560:3072]
        posf = w32[:, 3072:3584]
        angt = w32[:, 3584:4096]

        def dma(eng, out, in_, r=(), w=(), stream=None):
            return S.add(eng, lambda e, out=out, in_=in_: e.dma_start(out=out, in_=in_), r=r, w=w, stream=stream)

        for (dst, src, nm) in [(ident, ident_d, 'ident'), (negtri, negtri_d, 'negtri'), (negones, negones_d, 'negones'),
                               (rot32, rot32_d, 'rot32'), (invf, invf_d, 'invf'), (scl, scl_d, 'scl'),
                               (gains, gains_d, 'gains'), (bgate, bgate_d, 'bgate'), (onesf, onesf_d, 'onesf')]:
            dma('sp', dst[:, :], src[:, :], w=[nm], stream='const')
        dma('sp', sbm01[:, :, :], sbm01_d[:, :, :], w=['sbm01'], stream='const')
        dma('sp', sbneg[:, :, :], sbneg_d[:, :, :], w=['sbneg'], stream='const')
        dma('sp', mobam[:, :, :], mobam_d[:, :, :], w=['mobam'], stream='const')
        S.add('dve', lambda e: e.memset(ss[:, :], 0.0), w=['ss'])
        S.add('dve', lambda e: e.memset(epsc[:, :], EPS), w=['epsc'])
        S.barrier()

        norm_ctr = [0]

        def rmsnorm_group(tiles, gain_col0):
            cols = []
            for i, tt in enumerate(tiles):
                col = norm_ctr[0]
                norm_ctr[0] += 1
                cols.append(col)
                S.add('act', lambda e, tt=tt, col=col: e.activation(out=junk[:, :], in_=h_sb[:, tt, :], func=AF.Square,
                                                                     accum_out=ss[:, col:col + 1]),
                      r=[('h', tt), 'ss'], w=[('ss', col), 'junk'])
            c0 = cols[0]
            S.add('act', lambda e, c0=c0: e.activation(out=rstd[:, c0:c0 + 4], in_=ss[:, c0:c0 + 4], func=AF.Sqrt,
                                                       scale=1.0 / D, bias=epsc[:, 0:1]),
                  r=[('ss', c) for c in cols] + ['epsc'], w=[('rstd0', c0)])
            S.add('dve', lambda e, c0=c0: e.reciprocal(out=rstd[:, c0:c0 + 4], in_=rstd[:, c0:c0 + 4]),
                  r=[('rstd0', c0)], w=[('rstd', c) for c in cols])
            xn4 = xn_s
            for i, tt in enumerate(tiles):
                S.add('dve', lambda e, tt=tt, col=cols[i], i=i: e.tensor_scalar(
                    out=xn4[:, i, :], in0=h_sb[:, tt, :], scalar1=rstd[:, col:col + 1], scalar2=None, op0=ALU.mult),
                      r=[('h', tt), ('rstd', cols[i])], w=[('xn4', i)])
            for rnd in range(4):
                for kk in range(2):
                    k = rnd * 2 + kk
                    bk = 6 + kk
                    for i in range(4):
                        S.add('pe', lambda e, k=k, bk=bk, i=i: e.matmul(
                            banks[bk][:, i * 128:(i + 1) * 128], lhsT=xn4[:, i, k * 128:(k + 1) * 128], rhs=ident[:, :],
                            start=True, stop=True),
                              r=[('xn4', i), 'ident'], w=[('bank', bk)])
                    gc = gain_col0 + k
                    eng = 'dve' if kk == 0 else 'act'
                    if eng == 'dve':
                        S.add('dve', lambda e, k=k, bk=bk, gc=gc: e.tensor_scalar(
                            out=xnT[:, k, :], in0=banks[bk], scalar1=gains[:, gc:gc + 1], scalar2=None, op0=ALU.mult),
                              r=[('bank', bk), 'gains'], w=[('xnT', k)])
                    else:
                        S.add('act', lambda e, k=k, bk=bk, gc=gc: e.activation(
                            out=xnT[:, k, :], in_=banks[bk], func=AF.Copy, scale=gains[:, gc:gc + 1]),
                              r=[('bank', bk), 'gains'], w=[('xnT', k)])

        wcnt = {'g': 0, 'u': 0, 'd': 0, 'c': 0}

        xnTa = big2[:, :].rearrange("p (k n) -> p k n", n=TL)
        actTp = [w32[:, 1024 + i * 512:1024 + (i + 1) * 512].bitcast(BF16).rearrange("p (j n) -> p j n", n=512) for i in range(2)]
        sgt2 = [w32[:, 0:512], w32[:, 512:1024]]

        def rmsnorm_all(gain_col0, groups=tuple(range(NG))):
            for g in groups:
                tiles = [g * 4 + i for i in range(4)]
                cols = []
                for i, tt in enumerate(tiles):
                    col = norm_ctr[0]
                    norm_ctr[0] += 1
                    cols.append(col)
                    S.add('act', lambda e, tt=tt, col=col: e.activation(out=junk[:, :], in_=h_sb[:, tt, :], func=AF.Square,
                                                                         accum_out=ss[:, col:col + 1]),
                          r=[('h', tt), 'ss'], w=[('ss', col), 'junk'])
                c0 = cols[0]
                S.add('act', lambda e, c0=c0: e.activation(out=rstd[:, c0:c0 + 4], in_=ss[:, c0:c0 + 4], func=AF.Sqrt,
                                                           scale=1.0 / D, bias=epsc[:, 0:1]),
                      r=[('ss', c) for c in cols] + ['epsc'], w=[('rstd0', c0)])
                S.add('dve', lambda e, c0=c0: e.reciprocal(out=rstd[:, c0:c0 + 4], in_=rstd[:, c0:c0 + 4]),
                      r=[('rstd0', c0)], w=[('rstd', c) for c in cols])
                for i, tt in enumerate(tiles):
                    S.add('dve', lambda e, tt=tt, col=cols[i], i=i: e.tensor_scalar(
                        out=xn_s[:, i, :], in0=h_sb[:, tt, :], scalar1=rstd[:, col:col + 1], scalar2=None, op0=ALU.mult),
                          r=[('h', tt), ('rstd', cols[i])], w=[('xn4', i)])
                for rnd in range(4):
                    for kk in range(2):
                        k = rnd * 2 + kk
                        bk = 6 + kk
                        for i in range(4):
                            S.add('pe', lambda e, k=k, bk=bk, i=i: e.matmul(
                                banks[bk][:, i * 128:(i + 1) * 128], lhsT=xn_s[:, i, k * 128:(k + 1) * 128], rhs=ident[:, :],
                                start=True, stop=True),
                                  r=[('xn4', i), 'ident'], w=[('bank', bk)])
                        gc = gain_col0 + k
                        if kk == 0:
                            S.add('dve', lambda e, k=k, bk=bk, gc=gc, g=g: e.tensor_scalar(
                                out=xnTa[:, k, g * 512:(g + 1) * 512], in0=banks[bk], scalar1=gains[:, gc:gc + 1], scalar2=None,
                                op0=ALU.mult), r=[('bank', bk), 'gains'], w=[('xnTa', k, g)])
                        else:
                            S.add('act', lambda e, k=k, bk=bk, gc=gc, g=g: e.activation(
                                out=xnTa[:, k, g * 512:(g + 1) * 512], in_=banks[bk], func=AF.Copy, scale=gains[:, gc:gc + 1]),
                                  r=[('bank', bk), 'gains'], w=[('xnTa', k, g)])

        pool6 = [0, 1, 2, 3, 6, 7]
        dctr = [0]
        actp_ctr = [0]

        def ffn_all(widx, groups=tuple(range(NG))):
            for fp in range(NF // 2):
                slots = []
                for j in range(2):
                    f = 2 * fp + j
                    sg_ = wcnt['g'] % 4
                    wcnt['g'] += 1
                    slots.append(sg_)
                    S.add('pool', lambda e, f=f, sg_=sg_: e.dma_start(out=wgs[:, sg_, :, :], in_=wg_d[widx][f, :, :, :]),
                          w=[('wgs', sg_)], stream=('wg', sg_))
                    S.add('pool', lambda e, f=f, sg_=sg_: e.dma_start(out=wus[:, sg_, :, :], in_=wu_d[widx][f, :, :, :]),
                          w=[('wus', sg_)], stream=('wu', sg_))
                    S.add('pool', lambda e, f=f, sg_=sg_: e.dma_start(out=wds[:, sg_, :], in_=wd_d[widx][f, :, :]),
                          w=[('wds', sg_)], stream=('wd', sg_))
                for g in groups:
                    ap_ = actp_ctr[0] % 2
                    actp_ctr[0] += 1
                    for j in range(2):
                        sg_ = slots[j]
                        for k in range(8):
                            S.add('pe', lambda e, k=k, sg_=sg_, g=g: e.matmul(
                                banks[4], lhsT=wgs[:, sg_, k, :], rhs=xnTa[:, k, g * 512:(g + 1) * 512],
                                start=(k == 0), stop=(k == 7)),
                                  r=[('wgs', sg_), ('xnTa', k, g)], w=[('bank', 4)])
                        for k in range(8):
                            S.add('pe', lambda e, k=k, sg_=sg_, g=g: e.matmul(
                                banks[5], lhsT=wus[:, sg_, k, :], rhs=xnTa[:, k, g * 512:(g + 1) * 512],
                                start=(k == 0), stop=(k == 7)),
                                  r=[('wus', sg_), ('xnTa', k, g)], w=[('bank', 5)])
                        S.add('act', lambda e, j=j: e.activation(out=sgt2[j], in_=banks[4], func=AF.Silu),
                              r=[('bank', 4)], w=[('sgt2', j)])
                        S.add('dve', lambda e, j=j, ap_=ap_: e.tensor_tensor(out=actTp[ap_][:, j, :], in0=sgt2[j], in1=banks[5],
                                                                             op=ALU.mult),
                              r=[('sgt2', j), ('bank', 5)], w=[('actTp', ap_)])
                    for half in range(2):
                        for ts in range(4):
                            bk = pool6[dctr[0] % 6]
                            dctr[0] += 1
                            tt = g * 4 + ts
                            for j in range(2):
                                S.add('pe', lambda e, j=j, ap_=ap_, ts=ts, half=half, bk=bk, sg_=slots[j]: e.matmul(
                                    banks[bk], lhsT=actTp[ap_][:, j, ts * 128:(ts + 1) * 128],
                                    rhs=wds[:, sg_, half * 512:(half + 1) * 512], start=(j == 0), stop=(j == 1)),
                                      r=[('actTp', ap_), ('wds', slots[j])], w=[('bank', bk)])
                            S.add('dve', lambda e, bk=bk, tt=tt, half=half: e.scalar_tensor_tensor(
                                out=h_sb[:, tt, half * 512:(half + 1) * 512], in0=banks[bk], scalar=0.5,
                                in1=h_sb[:, tt, half * 512:(half + 1) * 512], op0=ALU.mult, op1=ALU.add),
                                  r=[('bank', bk), ('h', tt)], w=[('h', tt)])

        for tt in range(NTL):
            dma('sp', h_sb[:, tt, :], x_d[tt * 128:(tt + 1) * 128, :], w=[('h', tt)], stream=('x', tt))

        cs32 = big3[:, 0:8192].bitcast(F32).rearrange("p (a n) -> p a n", n=512)
        MAGIC = 12582912.0
        pctr = 0
        for hf in range(2):
            groups = (2 * hf, 2 * hf + 1)
            rmsnorm_all(0, groups)
            ffn_all(0, groups)
            S.barrier()
            if debug:
                for g in groups:
                    for tt in range(g * 4, g * 4 + 4):
                        dma('sp', dbg_h1[tt * 128:(tt + 1) * 128, :], h_sb[:, tt, :], r=[('h', tt)], stream='dbg')
            rmsnorm_all(8, groups)
            for g in groups:
                S.add('sp', lambda e, g=g: e.dma_start(out=posi[:, :], in_=pos_d[0:1, g * 512:(g + 1) * 512].broadcast_to([128, 512])),
                      w=['posi'], stream='pos')
                S.add('dve', lambda e: e.tensor_copy(out=posf, in_=posi[:, :]), r=['posi'], w=['posf'])
                S.add('dve', lambda e: e.tensor_scalar(out=angt, in0=posf, scalar1=invf[:, 0:1], scalar2=None, op0=ALU.mult),
                      r=['posf', 'invf'], w=['angt'])
                for which, shift in ((0, 0.0), (1, 0.5 * PI)):
                    dst = cs32[:, 2 * g + which, :]
                    if shift != 0.0:
                        S.add('dve', lambda e, shift=shift: e.tensor_scalar(out=rt2, in0=angt, scalar1=shift, scalar2=None, op0=ALU.add),
                              r=['angt'], w=['rt2'])
                        src_, srct = rt2, 'rt2'
                    else:
                        src_, srct = angt, 'angt'
                    S.add('dve', lambda e, src_=src_: e.tensor_scalar(out=rt1, in0=src_, scalar1=1.0 / (2 * PI), scalar2=MAGIC,
                                                                      op0=ALU.mult, op1=ALU.add), r=[srct], w=['rt1'])
                    S.add('dve', lambda e: e.tensor_scalar(out=rt1, in0=rt1, scalar1=-MAGIC, scalar2=None, op0=ALU.add),
                          r=['rt1'], w=['rt1'])
                    S.add('dve', lambda e, src_=src_: e.scalar_tensor_tensor(out=rt1, in0=rt1, scalar=-2 * PI, in1=src_,
                                                                             op0=ALU.mult, op1=ALU.add), r=['rt1', srct], w=['rt1'])
                    S.add('dve', lambda e: e.tensor_scalar(out=rt1, in0=rt1, scalar1=-PI, scalar2=PI, op0=ALU.max, op1=ALU.min),
                          r=['rt1'], w=['rt1'])
                    S.add('act', lambda e, dst=dst: e.activation(out=dst, in_=rt1, func=AF.Sin), r=['rt1'], w=[('cs', g, which)])
            for typ in (2, 1, 0):
                for hd in range(8):
                    cc = hd * 3 + typ
                    sc = wcnt['g'] % 4
                    wcnt['g'] += 1
                    S.add('pool', lambda e, cc=cc, sc=sc: e.dma_start(out=wgs[:, sc, :, :], in_=win_d[cc, :, :, :]),
                          w=[('wgs', sc)], stream=('wg', sc))
                    for gi, g in enumerate(groups):
                        bk = 6 + (pctr % 2)
                        so = pctr % 2
                        pctr += 1
                        for k in range(8):
                            S.add('pe', lambda e, k=k, sc=sc, bk=bk, g=g: e.matmul(
                                banks[bk], lhsT=wgs[:, sc, k, :], rhs=xnTa[:, k, g * 512:(g + 1) * 512], start=(k == 0), stop=(k == 7)),
                                  r=[('wgs', sc), ('xnTa', k, g)], w=[('bank', bk)])
                        if typ == 0:
                            S.add('act', lambda e, bk=bk: e.activation(out=xf32, in_=banks[bk], func=AF.Copy),
                                  r=[('bank', bk)], w=['xf32'])
                            S.add('pe', lambda e: e.matmul(banks[5], lhsT=rot32[:, :], rhs=xf32, start=True, stop=True),
                                  r=['xf32', 'rot32'], w=[('bank', 5)])
                            S.add('dve', lambda e, g=g: e.tensor_tensor(out=rt1, in0=xf32, in1=cs32[:, 2 * g + 1, :], op=ALU.mult),
                                  r=['xf32', ('cs', g, 1)], w=['rt1'])
                            S.add('dve', lambda e, g=g: e.tensor_tensor(out=rt2, in0=banks[5], in1=cs32[:, 2 * g, :], op=ALU.mult),
                                  r=[('bank', 5), ('cs', g, 0)], w=['rt2'])
                            S.add('dve', lambda e, so=so: e.tensor_tensor(out=stg[:, so, :], in0=rt1, in1=rt2, op=ALU.add),
                                  r=['rt1', 'rt2'], w=[('stg', so)])
                        else:
                            S.add('dve', lambda e, bk=bk, so=so, typ=typ: e.tensor_scalar(
                                out=stg[:, so, :], in0=banks[bk], scalar1=scl[:, typ - 1:typ], scalar2=None, op0=ALU.mult),
                                  r=[('bank', bk), 'scl'], w=[('stg', so)])
                        S.add('sp', lambda e, so=so, typ=typ, hd=hd, gi=gi, hf=hf: e.dma_start(
                            out=A_d[typ][hf][hd * 128:(hd + 1) * 128, gi * 512:(gi + 1) * 512], in_=stg[:, so, :]),
                              r=[('stg', so)], w=[('A', typ, hf)], stream=('aout', so))
                if not (hf == 1 and typ == 0):
                    S.add('pool', lambda e, typ=typ, hf=hf: e.collective_compute(
                        "AllGather", ALU.bypass, replica_groups=[list(range(NCORE))],
                        ins=[A_d[typ][hf].ap().opt()], outs=[G_d[typ][hf].ap().opt()]),
                          r=[('A', typ, hf)], w=[('G', typ, hf)], stream=('cc', typ), inc=None)
            S.barrier()
        S.add('pool', lambda e: e.collective_compute(
            "AllGather", ALU.bypass, replica_groups=[list(range(NCORE))],
            ins=[A_d[0][1].ap().opt()], outs=[G_d[0][1].ap().opt()]),
              r=[('A', 0, 1)], w=[('G', 0, 1)], stream=('cc', 0), inc=None)

        pid = {}

        def pull(typ):
            for hf in range(2):
                S.add('sp', lambda e, typ=typ, hf=hf: e.dma_start(
                    out=Lq_d[typ].ap().rearrange("(s o r) t -> s o r t", s=NCORE, o=1)[:, :, :, hf * 1024:(hf + 1) * 1024],
                    in_=G_d[typ][hf].ap().rearrange("(s h r) t -> s h r t", s=NCORE, h=NCORE)[:, bass.ds(pid['sp'], 1), :, :]),
                      r=[('G', typ, hf)], w=[('Lq', typ)], stream=('lq', typ))

        pull(2)
        pull(1)
        if debug:
            pull(0)
            S.barrier()

        def gsl(typ, src, off64, eng, cols=None):
            base = src * 128 + off64
            if cols is None:
                return Lq_d[typ].ap()[base:base + 64, :]
            return Lq_d[typ].ap()[base:base + 64, cols[0]:cols[1]]

        if debug:
            for it in range(6):
                typ, off = it // 2, (it % 2) * 64
                for src in range(NCORE):
                    S.add('sp', lambda e, it=it, typ=typ, off=off, src=src: e.dma_start(
                        out=big2[0:64, src * TL:(src + 1) * TL],
                        in_=gsl(typ, src, off, 'sp')), r=[('Lq', typ)], w=['dbgbuf'], stream='dbg')
                S.add('sp', lambda e, it=it: e.dma_start(out=dbg_q[it * 64:(it + 1) * 64, :], in_=big2[0:64, :]),
                      r=['dbgbuf'], w=['dbgq'], stream='dbg2')
            S.barrier()

        if phases >= 2:
            Vsb = big3[:, 0:8320].rearrange("p (a n) -> p a n", n=65)
            o = 8320
            Abuf = [big3[:, o + i * 1024:o + (i + 1) * 1024] for i in range(2)]
            o += 2048
            Lbuf = [big3[:, o + i * 1024:o + (i + 1) * 1024] for i in range(2)]
            o += 2048
            LaccA = [big3[:, o + i * 512:o + (i + 1) * 512] for i in range(3)]
            o += 1536
            LaccB = [big3[:, o + i * 512:o + (i + 1) * 512] for i in range(2)]
            o += 1024
            Qp = [big3[:, o + i * 512:o + (i + 1) * 512] for i in range(2)]
            qsb = Qp
            o += 1024
            selpad = big3[:, o:o + 128]
            o += 128
            vstage = big3[:, 8320:8320 + 2048]
            assert o <= T, o
            ebuf = [w32[:, i * 1024:(i + 1) * 1024] for i in range(2)]
            Lacc32 = w32[:, 2048:2560]
            Lsum = w32[:, 2560:3072]
            ystg32 = w32[:, 3072:3584]
            ybc = w32[:, 3584:4096]
            gsb = small[:, 0:64]
            m8 = small[:, 64:72]
            kms = small[:, 128:192]
            kmh = small[:, 192:256]
            recrow = small[:, 256:512]

            kmhi = stg[:, 0, 0:64]
            kmlo = stg[:, 0, 64:128]
            ystgb = stg[:, 1, :]

            KT = [('kT', s_) for s_ in range(NCORE)]
            vst2 = [big3[0:64, 8320:8320 + 2048], big3[64:128, 8320:8320 + 2048]]
            idn2 = [ident[0:64, 0:64], ident[64:128, 64:128]]

            def load_kv(ktyp, koff, vtyp, voff, tag):
                for src in range(NCORE):
                    S.add('sp', lambda e, src=src: e.dma_start(out=big2[0:64, src * TL:(src + 1) * TL],
                                                                 in_=gsl(ktyp, src, koff, 'sp')),
                          r=[('Lq', ktyp)], w=[('kT', src)], stream=('kv', src))
                for src in range(NCORE):
                    vs = src % 2
                    S.add('sp', lambda e, src=src, vs=vs: e.dma_start(out=vst2[vs], in_=gsl(vtyp, src, voff, 'sp')),
                          r=[('Lq', vtyp)], w=[('vstage', vs)], stream=('vst', vs))
                    for half in range(2):
                        bk = 6 + half
                        for i in range(8):
                            tl = half * 8 + i
                            S.add('pe', lambda e, bk=bk, i=i, tl=tl, vs=vs: e.matmul(
                                banks[bk][:, i * 64:(i + 1) * 64], lhsT=vst2[vs][:, tl * 128:(tl + 1) * 128],
                                rhs=idn2[vs], start=True, stop=True),
                                  r=[('vstage', vs), 'ident'], w=[('bank', bk)])
                        t0 = src * 16 + half * 8
                        S.add('dve', lambda e, bk=bk, t0=t0: e.tensor_copy(
                            out=Vsb[:, t0:t0 + 8, 0:64], in_=banks[bk].rearrange("p (a n) -> p a n", n=64)),
                              r=[('bank', bk)], w=['Vsb'])

            S.add('dve', lambda e: e.memset(Vsb[:, :, 64:65], 1.0), w=['Vsb'])
            load_kv(2, 0, 2, 64, 'sb')
            steps = []
            for qg in range(T // 512):
                n = 2 * qg + 2
                for s in range(n - 1, -1, -1):
                    steps.append((qg, s, s == n - 1, s == 0))
            NS = len(steps)
            zts = [psA, psB, psC]

            def qload(qg):
                src, gg = qg // 4, qg % 4
                sl = qg % 2
                S.add('sp', lambda e, src=src, gg=gg, sl=sl: e.dma_start(
                    out=qsb[sl][0:64, :], in_=gsl(1, src, 64, 'sp', (gg * 512, (gg + 1) * 512))),
                      r=[('Lq', 1)], w=[('qsb', sl)], stream=('q', sl))

            def z_mm(i):
                qg, s, first, last = steps[i]
                zs = i % 3
                for j, kt in enumerate((2 * s + 1, 2 * s)):
                    S.add('pe', lambda e, kt=kt, zs=zs, j=j, qg=qg: e.matmul(
                        banks[2 * zs + j], lhsT=big2[0:64, kt * 128:(kt + 1) * 128], rhs=qsb[qg % 2][0:64, :],
                        start=True, stop=True),
                          r=KT + [('qsb', qg % 2)], w=[('Z', zs)])

            def act12(i):
                qg, s, first, last = steps[i]
                zs = i % 3
                es_ = i % 2
                S.add('act', lambda e, zs=zs, es_=es_: e.activation(out=ebuf[es_], in_=zts[zs][:, :], func=AF.Exp),
                      r=[('Z', zs)], w=[('e', es_)])
                S.add('act', lambda e, es_=es_: e.activation(out=Lbuf[es_], in_=ebuf[es_], func=AF.Ln, bias=1.0),
                      r=[('e', es_)], w=[('L', es_)])
                diag = s >= 2 * qg
                if diag:
                    dj = (s - 2 * qg) * 2
                    S.add('pool', lambda e, es_=es_, dj=dj: e.tensor_tensor(
                        out=Lbuf[es_][:, 0:512], in0=Lbuf[es_][:, 0:512], in1=sbm01[:, dj + 1, :], op=ALU.mult),
                          r=[('L', es_), 'sbm01'], w=[('L', es_)])
                    S.add('pool', lambda e, es_=es_, dj=dj: e.tensor_tensor(
                        out=Lbuf[es_][:, 512:1024], in0=Lbuf[es_][:, 512:1024], in1=sbm01[:, dj, :], op=ALU.mult),
                          r=[('L', es_), 'sbm01'], w=[('L', es_)])

            def lacc(i):
                qg, s, first, last = steps[i]
                es_ = i % 2
                Lh = Lbuf[es_][:, 0:512]
                Ll = Lbuf[es_][:, 512:1024]
                if not first:
                    S.add('dve', lambda e, i=i, Lh=Lh: e.tensor_tensor(out=LaccB[i % 2], in0=Lacc32, in1=Lh, op=ALU.add),
                          r=[('L', es_), 'Lacc32'], w=[('LaccB', i % 2)])
                if last:
                    return
                if first:
                    S.add('dve', lambda e, Lh=Lh, Ll=Ll: e.tensor_tensor(out=Lacc32, in0=Lh, in1=Ll, op=ALU.add),
                          r=[('L', es_)], w=['Lacc32'])
                else:
                    S.add('dve', lambda e, Lh=Lh, Ll=Ll: e.tensor_tensor(out=Lsum, in0=Lh, in1=Ll, op=ALU.add),
                          r=[('L', es_)], w=['Lsum'])
                    S.add('dve', lambda e: e.tensor_tensor(out=Lacc32, in0=Lacc32, in1=Lsum, op=ALU.add),
                          r=['Lsum', 'Lacc32'], w=['Lacc32'])
                S.add('dve', lambda e, i=i: e.tensor_copy(out=LaccA[(i + 1) % 3], in_=Lacc32),
                      r=['Lacc32'], w=[('LaccA', (i + 1) % 3)])

            def p_mm(i):
                qg, s, first, last = steps[i]
                zs = i % 3
                es_ = i % 2
                diag = s >= 2 * qg
                dj = (s - 2 * qg) * 2
                Lh = Lbuf[es_][:, 0:512]
                for j in range(2):
                    Lme = Lbuf[es_][:, j * 512:(j + 1) * 512]
                    seq = [(negtri[:, :], Lme)]
                    if j == 0:
                        if not first:
                            seq.append((negones[:, :], LaccA[i % 3]))
                    else:
                        if first:
                            seq.append((negones[:, :], Lh))
                        else:
                            seq.append((negones[:, :], LaccB[i % 2]))
                    if diag:
                        seq.append((ident[:, :], sbneg[:, dj + (1 - j), :]))
                    for m, (lt, rh) in enumerate(seq):
                        S.add('pe', lambda e, zs=zs, j=j, lt=lt, rh=rh, m=m, nm=len(seq): e.matmul(
                            banks[2 * zs + j], lhsT=lt, rhs=rh, start=False, stop=(m == nm - 1)),
                              r=[('L', es_), ('LaccA', i % 3), ('LaccB', i % 2), 'sbneg', 'negtri', 'negones', 'ident', ('e', es_)],
                              w=[('Z', zs)])

            def act3(i):
                zs = i % 3
                as_ = i % 2
                S.add('act', lambda e, zs=zs, as_=as_: e.activation(out=Abuf[as_], in_=zts[zs][:, :], func=AF.Exp),
                      r=[('Z', zs)], w=[('A', as_)])

            def pv_mm(i):
                qg, s, first, last = steps[i]
                as_ = i % 2
                for j, kt in enumerate((2 * s + 1, 2 * s)):
                    S.add('pe', lambda e, kt=kt, j=j, as_=as_, first=first, last=last: e.matmul(
                        banks[6][0:64, :], lhsT=Vsb[:, kt, 0:64], rhs=Abuf[as_][:, j * 512:(j + 1) * 512],
                        start=(first and j == 0), stop=(last and j == 1)),
                          r=[('A', as_), 'Vsb'], w=['Y'])
                if last:
                    S.add('dve', lambda e: e.tensor_copy(out=ystgb[0:64, :], in_=banks[6][0:64, :]), r=['Y'], w=['ystgb'])
                    S.add('sp', lambda e, qg=qg: e.dma_start(out=B_d[1].ap()[:, qg * 512:(qg + 1) * 512], in_=ystgb[0:64, :]),
                          r=['ystgb'], w=[('B', 1)], stream=('bout', 0))

            qload(0)
            z_mm(0)
            act12(0)
            lacc(0)
            if NS > 1:
                if steps[1][2]:
                    qload(steps[1][0])
                z_mm(1)
            for i in range(NS):
                if i + 1 < NS:
                    act12(i + 1)
                    lacc(i + 1)
                p_mm(i)
                if i + 2 < NS:
                    if steps[i + 2][2]:
                        qload(steps[i + 2][0])
                    z_mm(i + 2)
                act3(i)
                pv_mm(i)

            S.barrier()
            S.add('pool', lambda e: e.collective_compute("AllGather", ALU.bypass, replica_groups=[list(range(NCORE))],
                                                         ins=[B_d[1].ap().opt()], outs=[GB_d[1].ap().opt()]),
                  r=[('B', 1)], w=[('GB', 1)], stream=('cc', 3), inc=None)
            if not debug:
                pull(0)
            load_kv(0, 64, 1, 0, 'moba')
            dma('sp', big2[64:128, :], blockind_d[:, :], w=['kTind'], stream='kvind')
            S.add('dve', lambda e: e.tensor_reduce(out=kms[0:64, :], in_=big2[0:64, :].rearrange("p (n b) -> p n b", b=256),
                                                   axis=AX.X, op=ALU.add), r=KT, w=['kms'])
            S.add('dve', lambda e: e.tensor_scalar(out=kms[0:64, :], in0=kms[0:64, :], scalar1=1.0 / 256, scalar2=None, op0=ALU.mult),
                  r=['kms'], w=['kms'])
            S.add('dve', lambda e: e.tensor_copy(out=kmhi[0:64, :], in_=kms[0:64, :]), r=['kms'], w=['kmhi'])
            S.add('dve', lambda e: e.tensor_copy(out=kmh[0:64, :], in_=kmhi[0:64, :]), r=['kmhi'], w=['kmh'])
            S.add('dve', lambda e: e.tensor_tensor(out=kmh[0:64, :], in0=kms[0:64, :], in1=kmh[0:64, :], op=ALU.subtract),
                  r=['kms', 'kmh'], w=['kmh'])
            S.add('dve', lambda e: e.tensor_copy(out=kmlo[0:64, :], in_=kmh[0:64, :]), r=['kmh'], w=['kmlo'])
            S.add('dve', lambda e: e.memset(gsb, -1e30), w=['gsb'])
            S.add('dve', lambda e: e.memset(selpad[:, 0:64], 0.0), w=['selpad'])

            def moba_prep_stages(qg):
                sl = qg % 2
                src, gg = qg // 4, qg % 4
                st = []

                def s_load():
                    S.add('sp', lambda e: e.dma_start(
                        out=Qp[sl][0:64, :], in_=gsl(0, src, 0, 'sp', (gg * 512, (gg + 1) * 512))),
                          r=[('Lq', 0)], w=[('Qp', sl)], stream=('q', sl))
                st.append(s_load)
                for ci in range(4):
                    c = qg * 4 + ci
                    own = c // 2

                    def s_gate(ci=ci, own=own):
                        if own <= 3:
                            S.add('dve', lambda e: e.memset(selpad[:, 64:128], NEGM), w=['selpad'])
                            S.add('dve', lambda e: e.memset(selpad[:, 64:64 + own + 1], 0.0), w=['selpad'])
                        else:
                            for m, km in enumerate((kmhi, kmlo)):
                                S.add('pe', lambda e, km=km, m=m: e.matmul(
                                    banks[7][:, 0:64], lhsT=Qp[sl][0:64, ci * 128:(ci + 1) * 128], rhs=km[0:64, :],
                                    start=(m == 0), stop=(m == 1)),
                                      r=[('Qp', sl), 'kmhi', 'kmlo'], w=[('bank', 7)])
                            S.add('dve', lambda e: e.tensor_copy(out=gsb[:, 0:own], in_=banks[7][:, 0:own]),
                                  r=[('bank', 7)], w=['gsb'])
                            S.add('dve', lambda e: e.max(out=m8, in_=gsb), r=['gsb'], w=['m8'])
                            S.add('dve', lambda e: e.tensor_scalar(out=selpad[:, 64:128], in0=gsb, scalar1=m8[:, 2:3],
                                                                   scalar2=NEGM, op0=ALU.is_lt, op1=ALU.mult),
                                  r=['gsb', 'm8'], w=['selpad'])
                            S.add('dve', lambda e: e.memset(selpad[:, 64 + own:64 + own + 1], 0.0), w=['selpad'])

                    def s_tr(ci=ci):
                        S.add('pe', lambda e: e.matmul(banks[7][:, 0:128], lhsT=selpad, rhs=ident[:, :], start=True, stop=True),
                              r=['selpad', 'ident'], w=[('bank', 7)])
                        S.add('dve', lambda e: e.tensor_copy(out=Qp[sl][64:128, ci * 128:(ci + 1) * 128],
                                                             in_=banks[7][64:128, 0:128]),
                              r=[('bank', 7)], w=[('Qp', sl)])
                    st.append(s_gate)
                    st.append(s_tr)
                return st

            msteps = []
            for qg in range(T // 512):
                nb = 2 * qg + 2
                for n in range(nb):
                    msteps.append((qg, n, n == 0, n == nb - 1))
            NM = len(msteps)

            def s_mm(i):
                qg, n, first, last = msteps[i]
                zs = i % 3
                diag = n >= 2 * qg
                for j in range(2):
                    kt = 2 * n + j
                    S.add('pe', lambda e, kt=kt, zs=zs, j=j, qg=qg, diag=diag: e.matmul(
                        banks[2 * zs + j], lhsT=big2[:, kt * 128:(kt + 1) * 128], rhs=Qp[qg % 2], start=True, stop=(not diag)),
                          r=KT + ['kTind', ('Qp', qg % 2)], w=[('Z', zs)])
                    if diag:
                        dj = (n - 2 * qg) * 2 + j
                        S.add('pe', lambda e, zs=zs, j=j, dj=dj: e.matmul(
                            banks[2 * zs + j], lhsT=ident[:, :], rhs=mobam[:, dj, :], start=False, stop=True),
                              r=['mobam', 'ident'], w=[('Z', zs)])

            def m_act(i):
                zs = i % 3
                as_ = i % 2
                S.add('act', lambda e, zs=zs, as_=as_: e.activation(out=Abuf[as_], in_=zts[zs][:, :], func=AF.Exp, scale=0.125),
                      r=[('Z', zs)], w=[('A', as_)])

            def m_pv(i):
                qg, n, first, last = msteps[i]
                as_ = i % 2
                for j in range(2):
                    kt = 2 * n + j
                    S.add('pe', lambda e, kt=kt, j=j, as_=as_, first=first, last=last: e.matmul(
                        banks[6][0:65, :], lhsT=Vsb[:, kt, 0:65], rhs=Abuf[as_][:, j * 512:(j + 1) * 512],
                        start=(first and j == 0), stop=(last and j == 1)),
                          r=[('A', as_), 'Vsb'], w=['Y'])
                if last:
                    S.add('dve', lambda e: e.reciprocal(out=ystg32[64:65, :], in_=banks[6][64:65, :]), r=['Y'], w=['rec'])
                    S.add('pe', lambda e: e.matmul(banks[7][0:64, :], lhsT=onesf[64:65, 0:64], rhs=ystg32[64:65, :],
                                                    start=True, stop=True), r=['rec', 'onesf'], w=[('bank', 7)])
                    S.add('act', lambda e: e.activation(out=ybc[0:64, :], in_=banks[7][0:64, :], func=AF.Copy),
                          r=[('bank', 7)], w=['ybc'])
                    S.add('dve', lambda e: e.tensor_tensor(out=ystgb[0:64, :], in0=banks[6][0:64, :], in1=ybc[0:64, :], op=ALU.mult),
                          r=['Y', 'ybc'], w=['ystgb'])
                    S.add('sp', lambda e, qg=qg: e.dma_start(out=B_d[0].ap()[:, qg * 512:(qg + 1) * 512], in_=ystgb[0:64, :]),
                          r=['ystgb'], w=[('B', 0)], stream=('bout', 0))

            for f_ in moba_prep_stages(0):
                f_()
            pending = moba_prep_stages(1)
            NQG = T // 512

            def issue_s(k):
                nonlocal pending
                if msteps[k][2] and msteps[k][0] > 0:
                    for f_ in pending:
                        f_()
                    nq = msteps[k][0] + 1
                    pending = moba_prep_stages(nq) if nq < NQG else []
                s_mm(k)

            issue_s(0)
            if NM > 1:
                issue_s(1)
            for i in range(NM):
                if i + 2 < NM:
                    issue_s(i + 2)
                m_act(i)
                m_pv(i)
                if pending:
                    pending.pop(0)()

        S.barrier()
        if debug and phases >= 2:
            dma('sp', dbg_B[0:64, :], B_d[0].ap()[:, :], r=[('B', 0)], stream='dbg')
            dma('sp', dbg_B[64:128, :], B_d[1].ap()[:, :], r=[('B', 1)], stream='dbg')

        if phases >= 3:
            S.add('pool', lambda e: e.collective_compute("AllGather", ALU.bypass, replica_groups=[list(range(NCORE))],
                                                         ins=[B_d[0].ap().opt()], outs=[GB_d[0].ap().opt()]),
                  r=[('B', 0)], w=[('GB', 0)], stream=('cc', 4), inc=None)
            S.barrier()
            for m in (1, 0):
                S.add('sp', lambda e, m=m: e.dma_start(out=LB_d[m].ap()[:, :], in_=GB_d[m].ap()[:, bass.ds(pid['sp'] * TL, TL)]),
                      r=[('GB', m)], w=['LB'], stream=('lq', 3))
            S.barrier()
            wa_s = big3[:, 0:4096].rearrange("p (k n) -> p k n", n=D)
            wb_s = big3[:, 4096:8192].rearrange("p (k n) -> p k n", n=D)
            wo_s = big3[:, 8192:16384].rearrange("p (k n) -> p k n", n=D)
            for kk in range(4):
                S.add('pool', lambda e, kk=kk: e.dma_start(out=wa_s[:, kk, :], in_=wa_d[kk, :, :]), w=['wa'], stream='w3')
                S.add('pool', lambda e, kk=kk: e.dma_start(out=wb_s[:, kk, :], in_=wb_d[kk, :, :]), w=['wb'], stream='w3')
            for kk in range(8):
                S.add('pool', lambda e, kk=kk: e.dma_start(out=wo_s[:, kk, :], in_=wo_d[kk, :, :]), w=['wo'], stream='w3')
            gfin = w32[:, 3072:4096]
            S.add('sp', lambda e: e.dma_start(out=gfin, in_=gfin_d[0:1, :].broadcast_to([128, D])), w=['gfin'], stream='gfin')
            S.barrier()
            for g in range(NG):
                tiles = [g * 4 + i for i in range(4)]
                rmsnorm_group(tiles, 8)
                for m in range(2):
                    for kk in range(4):
                        for hh in range(2):
                            hd = kk * 2 + hh
                            S.add('sp', lambda e, m=m, kk=kk, hh=hh, hd=hd, g=g: e.dma_start(
                                out=actT[hh * 64:(hh + 1) * 64, m * 4 + kk, :],
                                in_=LB_d[m].ap()[hd * 64:(hd + 1) * 64, g * 512:(g + 1) * 512]),
                                  r=['LB'], w=[('actT', m * 4 + kk)], stream=('yl', m))
                mT = actT[:, 8:16, :]
                for j in range(8):
                    for m in range(2):
                        cc = 24 + m * 8 + j
                        sc = wcnt['c'] % 3
                        wcnt['c'] += 1
                        S.add('pool', lambda e, cc=cc, sc=sc: e.dma_start(out=wgs[:, sc, :, :], in_=win_d[cc, :, :, :]),
                              w=[('wgs', sc)], stream=('wg', sc))
                        for k in range(8):
                            S.add('pe', lambda e, k=k, sc=sc, m=m: e.matmul(banks[4 + m], lhsT=wgs[:, sc, k, :], rhs=xnT[:, k, :],
                                                                           start=(k == 0), stop=(k == 7)),
                                  r=[('wgs', sc), ('xnT', k)], w=[('bank', 4 + m)])
                        gt = [sgt, xf32][m]
                        S.add('act', lambda e, m=m, j=j, gt=gt: e.activation(out=gt, in_=banks[4 + m], func=AF.Sigmoid,
                                                                            bias=bgate[:, m * 8 + j:m * 8 + j + 1]),
                              r=[('bank', 4 + m), 'bgate'], w=[('gt', m)])
                        ws = [wa_s, wb_s][m]
                        for kk in range(4):
                            S.add('pe', lambda e, kk=kk, m=m, j=j, ws=ws: e.matmul(
                                banks[6 + m], lhsT=ws[:, kk, j * 128:(j + 1) * 128], rhs=actT[:, m * 4 + kk, :],
                                start=(kk == 0), stop=(kk == 3)),
                                  r=['wa', 'wb', ('actT', m * 4 + kk)], w=[('bank', 6 + m)])
                    S.add('dve', lambda e: e.tensor_tensor(out=rt1, in0=sgt, in1=banks[6], op=ALU.mult),
                          r=[('gt', 0), ('bank', 6)], w=['rt1'])
                    S.add('dve', lambda e: e.tensor_tensor(out=rt2, in0=xf32, in1=banks[7], op=ALU.mult),
                          r=[('gt', 1), ('bank', 7)], w=['rt2'])
                    S.add('dve', lambda e, j=j: e.tensor_tensor(out=mT[:, j, :], in0=rt1, in1=rt2, op=ALU.add),
                          r=['rt1', 'rt2'], w=[('actT', 8 + j)])
                for half in range(2):
                    for ts in range(4):
                        for j in range(8):
                            S.add('pe', lambda e, ts=ts, j=j, half=half: e.matmul(
                                banks[ts], lhsT=mT[:, j, ts * 128:(ts + 1) * 128], rhs=wo_s[:, j, half * 512:(half + 1) * 512],
                                start=(j == 0), stop=(j == 7)),
                                  r=[('actT', 8 + j), 'wo'], w=[('bank', ts)])
                    for ts in range(4):
                        tt = tiles[ts]
                        S.add('dve', lambda e, ts=ts, tt=tt, half=half: e.tensor_tensor(
                            out=h_sb[:, tt, half * 512:(half + 1) * 512], in0=banks[ts],
                            in1=h_sb[:, tt, half * 512:(half + 1) * 512], op=ALU.add),
                              r=[('bank', ts), ('h', tt)], w=[('h', tt)])
            S.barrier()
            rmsnorm_all(16)
            ffn_all(1)
            for g in range(NG):
                tiles = [g * 4 + i for i in range(4)]
                for i, tt in enumerate(tiles):
                    col = norm_ctr[0]
                    norm_ctr[0] += 1
                    S.add('act', lambda e, tt=tt, col=col: e.activation(out=junk[:, :], in_=h_sb[:, tt, :], func=AF.Square,
                                                                         accum_out=ss[:, col:col + 1]),
                          r=[('h', tt), 'ss'], w=[('ss', col), 'junk'])
                    S.add('act', lambda e, col=col: e.activation(out=rstd[:, col:col + 1], in_=ss[:, col:col + 1], func=AF.Sqrt,
                                                                 scale=1.0 / D, bias=epsc[:, 0:1]),
                          r=[('ss', col), 'epsc'], w=[('rstd0', col)])
                    S.add('dve', lambda e, col=col: e.reciprocal(out=rstd[:, col:col + 1], in_=rstd[:, col:col + 1]),
                          r=[('rstd0', col)], w=[('rstd', col)])
                    S.add('dve', lambda e, tt=tt, col=col: e.scalar_tensor_tensor(
                        out=h_sb[:, tt, :], in0=h_sb[:, tt, :], scalar=rstd[:, col:col + 1], in1=gfin, op0=ALU.mult, op1=ALU.mult),
                          r=[('h', tt), ('rstd', col), 'gfin'], w=[('h', tt)])
                    dma('sp', y_d[tt * 128:(tt + 1) * 128, :], h_sb[:, tt, :], r=[('h', tt)], w=['yout'], stream='yout')
        else:
            for tt in range(NTL):
                dma('sp', y_d[tt * 128:(tt + 1) * 128, :], h_sb[:, tt, :], r=[('h', tt)], w=['yout'], stream='yout')
        S.barrier()
        S.add('sp', lambda e: e.nop(), r=['yout'])

        keys = S.finalize()
        semh = {}
        for k in keys:
            nm = "s_" + "_".join(str(x) for x in (k[1] if isinstance(k[1], tuple) else (k[1],)))
            semh[k] = es.enter_context(nc.semaphore(nm))
        block = es.enter_context(nc.Block())

        def section(engname):
            def body(e):
                if engname in ('sp', 'pool'):
                    pid[engname] = e.partition_id()
                S.emit_engine(engname, e, semh)
            return body

        block.sync(section('sp'))
        block.tensor(section('pe'))
        block.scalar(section('act'))
        block.vector(section('dve'))
        block.gpsimd(section('pool'))
    return nc


def _prep_inputs(x, positions, ffn1_norm, ffn1_w_gate, ffn1_w_up, ffn1_w_down, mix_norm, w_in, b_gate,
                 w_branch_moba, w_branch_sb, w_out, ffn2_norm, ffn2_w_gate, ffn2_w_up, ffn2_w_down, final_norm):
    f = lambda a: np.ascontiguousarray(np.asarray(a, dtype=np.float32))
    shared = {}

    def gu(w):
        w = f(w).reshape(8, 128, NF, 128)
        return np.ascontiguousarray(w.transpose(2, 1, 0, 3))

    shared['wg1'] = gu(ffn1_w_gate[0]); shared['wu1'] = gu(ffn1_w_up[0]); shared['wd1'] = f(ffn1_w_down[0]).reshape(NF, 128, D)
    shared['wg2'] = gu(ffn2_w_gate[0]); shared['wu2'] = gu(ffn2_w_up[0]); shared['wd2'] = f(ffn2_w_down[0]).reshape(NF, 128, D)
    wi = f(w_in[0])
    qa, ka, va, qb, kb, vb = [wi[:, i * 512:(i + 1) * 512] for i in range(6)]
    cols = []
    for h in range(8):
        for part in (qa, ka, va, qb, kb, vb):
            cols.append(part[:, h * 64:(h + 1) * 64])
    cols.append(wi[:, 3072:5120])
    wr = np.concatenate(cols, axis=1)
    wr = wr.reshape(8, 128, 40, 128)
    shared['win'] = np.ascontiguousarray(wr.transpose(2, 1, 0, 3))
    shared['wa'] = f(w_branch_moba[0]).reshape(4, 128, D)
    shared['wb'] = f(w_branch_sb[0]).reshape(4, 128, D)
    shared['wo'] = f(w_out[0]).reshape(8, 128, D)
    gn = np.concatenate([f(ffn1_norm[0]).reshape(8, 128).T, f(mix_norm[0]).reshape(8, 128).T,
                         f(ffn2_norm[0]).reshape(8, 128).T], axis=1)
    shared['gains'] = np.ascontiguousarray(gn)
    shared['bgate'] = np.ascontiguousarray(f(b_gate[0]).reshape(16, 128).T)
    shared['gfin'] = f(final_norm).reshape(1, D)
    shared.update(host_consts())
    xs = f(x)[0]
    ps_ = np.asarray(positions).astype(np.int32)[0]
    in_maps = []
    for c in range(NCORE):
        m = dict(shared)
        m['x'] = np.ascontiguousarray(xs[c * TL:(c + 1) * TL])
        m['pos'] = np.ascontiguousarray(ps_[c * TL:(c + 1) * TL]).reshape(1, TL)
        in_maps.append(m)
    return in_maps


_NC_CACHE = {}


def kernel(**inputs):
    debug = bool(int(os.environ.get("KDEBUG", "0")))
    phases = int(os.environ.get("KPHASES", "3"))
    key = (debug, phases)
    if key not in _NC_CACHE:
        _NC_CACHE[key] = build_program(debug=debug, phases=phases)
    nc = _NC_CACHE[key]
    in_maps = _prep_inputs(**inputs)
    res = run_bass_kernel_spmd(nc, in_maps, core_ids=list(range(NCORE)))
    out = np.concatenate([np.asarray(r["y"], dtype=np.float32) for r in res.results], axis=0)[None]
    if debug:
        kernel.last_results = res.results
    return out
```

```python
import os
import contextlib
import numpy as np
import ml_dtypes
import concourse.bass as bass
import concourse.mybir as mybir
from concourse.bass_utils import run_bass_kernel_spmd

F32 = mybir.dt.float32
BF16 = mybir.dt.bfloat16
I32 = mybir.dt.int32
AF = mybir.ActivationFunctionType
ALU = mybir.AluOpType
AX = mybir.AxisListType

NCORE = 8
T = 16384
D = 1024
TL = T // NCORE
NTL = TL // 128
NG = TL // 512
FF = 2816
NF = FF // 128
EPS = 1e-6
NEGM = -30000.0
PI = float(np.pi)


class Sched:
    def __init__(self):
        self.ops = []
        self.lastw = {}
        self.readers = {}
        self.stream_cnt = {}
        self.barriers = []
        self.last_on_eng = {}

    def add(self, eng, emit, r=(), w=(), stream=None, inc=16):
        idx = len(self.ops)
        deps = {}
        for t in r:
            x = self.lastw.get(t)
            if x is not None:
                deps[x] = 'raw'
        for t in w:
            x = self.lastw.get(t)
            if x is not None and x not in deps:
                deps[x] = 'waw'
            for y in self.readers.get(t, ()):
                if y not in deps:
                    deps[y] = 'war'
        op = dict(eng=eng, emit=emit, deps=deps, stream=stream, inc=inc, idx=idx,
                  nbar=len(self.barriers), sig=None, signaler=False)
        if stream is not None:
            c = self.stream_cnt.get(stream, 0) + (inc if inc is not None else 1)
            self.stream_cnt[stream] = c
            op['sig'] = (('s', stream), c)
        else:
            self.last_on_eng[eng] = idx
        self.ops.append(op)
        for t in r:
            self.readers.setdefault(t, []).append(idx)
        for t in w:
            self.lastw[t] = idx
            self.readers[t] = []
        return idx

    def barrier(self, skip_cc=False):
        st = {k: v for k, v in self.stream_cnt.items() if not (skip_cc and isinstance(k, tuple) and k[0] == 'cc')}
        self.barriers.append(dict(compute_last=dict(self.last_on_eng), streams=st))

    @staticmethod
    def _needs_wait(o, w, kind):
        if w['stream'] is not None:
            return True
        if o['stream'] is None and w['eng'] == o['eng']:
            if o['eng'] == 'pe':
                return False
            return kind == 'raw'
        return True

    def finalize(self):
        ops = self.ops
        for o in ops:
            for wi, kind in o['deps'].items():
                w = ops[wi]
                if self._needs_wait(o, w, kind) and w['stream'] is None:
                    w['signaler'] = True
        for b in self.barriers:
            for e, wi in b['compute_last'].items():
                ops[wi]['signaler'] = True
        cnt = {}
        for o in ops:
            if o['stream'] is None and o['signaler']:
                c = cnt.get(o['eng'], 0) + 1
                cnt[o['eng']] = c
                o['sig'] = (('e', o['eng']), c)
        sems = set()
        for o in ops:
            if o['sig'] is not None:
                sems.add(o['sig'][0])
        return sorted(sems, key=str)

    def emit_engine(self, eng, e, semh):
        ops = self.ops
        waited = {}

        def wait(key, c):
            if waited.get(key, 0) < c:
                e.wait_ge(semh[key], c)
                waited[key] = c

        nbar_done = 0
        for o in ops:
            if o['eng'] != eng:
                continue
            while nbar_done < o['nbar']:
                b = self.barriers[nbar_done]
                for en, wi in b['compute_last'].items():
                    if en == eng:
                        continue
                    k, c = ops[wi]['sig']
                    wait(k, c)
                for st, c in b['streams'].items():
                    wait(('s', st), c)
                nbar_done += 1
            for wi, kind in o['deps'].items():
                w = ops[wi]
                if self._needs_wait(o, w, kind):
                    k, c = w['sig']
                    wait(k, c)
            ins = o['emit'](e)
            if o['sig'] is not None:
                k, c = o['sig']
                if o['inc'] is None:
                    ins.then_inc(semh[k])
                elif o['stream'] is not None:
                    ins.then_inc(semh[k], o['inc'])
                else:
                    ins.then_inc(semh[k], 1)


def _bf(a):
    return np.ascontiguousarray(a.astype(ml_dtypes.bfloat16))


def host_consts():
    c = {}
    c['ident'] = _bf(np.eye(128, dtype=np.float32))
    j = np.arange(128)[:, None]
    s = np.arange(128)[None, :]
    c['negtri'] = _bf(np.where(j >= s, -1.0, 0.0))
    c['negones'] = _bf(-np.ones((128, 128), np.float32))
    Rm = np.zeros((128, 128), np.float32)
    for b in (0, 64):
        for i in range(8):
            Rm[b + i + 8, b + i] = -1.0
            Rm[b + i, b + i + 8] = 1.0
    c['rot32'] = Rm
    half = 8
    inv_freq = (np.float32(500000.0) ** (-np.arange(half, dtype=np.float32) * np.float32(2.0) / np.float32(16))).astype(np.float32)
    invf = np.zeros((128, 1), np.float32)
    for r in range(128):
        if (r % 64) < 16:
            invf[r, 0] = inv_freq[(r % 64) % 8]
    c['invf'] = invf
    scl = np.ones((128, 2), np.float32)
    scl[64:128, 0] = 0.125
    c['scl'] = scl
    p = np.arange(128)[:, None, None]
    i = np.arange(4)[None, :, None]
    t = np.arange(512)[None, None, :]
    valid = (128 * i + p) < t
    c['sbm01'] = _bf(np.where(valid, 1.0, 0.0))
    c['sbneg'] = _bf(np.where(valid, 0.0, NEGM))
    kp = 128 * i + p
    kb = kp // 256
    qb = t // 256
    mm = np.where(kb == qb, np.where(kp <= t, 0.0, NEGM), np.where(kb < qb, 0.0, NEGM))
    c['mobam'] = _bf(mm)
    n = np.arange(64)[:, None]
    ss = np.arange(T)[None, :]
    c['blockind'] = _bf(np.where((ss // 256) == n, 1.0, 0.0))
    c['ones_f32'] = np.ones((128, 64), np.float32)
    return c


class _PidProxy:
    def __init__(self, e):
        self.e = e

    def __mul__(self, k):
        return self.e.partition_id() * k


def build_program(debug=False, phases=3):
    nc = bass.Bass("TRN2", target_bir_lowering=False)
    S = Sched()

    def din(name, shape, dt=F32):
        return nc.dram_tensor(name, list(shape), dt, kind="ExternalInput").ap()

    x_d = din("x", [TL, D])
    pos_d = din("pos", [1, TL], I32)
    wg_d = [din(f"wg{i}", [NF, 128, 8, 128]) for i in (1, 2)]
    wu_d = [din(f"wu{i}", [NF, 128, 8, 128]) for i in (1, 2)]
    wd_d = [din(f"wd{i}", [NF, 128, D]) for i in (1, 2)]
    win_d = din("win", [40, 128, 8, 128])
    wa_d = din("wa", [4, 128, D])
    wb_d = din("wb", [4, 128, D])
    wo_d = din("wo", [8, 128, D])
    gains_d = din("gains", [128, 24])
    bgate_d = din("bgate", [128, 16])
    gfin_d = din("gfin", [1, D])
    ident_d = din("ident", [128, 128], BF16)
    negtri_d = din("negtri", [128, 128], BF16)
    negones_d = din("negones", [128, 128], BF16)
    rot32_d = din("rot32", [128, 128])
    invf_d = din("invf", [128, 1])
    scl_d = din("scl", [128, 2])
    sbm01_d = din("sbm01", [128, 4, 512], BF16)
    sbneg_d = din("sbneg", [128, 4, 512], BF16)
    mobam_d = din("mobam", [128, 4, 512], BF16)
    blockind_d = din("blockind", [64, T], BF16)
    onesf_d = din("ones_f32", [128, 64])
    y_d = nc.dram_tensor("y", [TL, D], F32, kind="ExternalOutput").ap()
    if debug:
        dbg_h1 = nc.dram_tensor("dbg_h1", [TL, D], F32, kind="ExternalOutput").ap()
        dbg_B = nc.dram_tensor("dbg_B", [128, T], BF16, kind="ExternalOutput").ap()
        dbg_q = nc.dram_tensor("dbg_q", [6 * 64, T], BF16, kind="ExternalOutput").ap()

    A_d = [[nc.dram_tensor(f"xa{i}_{hf}", [1024, TL // 2], BF16) for hf in range(2)] for i in range(3)]
    G_d = [[nc.dram_tensor(f"xg{i}_{hf}", [NCORE * 1024, TL // 2], BF16) for hf in range(2)] for i in range(3)]
    B_d = [nc.dram_tensor(f"xb{m}", [64, T], BF16) for m in range(2)]
    GB_d = [nc.dram_tensor(f"xgb{m}", [NCORE * 64, T], BF16) for m in range(2)]
    Lq_d = [nc.dram_tensor(f"xl{i}", [NCORE * 128, TL], BF16) for i in range(3)]
    LB_d = [nc.dram_tensor(f"xlb{m}", [NCORE * 64, TL], BF16) for m in range(2)]

    es = contextlib.ExitStack()
    with es:
        def sb(name, shape, dt):
            return es.enter_context(nc.sbuf_tensor(name, list(shape), dt))

        def ps(name, shape, dt):
            return es.enter_context(nc.psum_tensor(name, list(shape), dt))

        h_sb = sb("h_sb", [128, NTL, D], F32)
        big2 = sb("big2", [128, T], BF16)
        big3 = sb("big3", [128, T], BF16)
        w32 = sb("w32", [128, 4096], F32)
        wgs = sb("wgs", [128, 4, 8, 128], BF16)
        wus = sb("wus", [128, 4, 8, 128], BF16)
        wds = sb("wds", [128, 4, D], BF16)
        xn_s = sb("xn_s", [128, 4, D], BF16)
        stg = sb("stg", [128, 2, 512], BF16)
        junk = sb("junk", [128, D], BF16)
        ident = sb("identb", [128, 128], BF16)
        negtri = sb("negtrib", [128, 128], BF16)
        negones = sb("negonesb", [128, 128], BF16)
        rot32 = sb("rot32b", [128, 128], F32)
        invf = sb("invfb", [128, 1], F32)
        scl = sb("sclb", [128, 2], F32)
        gains = sb("gainsb", [128, 24], F32)
        bgate = sb("bgateb", [128, 16], F32)
        sbm01 = sb("sbm01b", [128, 4, 512], BF16)
        sbneg = sb("sbnegb", [128, 4, 512], BF16)
        mobam = sb("mobamb", [128, 4, 512], BF16)
        onesf = sb("onesfb", [128, 64], F32)
        ss = sb("ssb", [128, 96], F32)
        rstd = sb("rstdb", [128, 96], F32)
        small = sb("smallb", [128, 512], F32)
        posi = sb("posib", [128, 512], I32)
        epsc = sb("epscb", [128, 1], F32)

        psA = ps("psA", [128, 1024], F32)
        psB = ps("psB", [128, 1024], F32)
        psC = ps("psC", [128, 1024], F32)
        psD = ps("psD", [128, 1024], F32)
        banks = [psA[:, 0:512], psA[:, 512:1024], psB[:, 0:512], psB[:, 512:1024],
                 psC[:, 0:512], psC[:, 512:1024], psD[:, 0:512], psD[:, 512:1024]]

        xnT = big2[:, 0:4096].rearrange("p (k n) -> p k n", n=512)
        actT = big2[:, 4096:4096 + NF * 512].rearrange("p (k n) -> p k n", n=512)
        sgt = w32[:, 0:512]
        xf32 = w32[:, 512:1024]
        rt1 = w32[:, 1024:1536]
        rt2 = w32[:, 1536:2048]
        cfull = w32[:, 2048:2560]
        sfull = w32[:, 2560:3072]
        posf = w32[:, 3072:3584]
        angt = w32[:, 3584:4096]

        def dma(eng, out, in_, r=(), w=(), stream=None):
            return S.add(eng, lambda e, out=out, in_=in_: e.dma_start(out=out, in_=in_), r=r, w=w, stream=stream)

        for (dst, src, nm) in [(ident, ident_d, 'ident'), (negtri, negtri_d, 'negtri'), (negones, negones_d, 'negones'),
                               (rot32, rot32_d, 'rot32'), (invf, invf_d, 'invf'), (scl, scl_d, 'scl'),
                               (gains, gains_d, 'gains'), (bgate, bgate_d, 'bgate'), (onesf, onesf_d, 'onesf')]:
            dma('sp', dst[:, :], src[:, :], w=[nm], stream='const')
        dma('sp', sbm01[:, :, :], sbm01_d[:, :, :], w=['sbm01'], stream='const')
        dma('sp', sbneg[:, :, :], sbneg_d[:, :, :], w=['sbneg'], stream='const')
        dma('sp', mobam[:, :, :], mobam_d[:, :, :], w=['mobam'], stream='const')
        S.add('dve', lambda e: e.memset(ss[:, :], 0.0), w=['ss'])
        S.add('dve', lambda e: e.memset(epsc[:, :], EPS), w=['epsc'])
        S.barrier()

        norm_ctr = [0]

        def rmsnorm_group(tiles, gain_col0):
            cols = []
            for i, tt in enumerate(tiles):
                col = norm_ctr[0]
                norm_ctr[0] += 1
                cols.append(col)
                S.add('act', lambda e, tt=tt, col=col: e.activation(out=junk[:, :], in_=h_sb[:, tt, :], func=AF.Square,
                                                                     accum_out=ss[:, col:col + 1]),
                      r=[('h', tt), 'ss'], w=[('ss', col), 'junk'])
            c0 = cols[0]
            S.add('act', lambda e, c0=c0: e.activation(out=rstd[:, c0:c0 + 4], in_=ss[:, c0:c0 + 4], func=AF.Sqrt,
                                                       scale=1.0 / D, bias=epsc[:, 0:1]),
                  r=[('ss', c) for c in cols] + ['epsc'], w=[('rstd0', c0)])
            S.add('dve', lambda e, c0=c0: e.reciprocal(out=rstd[:, c0:c0 + 4], in_=rstd[:, c0:c0 + 4]),
                  r=[('rstd0', c0)], w=[('rstd', c) for c in cols])
            xn4 = xn_s
            for i, tt in enumerate(tiles):
                S.add('dve', lambda e, tt=tt, col=cols[i], i=i: e.tensor_scalar(
                    out=xn4[:, i, :], in0=h_sb[:, tt, :], scalar1=rstd[:, col:col + 1], scalar2=None, op0=ALU.mult),
                      r=[('h', tt), ('rstd', cols[i])], w=[('xn4', i)])
            for rnd in range(4):
                for kk in range(2):
                    k = rnd * 2 + kk
                    bk = 6 + kk
                    for i in range(4):
                        S.add('pe', lambda e, k=k, bk=bk, i=i: e.matmul(
                            banks[bk][:, i * 128:(i + 1) * 128], lhsT=xn4[:, i, k * 128:(k + 1) * 128], rhs=ident[:, :],
                            start=True, stop=True),
                              r=[('xn4', i), 'ident'], w=[('bank', bk)])
                    gc = gain_col0 + k
                    eng = 'dve' if kk == 0 else 'act'
                    if eng == 'dve':
                        S.add('dve', lambda e, k=k, bk=bk, gc=gc: e.tensor_scalar(
                            out=xnT[:, k, :], in0=banks[bk], scalar1=gains[:, gc:gc + 1], scalar2=None, op0=ALU.mult),
                              r=[('bank', bk), 'gains'], w=[('xnT', k)])
                    else:
                        S.add('act', lambda e, k=k, bk=bk, gc=gc: e.activation(
                            out=xnT[:, k, :], in_=banks[bk], func=AF.Copy, scale=gains[:, gc:gc + 1]),
                              r=[('bank', bk), 'gains'], w=[('xnT', k)])

        wcnt = {'g': 0, 'u': 0, 'd': 0, 'c': 0}

        xnTa = big2[:, :].rearrange("p (k n) -> p k n", n=TL)
        actTp = [w32[:, 1024 + i * 512:1024 + (i + 1) * 512].bitcast(BF16).rearrange("p (j n) -> p j n", n=512) for i in range(2)]
        sgt2 = [w32[:, 0:512], w32[:, 512:1024]]

        def rmsnorm_all(gain_col0, groups=tuple(range(NG))):
            for g in groups:
                tiles = [g * 4 + i for i in range(4)]
                cols = []
                for i, tt in enumerate(tiles):
                    col = norm_ctr[0]
                    norm_ctr[0] += 1
                    cols.append(col)
                    S.add('act', lambda e, tt=tt, col=col: e.activation(out=junk[:, :], in_=h_sb[:, tt, :], func=AF.Square,
                                                                         accum_out=ss[:, col:col + 1]),
                          r=[('h', tt), 'ss'], w=[('ss', col), 'junk'])
                c0 = cols[0]
                S.add('act', lambda e, c0=c0: e.activation(out=rstd[:, c0:c0 + 4], in_=ss[:, c0:c0 + 4], func=AF.Sqrt,
                                                           scale=1.0 / D, bias=epsc[:, 0:1]),
                      r=[('ss', c) for c in cols] + ['epsc'], w=[('rstd0', c0)])
                S.add('dve', lambda e, c0=c0: e.reciprocal(out=rstd[:, c0:c0 + 4], in_=rstd[:, c0:c0 + 4]),
                      r=[('rstd0', c0)], w=[('rstd', c) for c in cols])
                for i, tt in enumerate(tiles):
                    S.add('dve', lambda e, tt=tt, col=cols[i], i=i: e.tensor_scalar(
                        out=xn_s[:, i, :], in0=h_sb[:, tt, :], scalar1=rstd[:, col:col + 1], scalar2=None, op0=ALU.mult),
                          r=[('h', tt), ('rstd', cols[i])], w=[('xn4', i)])
                for rnd in range(4):
                    for kk in range(2):
                        k = rnd * 2 + kk
                        bk = 6 + kk
                        for i in range(4):
                            S.add('pe', lambda e, k=k, bk=bk, i=i: e.matmul(
                                banks[bk][:, i * 128:(i + 1) * 128], lhsT=xn_s[:, i, k * 128:(k + 1) * 128], rhs=ident[:, :],
                                start=True, stop=True),
                                  r=[('xn4', i), 'ident'], w=[('bank', bk)])
                        gc = gain_col0 + k
                        if kk == 0:
                            S.add('dve', lambda e, k=k, bk=bk, gc=gc, g=g: e.tensor_scalar(
                                out=xnTa[:, k, g * 512:(g + 1) * 512], in0=banks[bk], scalar1=gains[:, gc:gc + 1], scalar2=None,
                                op0=ALU.mult), r=[('bank', bk), 'gains'], w=[('xnTa', k, g)])
                        else:
                            S.add('act', lambda e, k=k, bk=bk, gc=gc, g=g: e.activation(
                                out=xnTa[:, k, g * 512:(g + 1) * 512], in_=banks[bk], func=AF.Copy, scale=gains[:, gc:gc + 1]),
                                  r=[('bank', bk), 'gains'], w=[('xnTa', k, g)])

        pool6 = [0, 1, 2, 3, 6, 7]
        dctr = [0]
        actp_ctr = [0]

        def ffn_all(widx, groups=tuple(range(NG))):
            for fp in range(NF // 2):
                slots = []
                for j in range(2):
                    f = 2 * fp + j
                    sg_ = wcnt['g'] % 4
                    wcnt['g'] += 1
                    slots.append(sg_)
                    S.add('pool', lambda e, f=f, sg_=sg_: e.dma_start(out=wgs[:, sg_, :, :], in_=wg_d[widx][f, :, :, :]),
                          w=[('wgs', sg_)], stream=('wg', sg_))
                    S.add('pool', lambda e, f=f, sg_=sg_: e.dma_start(out=wus[:, sg_, :, :], in_=wu_d[widx][f, :, :, :]),
                          w=[('wus', sg_)], stream=('wu', sg_))
                    S.add('pool', lambda e, f=f, sg_=sg_: e.dma_start(out=wds[:, sg_, :], in_=wd_d[widx][f, :, :]),
                          w=[('wds', sg_)], stream=('wd', sg_))
                for g in groups:
                    ap_ = actp_ctr[0] % 2
                    actp_ctr[0] += 1
                    for j in range(2):
                        sg_ = slots[j]
                        for k in range(8):
                            S.add('pe', lambda e, k=k, sg_=sg_, g=g: e.matmul(
                                banks[4], lhsT=wgs[:, sg_, k, :], rhs=xnTa[:, k, g * 512:(g + 1) * 512],
                                start=(k == 0), stop=(k == 7)),
                                  r=[('wgs', sg_), ('xnTa', k, g)], w=[('bank', 4)])
                        for k in range(8):
                            S.add('pe', lambda e, k=k, sg_=sg_, g=g: e.matmul(
                                banks[5], lhsT=wus[:, sg_, k, :], rhs=xnTa[:, k, g * 512:(g + 1) * 512],
                                start=(k == 0), stop=(k == 7)),
                                  r=[('wus', sg_), ('xnTa', k, g)], w=[('bank', 5)])
                        S.add('act', lambda e, j=j: e.activation(out=sgt2[j], in_=banks[4], func=AF.Silu),
                              r=[('bank', 4)], w=[('sgt2', j)])
                        S.add('dve', lambda e, j=j, ap_=ap_: e.tensor_tensor(out=actTp[ap_][:, j, :], in0=sgt2[j], in1=banks[5],
                                                                             op=ALU.mult),
                              r=[('sgt2', j), ('bank', 5)], w=[('actTp', ap_)])
                    for half in range(2):
                        for ts in range(4):
                            bk = pool6[dctr[0] % 6]
                            dctr[0] += 1
                            tt = g * 4 + ts
                            for j in range(2):
                                S.add('pe', lambda e, j=j, ap_=ap_, ts=ts, half=half, bk=bk, sg_=slots[j]: e.matmul(
                                    banks[bk], lhsT=actTp[ap_][:, j, ts * 128:(ts + 1) * 128],
                                    rhs=wds[:, sg_, half * 512:(half + 1) * 512], start=(j == 0), stop=(j == 1)),
                                      r=[('actTp', ap_), ('wds', slots[j])], w=[('bank', bk)])
                            S.add('dve', lambda e, bk=bk, tt=tt, half=half: e.scalar_tensor_tensor(
                                out=h_sb[:, tt, half * 512:(half + 1) * 512], in0=banks[bk], scalar=0.5,
                                in1=h_sb[:, tt, half * 512:(half + 1) * 512], op0=ALU.mult, op1=ALU.add),
                                  r=[('bank', bk), ('h', tt)], w=[('h', tt)])

        for tt in range(NTL):
            dma('sp', h_sb[:, tt, :], x_d[tt * 128:(tt + 1) * 128, :], w=[('h', tt)], stream=('x', tt))

        cs32 = big3[:, 0:8192].bitcast(F32).rearrange("p (a n) -> p a n", n=512)
        MAGIC = 12582912.0
        pctr = 0
        for hf in range(2):
            groups = (2 * hf, 2 * hf + 1)
            rmsnorm_all(0, groups)
            ffn_all(0, groups)
            S.barrier(skip_cc=True)
            if debug:
                for g in groups:
                    for tt in range(g * 4, g * 4 + 4):
                        dma('sp', dbg_h1[tt * 128:(tt + 1) * 128, :], h_sb[:, tt, :], r=[('h', tt)], stream='dbg')
            rmsnorm_all(8, groups)
            for g in groups:
                S.add('sp', lambda e, g=g: e.dma_start(out=posi[:, :], in_=pos_d[0:1, g * 512:(g + 1) * 512].broadcast_to([128, 512])),
                      w=['posi'], stream='pos')
                S.add('dve', lambda e: e.tensor_copy(out=posf, in_=posi[:, :]), r=['posi'], w=['posf'])
                S.add('dve', lambda e: e.tensor_scalar(out=angt, in0=posf, scalar1=invf[:, 0:1], scalar2=None, op0=ALU.mult),
                      r=['posf', 'invf'], w=['angt'])
                for which, shift in ((0, 0.0), (1, 0.5 * PI)):
                    dst = cs32[:, 2 * g + which, :]
                    if shift != 0.0:
                        S.add('dve', lambda e, shift=shift: e.tensor_scalar(out=rt2, in0=angt, scalar1=shift, scalar2=None, op0=ALU.add),
                              r=['angt'], w=['rt2'])
                        src_, srct = rt2, 'rt2'
                    else:
                        src_, srct = angt, 'angt'
                    S.add('dve', lambda e, src_=src_: e.tensor_scalar(out=rt1, in0=src_, scalar1=1.0 / (2 * PI), scalar2=MAGIC,
                                                                      op0=ALU.mult, op1=ALU.add), r=[srct], w=['rt1'])
                    S.add('dve', lambda e: e.tensor_scalar(out=rt1, in0=rt1, scalar1=-MAGIC, scalar2=None, op0=ALU.add),
                          r=['rt1'], w=['rt1'])
                    S.add('dve', lambda e, src_=src_: e.scalar_tensor_tensor(out=rt1, in0=rt1, scalar=-2 * PI, in1=src_,
                                                                             op0=ALU.mult, op1=ALU.add), r=['rt1', srct], w=['rt1'])
                    S.add('dve', lambda e: e.tensor_scalar(out=rt1, in0=rt1, scalar1=-PI, scalar2=PI, op0=ALU.max, op1=ALU.min),
                          r=['rt1'], w=['rt1'])
                    S.add('act', lambda e, dst=dst: e.activation(out=dst, in_=rt1, func=AF.Sin), r=['rt1'], w=[('cs', g, which)])
            for typ in (2, 1, 0):
                for hd in range(8):
                    cc = hd * 3 + typ
                    sc = wcnt['g'] % 4
                    wcnt['g'] += 1
                    S.add('pool', lambda e, cc=cc, sc=sc: e.dma_start(out=wgs[:, sc, :, :], in_=win_d[cc, :, :, :]),
                          w=[('wgs', sc)], stream=('wg', sc))
                    for gi, g in enumerate(groups):
                        bk = 6 + (pctr % 2)
                        so = pctr % 2
                        pctr += 1
                        for k in range(8):
                            S.add('pe', lambda e, k=k, sc=sc, bk=bk, g=g: e.matmul(
                                banks[bk], lhsT=wgs[:, sc, k, :], rhs=xnTa[:, k, g * 512:(g + 1) * 512], start=(k == 0), stop=(k == 7)),
                                  r=[('wgs', sc), ('xnTa', k, g)], w=[('bank', bk)])
                        if typ == 0:
                            S.add('act', lambda e, bk=bk: e.activation(out=xf32, in_=banks[bk], func=AF.Copy),
                                  r=[('bank', bk)], w=['xf32'])
                            S.add('pe', lambda e: e.matmul(banks[5], lhsT=rot32[:, :], rhs=xf32, start=True, stop=True),
                                  r=['xf32', 'rot32'], w=[('bank', 5)])
                            S.add('dve', lambda e, g=g: e.tensor_tensor(out=rt1, in0=xf32, in1=cs32[:, 2 * g + 1, :], op=ALU.mult),
                                  r=['xf32', ('cs', g, 1)], w=['rt1'])
                            S.add('dve', lambda e, g=g: e.tensor_tensor(out=rt2, in0=banks[5], in1=cs32[:, 2 * g, :], op=ALU.mult),
                                  r=[('bank', 5), ('cs', g, 0)], w=['rt2'])
                            S.add('dve', lambda e, so=so: e.tensor_tensor(out=stg[:, so, :], in0=rt1, in1=rt2, op=ALU.add),
                                  r=['rt1', 'rt2'], w=[('stg', so)])
                        else:
                            S.add('dve', lambda e, bk=bk, so=so, typ=typ: e.tensor_scalar(
                                out=stg[:, so, :], in0=banks[bk], scalar1=scl[:, typ - 1:typ], scalar2=None, op0=ALU.mult),
                                  r=[('bank', bk), 'scl'], w=[('stg', so)])
                        S.add('sp', lambda e, so=so, typ=typ, hd=hd, gi=gi, hf=hf: e.dma_start(
                            out=A_d[typ][hf][hd * 128:(hd + 1) * 128, gi * 512:(gi + 1) * 512], in_=stg[:, so, :]),
                              r=[('stg', so)], w=[('A', typ, hf)], stream=('aout', so))
                if not (hf == 1 and typ == 0):
                    S.add('pool', lambda e, typ=typ, hf=hf: e.collective_compute(
                        "AllGather", ALU.bypass, replica_groups=[list(range(NCORE))],
                        ins=[A_d[typ][hf].ap().opt()], outs=[G_d[typ][hf].ap().opt()]),
                          r=[('A', typ, hf)], w=[('G', typ, hf)], stream=('cc', typ), inc=None)
            S.barrier(skip_cc=True)
        S.add('pool', lambda e: e.collective_compute(
            "AllGather", ALU.bypass, replica_groups=[list(range(NCORE))],
            ins=[A_d[0][1].ap().opt()], outs=[G_d[0][1].ap().opt()]),
              r=[('A', 0, 1)], w=[('G', 0, 1)], stream=('cc', 0), inc=None)

        pid = {}

        def pull(typ):
            for hf in range(2):
                S.add('sp', lambda e, typ=typ, hf=hf: e.dma_start(
                    out=Lq_d[typ].ap().rearrange("(s o r) t -> s o r t", s=NCORE, o=1)[:, :, :, hf * 1024:(hf + 1) * 1024],
                    in_=G_d[typ][hf].ap().rearrange("(s h r) t -> s h r t", s=NCORE, h=NCORE)[:, bass.ds(pid['sp'], 1), :, :]),
                      r=[('G', typ, hf)], w=[('Lq', typ)], stream=('lq', typ))

        pull(2)
        pull(1)
        if debug:
            pull(0)
            S.barrier()

        def gsl(typ, src, off64, eng, cols=None):
            base = src * 128 + off64
            if cols is None:
                return Lq_d[typ].ap()[base:base + 64, :]
            return Lq_d[typ].ap()[base:base + 64, cols[0]:cols[1]]

        if debug:
            for it in range(6):
                typ, off = it // 2, (it % 2) * 64
                for src in range(NCORE):
                    S.add('sp', lambda e, it=it, typ=typ, off=off, src=src: e.dma_start(
                        out=big2[0:64, src * TL:(src + 1) * TL],
                        in_=gsl(typ, src, off, 'sp')), r=[('Lq', typ)], w=['dbgbuf'], stream='dbg')
                S.add('sp', lambda e, it=it: e.dma_start(out=dbg_q[it * 64:(it + 1) * 64, :], in_=big2[0:64, :]),
                      r=['dbgbuf'], w=['dbgq'], stream='dbg2')
            S.barrier()

        if phases >= 2:
            Vsb = big3[:, 0:8320].rearrange("p (a n) -> p a n", n=65)
            o = 8320
            Abuf = [big3[:, o + i * 1024:o + (i + 1) * 1024] for i in range(2)]
            o += 2048
            Lbuf = [big3[:, o + i * 1024:o + (i + 1) * 1024] for i in range(2)]
            o += 2048
            LaccA = [big3[:, o + i * 512:o + (i + 1) * 512] for i in range(3)]
            o += 1536
            LaccB = [big3[:, o + i * 512:o + (i + 1) * 512] for i in range(2)]
            o += 1024
            Qp = [big3[:, o + i * 512:o + (i + 1) * 512] for i in range(2)]
            qsb = Qp
            o += 1024
            selpad = big3[:, o:o + 128]
            o += 128
            vstage = big3[:, 8320:8320 + 2048]
            assert o <= T, o
            ebuf = [w32[:, i * 1024:(i + 1) * 1024] for i in range(2)]
            Lacc32 = w32[:, 2048:2560]
            Lsum = w32[:, 2560:3072]
            ystg32 = w32[:, 3072:3584]
            ybc = w32[:, 3584:4096]
            gsb = small[:, 0:64]
            m8 = small[:, 64:72]
            kms = small[:, 128:192]
            kmh = small[:, 192:256]
            recrow = small[:, 256:512]

            kmhi = stg[:, 0, 0:64]
            kmlo = stg[:, 0, 64:128]
            ystgb = stg[:, 1, :]

            KT = [('kT', s_) for s_ in range(NCORE)]
            vst2 = [big3[0:64, 8320:8320 + 2048], big3[64:128, 8320:8320 + 2048]]
            idn2 = [ident[0:64, 0:64], ident[64:128, 64:128]]

            def load_kv(ktyp, koff, vtyp, voff, tag):
                for src in range(NCORE):
                    S.add('sp', lambda e, src=src: e.dma_start(out=big2[0:64, src * TL:(src + 1) * TL],
                                                                 in_=gsl(ktyp, src, koff, 'sp')),
                          r=[('Lq', ktyp)], w=[('kT', src)], stream=('kv', src))
                for src in range(NCORE):
                    vs = src % 2
                    S.add('sp', lambda e, src=src, vs=vs: e.dma_start(out=vst2[vs], in_=gsl(vtyp, src, voff, 'sp')),
                          r=[('Lq', vtyp)], w=[('vstage', vs)], stream=('vst', vs))
                    for half in range(2):
                        bk = 6 + half
                        for i in range(8):
                            tl = half * 8 + i
                            S.add('pe', lambda e, bk=bk, i=i, tl=tl, vs=vs: e.matmul(
                                banks[bk][:, i * 64:(i + 1) * 64], lhsT=vst2[vs][:, tl * 128:(tl + 1) * 128],
                                rhs=idn2[vs], start=True, stop=True),
                                  r=[('vstage', vs), 'ident'], w=[('bank', bk)])
                        t0 = src * 16 + half * 8
                        S.add('dve', lambda e, bk=bk, t0=t0: e.tensor_copy(
                            out=Vsb[:, t0:t0 + 8, 0:64], in_=banks[bk].rearrange("p (a n) -> p a n", n=64)),
                              r=[('bank', bk)], w=['Vsb'])

            S.add('dve', lambda e: e.memset(Vsb[:, :, 64:65], 1.0), w=['Vsb'])
            load_kv(2, 0, 2, 64, 'sb')
            dma('sp', big2[64:128, :], blockind_d[:, :], w=['kTind'], stream='kvind')
            steps = []
            for qg in range(T // 512):
                n = 2 * qg + 2
                for s in range(n - 1, -1, -1):
                    steps.append((qg, s, s == n - 1, s == 0))
            NS = len(steps)
            zts = [psA, psB, psC]

            def qload(qg):
                src, gg = qg // 4, qg % 4
                sl = qg % 2
                S.add('sp', lambda e, src=src, gg=gg, sl=sl: e.dma_start(
                    out=qsb[sl][0:64, :], in_=gsl(1, src, 64, 'sp', (gg * 512, (gg + 1) * 512))),
                      r=[('Lq', 1)], w=[('qsb', sl)], stream=('q', sl))

            def z_mm(i):
                qg, s, first, last = steps[i]
                zs = i % 3
                for j, kt in enumerate((2 * s + 1, 2 * s)):
                    S.add('pe', lambda e, kt=kt, zs=zs, j=j, qg=qg: e.matmul(
                        banks[2 * zs + j], lhsT=big2[0:64, kt * 128:(kt + 1) * 128], rhs=qsb[qg % 2][0:64, :],
                        start=True, stop=True),
                          r=KT + [('qsb', qg % 2)], w=[('Z', zs)])

            def act12(i):
                qg, s, first, last = steps[i]
                zs = i % 3
                es_ = i % 2
                S.add('act', lambda e, zs=zs, es_=es_: e.activation(out=ebuf[es_], in_=zts[zs][:, :], func=AF.Exp),
                      r=[('Z', zs)], w=[('e', es_)])
                S.add('act', lambda e, es_=es_: e.activation(out=Lbuf[es_], in_=ebuf[es_], func=AF.Ln, bias=1.0),
                      r=[('e', es_)], w=[('L', es_)])
                diag = s >= 2 * qg
                if diag:
                    dj = (s - 2 * qg) * 2
                    S.add('pool', lambda e, es_=es_, dj=dj: e.tensor_tensor(
                        out=Lbuf[es_][:, 0:512], in0=Lbuf[es_][:, 0:512], in1=sbm01[:, dj + 1, :], op=ALU.mult),
                          r=[('L', es_), 'sbm01'], w=[('L', es_)])
                    S.add('pool', lambda e, es_=es_, dj=dj: e.tensor_tensor(
                        out=Lbuf[es_][:, 512:1024], in0=Lbuf[es_][:, 512:1024], in1=sbm01[:, dj, :], op=ALU.mult),
                          r=[('L', es_), 'sbm01'], w=[('L', es_)])

            def lacc(i):
                qg, s, first, last = steps[i]
                es_ = i % 2
                Lh = Lbuf[es_][:, 0:512]
                Ll = Lbuf[es_][:, 512:1024]
                if not first:
                    S.add('dve', lambda e, i=i, Lh=Lh: e.tensor_tensor(out=LaccB[i % 2], in0=Lacc32, in1=Lh, op=ALU.add),
                          r=[('L', es_), 'Lacc32'], w=[('LaccB', i % 2)])
                if last:
                    return
                if first:
                    S.add('dve', lambda e, Lh=Lh, Ll=Ll: e.tensor_tensor(out=Lacc32, in0=Lh, in1=Ll, op=ALU.add),
                          r=[('L', es_)], w=['Lacc32'])
                else:
                    S.add('dve', lambda e, Lh=Lh, Ll=Ll: e.tensor_tensor(out=Lsum, in0=Lh, in1=Ll, op=ALU.add),
                          r=[('L', es_)], w=['Lsum'])
                    S.add('dve', lambda e: e.tensor_tensor(out=Lacc32, in0=Lacc32, in1=Lsum, op=ALU.add),
                          r=['Lsum', 'Lacc32'], w=['Lacc32'])
                S.add('dve', lambda e, i=i: e.tensor_copy(out=LaccA[(i + 1) % 3], in_=Lacc32),
                      r=['Lacc32'], w=[('LaccA', (i + 1) % 3)])

            def p_mm(i):
                qg, s, first, last = steps[i]
                zs = i % 3
                es_ = i % 2
                diag = s >= 2 * qg
                dj = (s - 2 * qg) * 2
                Lh = Lbuf[es_][:, 0:512]
                for j in range(2):
                    Lme = Lbuf[es_][:, j * 512:(j + 1) * 512]
                    seq = [(negtri[:, :], Lme)]
                    if j == 0:
                        if not first:
                            seq.append((negones[:, :], LaccA[i % 3]))
                    else:
                        if first:
                            seq.append((negones[:, :], Lh))
                        else:
                            seq.append((negones[:, :], LaccB[i % 2]))
                    if diag:
                        seq.append((ident[:, :], sbneg[:, dj + (1 - j), :]))
                    for m, (lt, rh) in enumerate(seq):
                        S.add('pe', lambda e, zs=zs, j=j, lt=lt, rh=rh, m=m, nm=len(seq): e.matmul(
                            banks[2 * zs + j], lhsT=lt, rhs=rh, start=False, stop=(m == nm - 1)),
                              r=[('L', es_), ('LaccA', i % 3), ('LaccB', i % 2), 'sbneg', 'negtri', 'negones', 'ident', ('e', es_)],
                              w=[('Z', zs)])

            def act3(i):
                zs = i % 3
                as_ = i % 2
                S.add('act', lambda e, zs=zs, as_=as_: e.activation(out=Abuf[as_], in_=zts[zs][:, :], func=AF.Exp),
                      r=[('Z', zs)], w=[('A', as_)])

            def pv_mm(i):
                qg, s, first, last = steps[i]
                as_ = i % 2
                for j, kt in enumerate((2 * s + 1, 2 * s)):
                    S.add('pe', lambda e, kt=kt, j=j, as_=as_, first=first, last=last: e.matmul(
                        banks[6][0:64, :], lhsT=Vsb[:, kt, 0:64], rhs=Abuf[as_][:, j * 512:(j + 1) * 512],
                        start=(first and j == 0), stop=(last and j == 1)),
                          r=[('A', as_), 'Vsb'], w=['Y'])
                if last:
                    S.add('dve', lambda e: e.tensor_copy(out=ystgb[0:64, :], in_=banks[6][0:64, :]), r=['Y'], w=['ystgb'])
                    S.add('sp', lambda e, qg=qg: e.dma_start(out=B_d[1].ap()[:, qg * 512:(qg + 1) * 512], in_=ystgb[0:64, :]),
                          r=['ystgb'], w=[('B', 1)], stream=('bout', 0))

            qload(0)
            z_mm(0)
            act12(0)
            lacc(0)
            if NS > 1:
                if steps[1][2]:
                    qload(steps[1][0])
                z_mm(1)
            for i in range(NS):
                if i == 300 and not debug:
                    pull(0)
                if i + 1 < NS:
                    act12(i + 1)
                    lacc(i + 1)
                p_mm(i)
                if i + 2 < NS:
                    if steps[i + 2][2]:
                        qload(steps[i + 2][0])
                    z_mm(i + 2)
                act3(i)
                pv_mm(i)

            S.barrier()
            S.add('pool', lambda e: e.collective_compute("AllGather", ALU.bypass, replica_groups=[list(range(NCORE))],
                                                         ins=[B_d[1].ap().opt()], outs=[GB_d[1].ap().opt()]),
                  r=[('B', 1)], w=[('GB', 1)], stream=('cc', 3), inc=None)
            load_kv(0, 64, 1, 0, 'moba')
            S.add('dve', lambda e: e.tensor_reduce(out=kms[0:64, :], in_=big2[0:64, :].rearrange("p (n b) -> p n b", b=256),
                                                   axis=AX.X, op=ALU.add), r=KT, w=['kms'])
            S.add('dve', lambda e: e.tensor_scalar(out=kms[0:64, :], in0=kms[0:64, :], scalar1=1.0 / 256, scalar2=None, op0=ALU.mult),
                  r=['kms'], w=['kms'])
            S.add('dve', lambda e: e.tensor_copy(out=kmhi[0:64, :], in_=kms[0:64, :]), r=['kms'], w=['kmhi'])
            S.add('dve', lambda e: e.tensor_copy(out=kmh[0:64, :], in_=kmhi[0:64, :]), r=['kmhi'], w=['kmh'])
            S.add('dve', lambda e: e.tensor_tensor(out=kmh[0:64, :], in0=kms[0:64, :], in1=kmh[0:64, :], op=ALU.subtract),
                  r=['kms', 'kmh'], w=['kmh'])
            S.add('dve', lambda e: e.tensor_copy(out=kmlo[0:64, :], in_=kmh[0:64, :]), r=['kmh'], w=['kmlo'])
            S.add('dve', lambda e: e.memset(gsb, -1e30), w=['gsb'])
            S.add('dve', lambda e: e.memset(selpad[:, 0:64], 0.0), w=['selpad'])

            def moba_prep_stages(qg):
                sl = qg % 2
                src, gg = qg // 4, qg % 4
                st = []

                def s_load():
                    S.add('sp', lambda e: e.dma_start(
                        out=Qp[sl][0:64, :], in_=gsl(0, src, 0, 'sp', (gg * 512, (gg + 1) * 512))),
                          r=[('Lq', 0)], w=[('Qp', sl)], stream=('q', sl))
                st.append(s_load)
                for ci in range(4):
                    c = qg * 4 + ci
                    own = c // 2

                    def s_gate(ci=ci, own=own):
                        if own <= 3:
                            S.add('dve', lambda e: e.memset(selpad[:, 64:128], NEGM), w=['selpad'])
                            S.add('dve', lambda e: e.memset(selpad[:, 64:64 + own + 1], 0.0), w=['selpad'])
                        else:
                            for m, km in enumerate((kmhi, kmlo)):
                                S.add('pe', lambda e, km=km, m=m: e.matmul(
                                    banks[7][:, 0:64], lhsT=Qp[sl][0:64, ci * 128:(ci + 1) * 128], rhs=km[0:64, :],
                                    start=(m == 0), stop=(m == 1)),
                                      r=[('Qp', sl), 'kmhi', 'kmlo'], w=[('bank', 7)])
                            S.add('dve', lambda e: e.tensor_copy(out=gsb[:, 0:own], in_=banks[7][:, 0:own]),
                                  r=[('bank', 7)], w=['gsb'])
                            S.add('dve', lambda e: e.max(out=m8, in_=gsb), r=['gsb'], w=['m8'])
                            S.add('dve', lambda e: e.tensor_scalar(out=selpad[:, 64:128], in0=gsb, scalar1=m8[:, 2:3],
                                                                   scalar2=NEGM, op0=ALU.is_lt, op1=ALU.mult),
                                  r=['gsb', 'm8'], w=['selpad'])
                            S.add('dve', lambda e: e.memset(selpad[:, 64 + own:64 + own + 1], 0.0), w=['selpad'])

                    def s_tr(ci=ci):
                        S.add('pe', lambda e: e.matmul(banks[7][:, 0:128], lhsT=selpad, rhs=ident[:, :], start=True, stop=True),
                              r=['selpad', 'ident'], w=[('bank', 7)])
                        S.add('dve', lambda e: e.tensor_copy(out=Qp[sl][64:128, ci * 128:(ci + 1) * 128],
                                                             in_=banks[7][64:128, 0:128]),
                              r=[('bank', 7)], w=[('Qp', sl)])
                    st.append(s_gate)
                    st.append(s_tr)
                return st

            msteps = []
            for qg in range(T // 512):
                nb = 2 * qg + 2
                for n in range(nb):
                    msteps.append((qg, n, n == 0, n == nb - 1))
            NM = len(msteps)

            def s_mm(i):
                qg, n, first, last = msteps[i]
                zs = i % 3
                diag = n >= 2 * qg
                for j in range(2):
                    kt = 2 * n + j
                    S.add('pe', lambda e, kt=kt, zs=zs, j=j, qg=qg, diag=diag: e.matmul(
                        banks[2 * zs + j], lhsT=big2[:, kt * 128:(kt + 1) * 128], rhs=Qp[qg % 2], start=True, stop=(not diag)),
                          r=KT + ['kTind', ('Qp', qg % 2)], w=[('Z', zs)])
                    if diag:
                        dj = (n - 2 * qg) * 2 + j
                        S.add('pe', lambda e, zs=zs, j=j, dj=dj: e.matmul(
                            banks[2 * zs + j], lhsT=ident[:, :], rhs=mobam[:, dj, :], start=False, stop=True),
                              r=['mobam', 'ident'], w=[('Z', zs)])

            def m_act(i):
                zs = i % 3
                as_ = i % 2
                S.add('act', lambda e, zs=zs, as_=as_: e.activation(out=Abuf[as_], in_=zts[zs][:, :], func=AF.Exp, scale=0.125),
                      r=[('Z', zs)], w=[('A', as_)])

            def m_pv(i):
                qg, n, first, last = msteps[i]
                as_ = i % 2
                for j in range(2):
                    kt = 2 * n + j
                    S.add('pe', lambda e, kt=kt, j=j, as_=as_, first=first, last=last: e.matmul(
                        banks[6][0:65, :], lhsT=Vsb[:, kt, 0:65], rhs=Abuf[as_][:, j * 512:(j + 1) * 512],
                        start=(first and j == 0), stop=(last and j == 1)),
                          r=[('A', as_), 'Vsb'], w=['Y'])
                if last:
                    S.add('dve', lambda e: e.reciprocal(out=ystg32[64:65, :], in_=banks[6][64:65, :]), r=['Y'], w=['rec'])
                    S.add('pe', lambda e: e.matmul(banks[7][0:64, :], lhsT=onesf[64:65, 0:64], rhs=ystg32[64:65, :],
                                                    start=True, stop=True), r=['rec', 'onesf'], w=[('bank', 7)])
                    S.add('act', lambda e: e.activation(out=ybc[0:64, :], in_=banks[7][0:64, :], func=AF.Copy),
                          r=[('bank', 7)], w=['ybc'])
                    S.add('dve', lambda e: e.tensor_tensor(out=ystgb[0:64, :], in0=banks[6][0:64, :], in1=ybc[0:64, :], op=ALU.mult),
                          r=['Y', 'ybc'], w=['ystgb'])
                    S.add('sp', lambda e, qg=qg: e.dma_start(out=B_d[0].ap()[:, qg * 512:(qg + 1) * 512], in_=ystgb[0:64, :]),
                          r=['ystgb'], w=[('B', 0)], stream=('bout', 0))

            for f_ in moba_prep_stages(0):
                f_()
            pending = moba_prep_stages(1)
            NQG = T // 512

            def issue_s(k):
                nonlocal pending
                if msteps[k][2] and msteps[k][0] > 0:
                    for f_ in pending:
                        f_()
                    nq = msteps[k][0] + 1
                    pending = moba_prep_stages(nq) if nq < NQG else []
                s_mm(k)

            issue_s(0)
            if NM > 1:
                issue_s(1)
            for i in range(NM):
                if i + 2 < NM:
                    issue_s(i + 2)
                m_act(i)
                m_pv(i)
                if pending:
                    pending.pop(0)()

        S.barrier()
        if debug and phases >= 2:
            dma('sp', dbg_B[0:64, :], B_d[0].ap()[:, :], r=[('B', 0)], stream='dbg')
            dma('sp', dbg_B[64:128, :], B_d[1].ap()[:, :], r=[('B', 1)], stream='dbg')

        if phases >= 3:
            S.add('pool', lambda e: e.collective_compute("AllGather", ALU.bypass, replica_groups=[list(range(NCORE))],
                                                         ins=[B_d[0].ap().opt()], outs=[GB_d[0].ap().opt()]),
                  r=[('B', 0)], w=[('GB', 0)], stream=('cc', 4), inc=None)
            S.barrier()
            for m in (1, 0):
                S.add('sp', lambda e, m=m: e.dma_start(out=LB_d[m].ap()[:, :], in_=GB_d[m].ap()[:, bass.ds(pid['sp'] * TL, TL)]),
                      r=[('GB', m)], w=['LB'], stream=('lq', 3))
            S.barrier()
            wa_s = big3[:, 0:4096].rearrange("p (k n) -> p k n", n=D)
            wb_s = big3[:, 4096:8192].rearrange("p (k n) -> p k n", n=D)
            wo_s = big3[:, 8192:16384].rearrange("p (k n) -> p k n", n=D)
            for kk in range(4):
                S.add('pool', lambda e, kk=kk: e.dma_start(out=wa_s[:, kk, :], in_=wa_d[kk, :, :]), w=['wa'], stream='w3')
                S.add('pool', lambda e, kk=kk: e.dma_start(out=wb_s[:, kk, :], in_=wb_d[kk, :, :]), w=['wb'], stream='w3')
            for kk in range(8):
                S.add('pool', lambda e, kk=kk: e.dma_start(out=wo_s[:, kk, :], in_=wo_d[kk, :, :]), w=['wo'], stream='w3')
            gfin = w32[:, 3072:4096]
            S.add('sp', lambda e: e.dma_start(out=gfin, in_=gfin_d[0:1, :].broadcast_to([128, D])), w=['gfin'], stream='gfin')
            S.barrier()
            for g in range(NG):
                tiles = [g * 4 + i for i in range(4)]
                rmsnorm_group(tiles, 8)
                for m in range(2):
                    for kk in range(4):
                        for hh in range(2):
                            hd = kk * 2 + hh
                            S.add('sp', lambda e, m=m, kk=kk, hh=hh, hd=hd, g=g: e.dma_start(
                                out=actT[hh * 64:(hh + 1) * 64, m * 4 + kk, :],
                                in_=LB_d[m].ap()[hd * 64:(hd + 1) * 64, g * 512:(g + 1) * 512]),
                                  r=['LB'], w=[('actT', m * 4 + kk)], stream=('yl', m))
                mT = actT[:, 8:16, :]
                for j in range(8):
                    for m in range(2):
                        cc = 24 + m * 8 + j
                        sc = wcnt['c'] % 3
                        wcnt['c'] += 1
                        S.add('pool', lambda e, cc=cc, sc=sc: e.dma_start(out=wgs[:, sc, :, :], in_=win_d[cc, :, :, :]),
                              w=[('wgs', sc)], stream=('wg', sc))
                        for k in range(8):
                            S.add('pe', lambda e, k=k, sc=sc, m=m: e.matmul(banks[4 + m], lhsT=wgs[:, sc, k, :], rhs=xnT[:, k, :],
                                                                           start=(k == 0), stop=(k == 7)),
                                  r=[('wgs', sc), ('xnT', k)], w=[('bank', 4 + m)])
                        gt = [sgt, xf32][m]
                        S.add('act', lambda e, m=m, j=j, gt=gt: e.activation(out=gt, in_=banks[4 + m], func=AF.Sigmoid,
                                                                            bias=bgate[:, m * 8 + j:m * 8 + j + 1]),
                              r=[('bank', 4 + m), 'bgate'], w=[('gt', m)])
                        ws = [wa_s, wb_s][m]
                        for kk in range(4):
                            S.add('pe', lambda e, kk=kk, m=m, j=j, ws=ws: e.matmul(
                                banks[6 + m], lhsT=ws[:, kk, j * 128:(j + 1) * 128], rhs=actT[:, m * 4 + kk, :],
                                start=(kk == 0), stop=(kk == 3)),
                                  r=['wa', 'wb', ('actT', m * 4 + kk)], w=[('bank', 6 + m)])
                    S.add('dve', lambda e: e.tensor_tensor(out=rt1, in0=sgt, in1=banks[6], op=ALU.mult),
                          r=[('gt', 0), ('bank', 6)], w=['rt1'])
                    S.add('dve', lambda e: e.tensor_tensor(out=rt2, in0=xf32, in1=banks[7], op=ALU.mult),
                          r=[('gt', 1), ('bank', 7)], w=['rt2'])
                    S.add('dve', lambda e, j=j: e.tensor_tensor(out=mT[:, j, :], in0=rt1, in1=rt2, op=ALU.add),
                          r=['rt1', 'rt2'], w=[('actT', 8 + j)])
                for half in range(2):
                    for ts in range(4):
                        for j in range(8):
                            S.add('pe', lambda e, ts=ts, j=j, half=half: e.matmul(
                                banks[ts], lhsT=mT[:, j, ts * 128:(ts + 1) * 128], rhs=wo_s[:, j, half * 512:(half + 1) * 512],
                                start=(j == 0), stop=(j == 7)),
                                  r=[('actT', 8 + j), 'wo'], w=[('bank', ts)])
                    for ts in range(4):
                        tt = tiles[ts]
                        S.add('dve', lambda e, ts=ts, tt=tt, half=half: e.tensor_tensor(
                            out=h_sb[:, tt, half * 512:(half + 1) * 512], in0=banks[ts],
                            in1=h_sb[:, tt, half * 512:(half + 1) * 512], op=ALU.add),
                              r=[('bank', ts), ('h', tt)], w=[('h', tt)])
            S.barrier()
            rmsnorm_all(16)
            ffn_all(1)
            for g in range(NG):
                tiles = [g * 4 + i for i in range(4)]
                for i, tt in enumerate(tiles):
                    col = norm_ctr[0]
                    norm_ctr[0] += 1
                    S.add('act', lambda e, tt=tt, col=col: e.activation(out=junk[:, :], in_=h_sb[:, tt, :], func=AF.Square,
                                                                         accum_out=ss[:, col:col + 1]),
                          r=[('h', tt), 'ss'], w=[('ss', col), 'junk'])
                    S.add('act', lambda e, col=col: e.activation(out=rstd[:, col:col + 1], in_=ss[:, col:col + 1], func=AF.Sqrt,
                                                                 scale=1.0 / D, bias=epsc[:, 0:1]),
                          r=[('ss', col), 'epsc'], w=[('rstd0', col)])
                    S.add('dve', lambda e, col=col: e.reciprocal(out=rstd[:, col:col + 1], in_=rstd[:, col:col + 1]),
                          r=[('rstd0', col)], w=[('rstd', col)])
                    S.add('dve', lambda e, tt=tt, col=col: e.scalar_tensor_tensor(
                        out=h_sb[:, tt, :], in0=h_sb[:, tt, :], scalar=rstd[:, col:col + 1], in1=gfin, op0=ALU.mult, op1=ALU.mult),
                          r=[('h', tt), ('rstd', col), 'gfin'], w=[('h', tt)])
                    dma('sp', y_d[tt * 128:(tt + 1) * 128, :], h_sb[:, tt, :], r=[('h', tt)], w=['yout'], stream='yout')
        else:
            for tt in range(NTL):
                dma('sp', y_d[tt * 128:(tt + 1) * 128, :], h_sb[:, tt, :], r=[('h', tt)], w=['yout'], stream='yout')
        S.barrier()
        S.add('sp', lambda e: e.nop(), r=['yout'])

        keys = S.finalize()
        semh = {}
        for k in keys:
            nm = "s_" + "_".join(str(x) for x in (k[1] if isinstance(k[1], tuple) else (k[1],)))
            semh[k] = es.enter_context(nc.semaphore(nm))
        block = es.enter_context(nc.Block())

        def section(engname):
            def body(e):
                if engname in ('sp', 'pool'):
                    pid[engname] = e.partition_id()
                S.emit_engine(engname, e, semh)
            return body

        block.sync(section('sp'))
        block.tensor(section('pe'))
        block.scalar(section('act'))
        block.vector(section('dve'))
        block.gpsimd(section('pool'))
    return nc


def _prep_inputs(x, positions, ffn1_norm, ffn1_w_gate, ffn1_w_up, ffn1_w_down, mix_norm, w_in, b_gate,
                 w_branch_moba, w_branch_sb, w_out, ffn2_norm, ffn2_w_gate, ffn2_w_up, ffn2_w_down, final_norm):
    f = lambda a: np.ascontiguousarray(np.asarray(a, dtype=np.float32))
    shared = {}

    def gu(w):
        w = f(w).reshape(8, 128, NF, 128)
        return np.ascontiguousarray(w.transpose(2, 1, 0, 3))

    shared['wg1'] = gu(ffn1_w_gate[0]); shared['wu1'] = gu(ffn1_w_up[0]); shared['wd1'] = f(ffn1_w_down[0]).reshape(NF, 128, D)
    shared['wg2'] = gu(ffn2_w_gate[0]); shared['wu2'] = gu(ffn2_w_up[0]); shared['wd2'] = f(ffn2_w_down[0]).reshape(NF, 128, D)
    wi = f(w_in[0])
    qa, ka, va, qb, kb, vb = [wi[:, i * 512:(i + 1) * 512] for i in range(6)]
    cols = []
    for h in range(8):
        for part in (qa, ka, va, qb, kb, vb):
            cols.append(part[:, h * 64:(h + 1) * 64])
    cols.append(wi[:, 3072:5120])
    wr = np.concatenate(cols, axis=1)
    wr = wr.reshape(8, 128, 40, 128)
    shared['win'] = np.ascontiguousarray(wr.transpose(2, 1, 0, 3))
    shared['wa'] = f(w_branch_moba[0]).reshape(4, 128, D)
    shared['wb'] = f(w_branch_sb[0]).reshape(4, 128, D)
    shared['wo'] = f(w_out[0]).reshape(8, 128, D)
    gn = np.concatenate([f(ffn1_norm[0]).reshape(8, 128).T, f(mix_norm[0]).reshape(8, 128).T,
                         f(ffn2_norm[0]).reshape(8, 128).T], axis=1)
    shared['gains'] = np.ascontiguousarray(gn)
    shared['bgate'] = np.ascontiguousarray(f(b_gate[0]).reshape(16, 128).T)
    shared['gfin'] = f(final_norm).reshape(1, D)
    shared.update(host_consts())
    xs = f(x)[0]
    ps_ = np.asarray(positions).astype(np.int32)[0]
    in_maps = []
    for c in range(NCORE):
        m = dict(shared)
        m['x'] = np.ascontiguousarray(xs[c * TL:(c + 1) * TL])
        m['pos'] = np.ascontiguousarray(ps_[c * TL:(c + 1) * TL]).reshape(1, TL)
        in_maps.append(m)
    return in_maps


_NC_CACHE = {}


def kernel(**inputs):
    debug = bool(int(os.environ.get("KDEBUG", "0")))
    phases = int(os.environ.get("KPHASES", "3"))
    key = (debug, phases)
    if key not in _NC_CACHE:
        _NC_CACHE[key] = build_program(debug=debug, phases=phases)
    nc = _NC_CACHE[key]
    in_maps = _prep_inputs(**inputs)
    res = run_bass_kernel_spmd(nc, in_maps, core_ids=list(range(NCORE)))
    out = np.concatenate([np.asarray(r["y"], dtype=np.float32) for r in res.results], axis=0)[None]
    if debug:
        kernel.last_results = res.results
    return out
```
